# Optimizing a Trainium2 kernel written in Bass

```python
import jax, jax.numpy as jnp
from jax import lax
import numpy as np

D_MODEL = 1024
BATCH = 4
SEQ = 4096
DEPTH = 4

CHUNK = 64
N_MIXERS = 3
N_HEADS = 16
HEAD_DIM = D_MODEL // N_HEADS
Q_BLOCK = 128
SG_CHUNK = 128
SG_WIDTH = D_MODEL
SG_GROUPS = 8
SG_GROUP_DIM = SG_WIDTH // SG_GROUPS
CONV_WIDTH = 31
D_FF = 4 * D_MODEL
N_A = (DEPTH + 2) // N_MIXERS
N_B = (DEPTH + 1) // N_MIXERS
N_C = DEPTH // N_MIXERS
EPS = 1e-6

kernel_name = "chunk_causal_hybrid_fox_gmlp_conformer"


def rms_norm(x, g):
    xf = x.astype(jnp.float32)
    y = xf * lax.rsqrt(jnp.mean(xf * xf, axis=-1, keepdims=True) + EPS)
    return (y * g.astype(jnp.float32)).astype(x.dtype)


def layer_norm(x, g, b):
    xf = x.astype(jnp.float32)
    mu = jnp.mean(xf, axis=-1, keepdims=True)
    xc = xf - mu
    y = xc * lax.rsqrt(jnp.mean(xc * xc, axis=-1, keepdims=True) + EPS)
    return (y * g.astype(jnp.float32) + b.astype(jnp.float32)).astype(x.dtype)


def fox_mixer(h, w_in, b_f, q_g, k_g, w_out):
    B, S, D = h.shape
    proj = h @ w_in
    q, k, v, f_pre = jnp.split(proj, [D, 2 * D, 3 * D], axis=-1)
    q = rms_norm(q.reshape(B, S, N_HEADS, HEAD_DIM), q_g)
    k = rms_norm(k.reshape(B, S, N_HEADS, HEAD_DIM), k_g)
    v = v.reshape(B, S, N_HEADS, HEAD_DIM)
    log_f = jax.nn.log_sigmoid(f_pre.astype(jnp.float32) + b_f.astype(jnp.float32))
    F = jnp.cumsum(log_f, axis=1).transpose(0, 2, 1)
    nb = S // Q_BLOCK
    qb = q.reshape(B, nb, Q_BLOCK, N_HEADS, HEAD_DIM).swapaxes(0, 1)
    Fq = F.reshape(B, N_HEADS, nb, Q_BLOCK).transpose(2, 0, 1, 3)
    k_pos = jnp.arange(S)
    scale = HEAD_DIM ** -0.5

    def block(args):
        q_i, F_i, b_i = args
        logits = (jnp.einsum('bqhd,bkhd->bhqk', q_i, k).astype(jnp.float32) * scale
                  + (F_i[..., :, None] - F[..., None, :]))
        q_pos = b_i * Q_BLOCK + jnp.arange(Q_BLOCK)
        logits = jnp.where(k_pos[None, :] <= q_pos[:, None], logits, -jnp.inf)
        p = jax.nn.softmax(logits, axis=-1).astype(v.dtype)
        return jnp.einsum('bhqk,bkhd->bqhd', p, v)

    o = lax.map(block, (qb, Fq, jnp.arange(nb)))
    o = o.swapaxes(0, 1).reshape(B, S, D)
    return o @ w_out


def gmlp_mixer(h, w_in, ln_g, ln_b, w_s, b_s, w_out):
    B, S, _ = h.shape
    uv = jax.nn.gelu(h @ w_in)
    u, v = jnp.split(uv, 2, axis=-1)
    v = layer_norm(v, ln_g, ln_b)
    v = v.reshape(B, S // SG_CHUNK, SG_CHUNK, SG_GROUPS, SG_GROUP_DIM)
    cid = jnp.arange(SG_CHUNK) // CHUNK
    mask = cid[None, :] <= cid[:, None]
    ws = jnp.where(mask[None], w_s, jnp.zeros_like(w_s))
    v = jnp.einsum('gts,bnsgc->bntgc', ws, v) + b_s.T[:, :, None]
    v = v.reshape(B, S, SG_WIDTH)
    return (u * v) @ w_out


def conv_mixer(h, w_pw1, b_pw1, w_dw, b_dw, ln_g, ln_b, w_pw2, b_pw2):
    D = h.shape[-1]
    y = jax.nn.glu(h @ w_pw1 + b_pw1, axis=-1)
    y = lax.conv_general_dilated(y, w_dw[:, None, :], window_strides=(1,),
                                 padding=[(CONV_WIDTH - 1, 0)],
                                 dimension_numbers=('NWC', 'WIO', 'NWC'),
                                 feature_group_count=D) + b_dw
    y = jax.nn.silu(layer_norm(y, ln_g, ln_b))
    return y @ w_pw2 + b_pw2


def setup_inputs(seed: int = 0) -> dict:
    key = jax.random.key(seed)
    ks = iter(jax.random.split(key, 40))
    D = D_MODEL

    def nrm(shape, scale):
        return jax.random.normal(next(ks), shape, jnp.float32) * scale

    def gain(shape):
        return 1.0 + nrm(shape, 0.05)

    return {
        "x": nrm((BATCH, SEQ, D), 1.0),
        "c": nrm((BATCH, D), 1.0),
        "norm_mix": gain((DEPTH, D)),
        "norm_mlp": gain((DEPTH, D)),
        "w_ada": nrm((DEPTH, D, 6 * D), 0.5 * D ** -0.5),
        "b_ada": nrm((DEPTH, 6 * D), 0.02),
        "w_mlp_in": nrm((DEPTH, D, D_FF), D ** -0.5),
        "w_mlp_out": nrm((DEPTH, D_FF, D), D_FF ** -0.5),
        "fox_w_in": nrm((N_A, D, 3 * D + N_HEADS), D ** -0.5),
        "fox_b_f": jax.random.uniform(next(ks), (N_A, N_HEADS), jnp.float32, 1.0, 6.0),
        "fox_q_norm": gain((N_A, HEAD_DIM)),
        "fox_k_norm": gain((N_A, HEAD_DIM)),
        "fox_w_out": nrm((N_A, D, D), D ** -0.5),
        "sg_w_in": nrm((N_B, D, 2 * SG_WIDTH), D ** -0.5),
        "sg_ln_g": gain((N_B, SG_WIDTH)),
        "sg_ln_b": nrm((N_B, SG_WIDTH), 0.02),
        "sg_w_s": nrm((N_B, SG_GROUPS, SG_CHUNK, SG_CHUNK), 0.5 * SG_CHUNK ** -0.5),
        "sg_b_s": 1.0 + nrm((N_B, SG_GROUPS, SG_CHUNK), 0.02),
        "sg_w_out": nrm((N_B, SG_WIDTH, D), SG_WIDTH ** -0.5),
        "cv_w_pw1": nrm((N_C, D, 2 * D), D ** -0.5),
        "cv_b_pw1": nrm((N_C, 2 * D), 0.02),
        "cv_w_dw": nrm((N_C, CONV_WIDTH, D), CONV_WIDTH ** -0.5),
        "cv_b_dw": nrm((N_C, D), 0.02),
        "cv_ln_g": gain((N_C, D)),
        "cv_ln_b": nrm((N_C, D), 0.02),
        "cv_w_pw2": nrm((N_C, D, D), D ** -0.5),
        "cv_b_pw2": nrm((N_C, D), 0.02),
    }


def reference(x, c, norm_mix, norm_mlp, w_ada, b_ada, w_mlp_in, w_mlp_out,
              fox_w_in, fox_b_f, fox_q_norm, fox_k_norm, fox_w_out,
              sg_w_in, sg_ln_g, sg_ln_b, sg_w_s, sg_b_s, sg_w_out,
              cv_w_pw1, cv_b_pw1, cv_w_dw, cv_b_dw, cv_ln_g, cv_ln_b, cv_w_pw2, cv_b_pw2):
    c_act = jax.nn.silu(c)
    for i in range(DEPTH):
        kind = i % N_MIXERS
        j = i // N_MIXERS
        mod = c_act @ w_ada[i] + b_ada[i]
        sh_m, sc_m, g_m, sh_f, sc_f, g_f = [m[:, None, :] for m in jnp.split(mod, 6, axis=-1)]
        h = rms_norm(x, norm_mix[i]) * (1 + sc_m) + sh_m
        if kind == 0:
            y = fox_mixer(h, fox_w_in[j], fox_b_f[j], fox_q_norm[j], fox_k_norm[j], fox_w_out[j])
        elif kind == 1:
            y = gmlp_mixer(h, sg_w_in[j], sg_ln_g[j], sg_ln_b[j], sg_w_s[j], sg_b_s[j], sg_w_out[j])
        else:
            y = conv_mixer(h, cv_w_pw1[j], cv_b_pw1[j], cv_w_dw[j], cv_b_dw[j],
                           cv_ln_g[j], cv_ln_b[j], cv_w_pw2[j], cv_b_pw2[j])
        x = x + g_m * y
        h = rms_norm(x, norm_mlp[i]) * (1 + sc_f) + sh_f
        x = x + g_f * (jnp.square(jax.nn.relu(h @ w_mlp_in[i])) @ w_mlp_out[i])
    return x
```

```python
import os
import numpy as np
from contextlib import ExitStack
import concourse.bass as bass
import concourse.mybir as mybir
from concourse.bass_utils import run_bass_kernel_spmd

F32 = mybir.dt.float32
BF16 = mybir.dt.bfloat16
AF = mybir.ActivationFunctionType
ALU = mybir.AluOpType
AX = mybir.AxisListType

D = 1024
DFF = 4096
NH = 16
HD = 64
KC = 8
DEPTH = 4
EPS = 1e-6
HA = 67
CW = 31
NEG = -30000.0
SLOT = 8192
NSLOT = 6


class Buf:
    __slots__ = ("name", "w", "r", "ds")

    def __init__(self, name=""):
        self.name = name
        self.w = None
        self.r = {}
        self.ds = None


class SemC:
    def __init__(self, k, name):
        self.name = name
        self.sem = k.es.enter_context(k.nc.semaphore(name))
        self.cnt = 0


class Eng:
    def __init__(self, k, e, name, in_order_self=False):
        self.e = e
        self.name = name
        self.sc = SemC(k, "s_" + name)
        self.seen = {}
        self.in_order_self = in_order_self

    def wait(self, tk):
        if tk is None:
            return
        sc, val = tk
        if val <= 0:
            return
        if sc is self.sc and self.in_order_self:
            return
        if self.seen.get(sc, 0) >= val:
            return
        self.e.wait_ge(sc.sem, val)
        self.seen[sc] = val


class K:
    def __init__(self, nc, es):
        self.nc = nc
        self.es = es
        self.pe = Eng(self, nc.tensor, "pe", in_order_self=True)
        self.act = Eng(self, nc.scalar, "act")
        self.dve = Eng(self, nc.vector, "dve")
        self.pool = Eng(self, nc.gpsimd, "pool")
        self.sp = Eng(self, nc.sync, "sp")
        self.engs = [self.pe, self.act, self.dve, self.pool, self.sp]
        self.bar_sems = []
        self.dpool = []
        self.dnext = 0
        self.NDPOOL = 56

    def _buf_sem(self, writes):
        b = writes[0]
        if b.ds is None:
            if len(self.dpool) < self.NDPOOL:
                self.dpool.append(self.dsem("dp%d" % len(self.dpool)))
            b.ds = self.dpool[self.dnext % self.NDPOOL]
            self.dnext += 1
        return b.ds

    def sb(self, name, shape, dt):
        return self.es.enter_context(self.nc.sbuf_tensor(name, list(shape), dt))

    def ps(self, name, shape, dt):
        return self.es.enter_context(self.nc.psum_tensor(name, list(shape), dt))

    def dsem(self, name, barrier=True):
        s = SemC(self, name)
        if barrier:
            self.bar_sems.append(s)
        return s

    def _deps(self, E, reads, writes):
        for b in reads:
            E.wait(b.w)
        for b in writes:
            E.wait(b.w)
            for sc, v in b.r.items():
                E.wait((sc, v))

    def _record(self, tk, reads, writes):
        sc, v = tk
        for b in reads:
            if b.r.get(sc, 0) < v:
                b.r[sc] = v
        for b in writes:
            b.w = tk
            b.r = {}

    def op(self, E, emit, reads=(), writes=(), inc=True):
        self._deps(E, reads, writes)
        ins = emit(E.e)
        if inc:
            E.sc.cnt += 1
            ins.then_inc(E.sc.sem, 1)
            tk = (E.sc, E.sc.cnt)
        else:
            tk = (E.sc, E.sc.cnt + 1)
        self._record(tk, reads, writes)
        return tk

    def dma(self, Q, dsem, out, in_, reads=(), writes=()):
        if writes:
            dsem = self._buf_sem(writes)
        self._deps(Q, reads, writes)
        ins = Q.e.dma_start(out=out, in_=in_)
        dsem.cnt += 16
        ins.then_inc(dsem.sem, 16)
        tk = (dsem, dsem.cnt)
        self._record(tk, reads, writes)
        return tk

    def coll(self, csem, kind, groups, in_ap, out_ap, reads=(), writes=()):
        Q = self.pool
        b = writes[0]
        if b.ds is None:
            b.ds = self.dsem("cs_" + b.name)
        csem = b.ds
        self._deps(Q, reads, writes)
        ins = Q.e.collective_compute(kind, ALU.bypass, replica_groups=groups, ins=[in_ap], outs=[out_ap])
        csem.cnt += 1
        ins.then_inc(csem.sem)
        tk = (csem, csem.cnt)
        self._record(tk, reads, writes)
        return tk

    def barrier(self):
        for E in self.engs:
            for E2 in self.engs:
                if E2 is not E:
                    E.wait((E2.sc, E2.sc.cnt))
            for s in self.bar_sems:
                E.wait((s, s.cnt))


def build(NT, layers=(0, 1, 2, 3), skip=()):
    T = NT * 128
    NCH = NT // 4
    assert NT % 4 == 0
    nc = bass.Bass("TRN2", target_bir_lowering=False)
    es = ExitStack()
    with es:
        k = K(nc, es)
        pe, act, dve, pool, sp = k.pe, k.act, k.dve, k.pool, k.sp

        def din(name, shape, dt=F32):
            return nc.dram_tensor(name, list(shape), dt, kind="ExternalInput").ap()

        x_d = din("x", [T, D])
        cT_d = din("cT", [128, KC])
        flag_d = din("flag", [128, 1])
        nmix_d = din("norm_mix", [DEPTH, D])
        nmlp_d = din("norm_mlp", [DEPTH, D])
        wada_d = din("w_ada", [DEPTH, D, 6 * D])
        bada_d = din("b_ada", [DEPTH, 6 * D])
        wmi_d = din("w_mlp_in", [DEPTH, D, DFF])
        wmo_d = din("w_mlp_out", [DEPTH, DFF, D])
        fwin_d = din("fox_w_in", [2, D, 3 * D + NH])
        fbf_d = din("fox_bf_bc", [2, 128, NH])
        fqg_d = din("fox_qg_bc", [2, 128, HD])
        fkg_d = din("fox_kg_bc", [2, 128, HD])
        fwo_d = din("fox_w_out", [2, D, D])
        swin_d = din("sg_w_in", [D, 2 * D])
        slng_d = din("sg_lng_bc", [128, D])
        slnb_d = din("sg_lnb_bc", [128, D])
        sws_d = din("sg_w_s", [8, 128, 128])
        sbs_d = din("sg_bs_c", [128, 8])
        swo_d = din("sg_w_out", [D, D])
        cw1_d = din("cv_w_pw1", [D, 2 * D])
        cb1_d = din("cv_bpw1_c", [128, 16])
        cwd_d = din("cv_wdw_c", [128, KC * CW])
        cbd_d = din("cv_bdw_c", [128, KC])
        clg_d = din("cv_lng_c", [128, KC])
        clb_d = din("cv_lnb_c", [128, KC])
        cw2_d = din("cv_w_pw2", [D, D])
        cb2_d = din("cv_bpw2", [1, D])
        out_d = nc.dram_tensor("out", [T, D], F32, kind="ExternalOutput").ap()

        qt_t = nc.dram_tensor("qt_d", [NH * HA, T], BF16)
        kto_t = nc.dram_tensor("kt_own_d", [NCH * NH * HA, 512], BF16)
        kta_t = nc.dram_tensor("kt_all_d", [NCH * 2 * NH * HA, 512], BF16)
        vo_t = nc.dram_tensor("v_own_d", [T, D], BF16)
        va_t = nc.dram_tensor("v_all_d", [NCH * 2 * 512, D], BF16)
        kbo_t = nc.dram_tensor("kb_own_d", [128, NT * NH], F32)
        kba_t = nc.dram_tensor("kb_all_d", [256, NT * NH], F32)
        hlo_t = nc.dram_tensor("hl_own_d", [128, KC * 32], BF16)
        hla_t = nc.dram_tensor("hl_all_d", [256, KC * 32], BF16)
        b_qt, b_kto, b_vo = Buf("qt"), Buf("kto"), Buf("vo")
        b_kta = [Buf("kta%d" % c) for c in range(NCH)]
        b_va = [Buf("va%d" % c) for c in range(NCH)]
        b_kbo, b_kba, b_hlo, b_hla = Buf("kbo"), Buf("kba"), Buf("hlo"), Buf("hla")
        GROUPS = [[0, 1], [2, 3], [4, 5], [6, 7]]

        X = k.sb("X", [128, NT, D], F32)
        bX = [Buf("X%d" % t) for t in range(NT)]
        R1 = k.sb("R1", [128, KC, T], BF16)
        SL = k.sb("SL", [128, NSLOT, SLOT], BF16)
        ident = k.sb("ident", [128, 128], BF16)
        identf = k.sb("identf", [128, 128], F32)
        triu = k.sb("triu", [128, 128], F32)
        onesf = k.sb("onesf", [128, 128], F32)
        onesb = k.sb("onesb", [128, 128], BF16)
        invd = k.sb("invd", [128, 128], BF16)
        maskT = k.sb("maskT", [128, 128], BF16)
        gbc = k.sb("gbc", [128, 2, D], F32)
        modc = k.sb("modc", [128, 4, KC], F32)
        cact = k.sb("cact", [128, KC], BF16)
        cin = k.sb("cin", [128, KC], F32)
        flagc = k.sb("flagc", [128, 1], F32)
        mbias = k.sb("mbias", [128, 1], F32)
        epsc = k.sb("epsc", [128, 1], F32)
        rtmp = k.sb("rtmp", [128, 2, 512], F32)
        ssq = k.sb("ssq", [128, NT], F32)
        rstd = k.sb("rstd", [128, NT], F32)
        b_const, b_gbc, b_modc, b_cact, b_rowb, b_nrow, b_ssq = Buf(), Buf(), Buf(), Buf(), [Buf(), Buf(), Buf()], Buf(), Buf()

        PS = [k.ps("ps%d" % i, [128, 512], F32) for i in range(6)]
        bPS = [Buf("ps%d" % i) for i in range(6)]
        PT = [k.ps("pt%d" % i, [128, 1024], BF16) for i in range(2)]
        bPT = [Buf("pt%d" % i) for i in range(2)]

        for b_ in (b_qt, b_kto, b_vo, b_kbo, b_hlo):
            b_.ds = k.dsem("dd_" + b_.name)
        d_x = k.dsem("d_x")
        for b_ in bX:
            b_.ds = d_x
        d_in = k.dsem("d_in")
        d_w = k.dsem("d_w")
        d_st = k.dsem("d_st")
        c_sem = k.dsem("c_sem")

        def slot_bf(i, off, n):
            return SL[:, i, off:off + n]

        def slot_f32(i, off, n):
            return SL[:, i, 2 * off:2 * off + 2 * n].bitcast(F32)

        rowb = slot_f32(2, 0, 1536).rearrange("p (r n) -> p r n", r=3)
        nrow = slot_f32(2, 1536, 2 * D).rearrange("p (r n) -> p r n", r=2)

        def consts():
            w = [b_const]
            k.op(dve, lambda e: e.memset(identf[:], 0.0), writes=w)
            k.op(pool, lambda e: e.affine_select(out=identf[:], in_=identf[:], compare_op=ALU.not_equal, fill=1.0,
                                                 base=0, pattern=[[-1, 128]], channel_multiplier=1), writes=w)
            k.op(dve, lambda e: e.tensor_copy(out=ident[:], in_=identf[:]), writes=w)
            k.op(dve, lambda e: e.memset(onesf[:], 1.0), writes=w)
            k.op(dve, lambda e: e.memset(onesb[:], 1.0), writes=w)
            k.op(dve, lambda e: e.memset(invd[:], 1.0 / D), writes=w)
            k.op(dve, lambda e: e.memset(epsc[:], EPS), writes=w)
            k.op(pool, lambda e: e.affine_select(out=triu[:], in_=onesf[:], compare_op=ALU.is_ge, fill=0.0,
                                                 base=0, pattern=[[1, 128]], channel_multiplier=-1), writes=w)
            k.op(dve, lambda e: e.tensor_scalar(out=identf[:], in0=triu[:], scalar1=-1.0, scalar2=-NEG,
                                                op0=ALU.add, op1=ALU.mult), writes=w)
            k.op(dve, lambda e: e.tensor_copy(out=maskT[:], in_=identf[:]), writes=w)
            k.dma(sp, d_in, flagc[:], flag_d[:, :], writes=w)
            k.dma(sp, d_in, cin[:], cT_d[:, :], writes=w)
            k.op(dve, lambda e: e.tensor_scalar(out=mbias[:], in0=flagc[:], scalar1=-1.0, scalar2=-NEG,
                                                op0=ALU.add, op1=ALU.mult), writes=w)
            k.op(act, lambda e: e.activation(out=cact[:], in_=cin[:], func=AF.Silu), writes=w)

        def load_x():
            xv = x_d.rearrange("(t p) d -> p t d", p=128)
            for t0 in range(0, NT, 4):
                k.dma(sp, d_in, X[:, t0:t0 + 4, :], xv[:, t0:t0 + 4, :], writes=bX[t0:t0 + 4])

        def mods(i):
            wring = [slot_bf(0, 0, KC * 512).rearrange("p (k n) -> p k n", k=KC),
                     slot_bf(1, 0, KC * 512).rearrange("p (k n) -> p k n", k=KC)]
            bw = [Buf(), Buf()]
            wv = wada_d[i].rearrange("(k p) n -> p k n", p=128)
            k.dma(sp, d_in, nrow[0:1, 0, :], nmix_d[i:i + 1, :], writes=[b_nrow])
            k.dma(sp, d_in, nrow[0:1, 1, :], nmlp_d[i:i + 1, :], writes=[b_nrow])
            for n in range(12):
                kind = n // 2
                half = n % 2
                wb, bwb = wring[n % 2], bw[n % 2]
                k.dma(pool, d_w, wb, wv[:, :, n * 512:(n + 1) * 512], writes=[bwb])
                rb = n % 3
                k.dma(sp, d_in, rowb[0:1, rb, :], bada_d[i:i + 1, n * 512:(n + 1) * 512], writes=[b_rowb[rb]])
                ps = PS[n % 2]
                bps = bPS[n % 2]
                for kc in range(KC):
                    k.op(pe, lambda e, kc=kc: e.matmul(ps[0:1, :], cact[:, kc:kc + 1], wb[:, kc, :],
                                                       start=(kc == 0), stop=(kc == KC - 1)),
                         reads=[bwb, b_const], writes=[bps], inc=(kc == KC - 1))
                k.op(dve, lambda e: e.tensor_tensor(out=rowb[0:1, rb, :], in0=ps[0:1, :], in1=rowb[0:1, rb, :], op=ALU.add),
                     reads=[bps], writes=[b_rowb[rb]])
                if kind in (1, 4):
                    g = 0 if kind == 1 else 1
                    k.op(dve, lambda e: e.scalar_tensor_tensor(out=rowb[0:1, rb, :], in0=rowb[0:1, rb, :], scalar=1.0,
                                                               in1=nrow[0:1, g, half * 512:(half + 1) * 512],
                                                               op0=ALU.add, op1=ALU.mult),
                         reads=[b_nrow], writes=[b_rowb[rb]])
                if kind in (2, 5):
                    g = 0 if kind == 2 else 1
                    pb = PS[2 + n % 2]
                    bpb = bPS[2 + n % 2]
                    k.op(pe, lambda e: e.matmul(pb[:, :], onesf[0:1, :], rowb[0:1, rb, :], start=True, stop=True),
                         reads=[b_rowb[rb], b_const], writes=[bpb])
                    k.op(act, lambda e: e.copy(out=gbc[:, g, half * 512:(half + 1) * 512], in_=pb[:, :]),
                         reads=[bpb], writes=[b_gbc])
                else:
                    col = {0: 0, 1: 1, 3: 2, 4: 3}[kind]
                    pb = PS[2 + n % 2]
                    bpb = bPS[2 + n % 2]
                    for q in range(4):
                        k.op(pe, lambda e, q=q: e.matmul(pb[:, q:q + 1], rowb[0:1, rb, q * 128:(q + 1) * 128], onesf[0:1, 0:1],
                                                         start=True, stop=True),
                             reads=[b_rowb[rb], b_const], writes=[bpb], inc=(q == 3))
                    k.op(dve, lambda e: e.tensor_copy(out=modc[:, col, half * 4:(half + 1) * 4], in_=pb[:, 0:4]),
                         reads=[bpb], writes=[b_modc])

        def norm_to_hT(which, bH):
            shc, gc = (0, 1) if which == 0 else (2, 3)
            junk = slot_bf(5, 0, D)
            bj = Buf()
            xs = [slot_bf(5, D * (1 + q), D) for q in range(4)]
            bxs = [Buf() for _ in range(4)]
            for t in range(NT):
                k.op(act, lambda e, t=t: e.activation(out=junk, in_=X[:, t, :], func=AF.Square, accum_out=ssq[:, t:t + 1]),
                     reads=[bX[t]], writes=[bj, b_ssq])
            k.op(act, lambda e: e.activation(out=rstd[:], in_=ssq[:], func=AF.Sqrt, bias=epsc[:, 0:1], scale=1.0 / D),
                 reads=[b_ssq, b_const], writes=[b_ssq])
            k.op(dve, lambda e: e.reciprocal(out=rstd[:], in_=rstd[:]), reads=[b_ssq], writes=[b_ssq])
            for c in range(NCH):
                for q in range(4):
                    t = 4 * c + q
                    k.op(dve, lambda e, t=t, q=q: e.tensor_scalar(out=xs[q], in0=X[:, t, :], scalar1=rstd[:, t:t + 1], scalar2=None,
                                                                  op0=ALU.mult),
                         reads=[bX[t], b_ssq], writes=[bxs[q]])
                for kp in range(4):
                    pt, bpt = PT[kp % 2], bPT[kp % 2]
                    for kk in range(2):
                        kc = 2 * kp + kk
                        for q in range(4):
                            k.op(pe, lambda e, kc=kc, kk=kk, q=q: e.transpose(pt[:, kk * 512 + q * 128: kk * 512 + (q + 1) * 128],
                                                                              xs[q][:, kc * 128:(kc + 1) * 128], ident[:]),
                                 reads=[bxs[q], b_const], writes=[bpt], inc=(kk == 1 and q == 3))
                    for kk in range(2):
                        kc = 2 * kp + kk
                        E = act if kk == 0 else dve
                        if E is act:
                            k.op(act, lambda e, kc=kc, kk=kk: e.activation(out=R1[:, kc, c * 512:(c + 1) * 512], in_=pt[:, kk * 512:(kk + 1) * 512],
                                                                           func=AF.Identity, bias=modc[:, shc, kc:kc + 1], scale=modc[:, gc, kc:kc + 1]),
                                 reads=[bpt, b_modc], writes=[bH[c]])
                        else:
                            k.op(dve, lambda e, kc=kc, kk=kk: e.tensor_scalar(out=R1[:, kc, c * 512:(c + 1) * 512], in0=pt[:, kk * 512:(kk + 1) * 512],
                                                                              scalar1=modc[:, gc, kc:kc + 1], scalar2=modc[:, shc, kc:kc + 1],
                                                                              op0=ALU.mult, op1=ALU.add),
                                 reads=[bpt, b_modc], writes=[bH[c]])

        class Resid:
            def __init__(self, g):
                self.g = g
                self.tmp = [rtmp[:, q, :] for q in range(2)]
                self.bt = [Buf(), Buf()]
                self.n = 0

            def add(self, ps, bps, t, half):
                q = self.n % 2
                self.n += 1
                tmp, bt = self.tmp[q], self.bt[q]
                k.op(dve, lambda e: e.tensor_tensor(out=tmp, in0=ps[:, :], in1=gbc[:, self.g, half * 512:(half + 1) * 512], op=ALU.mult),
                     reads=[bps, b_gbc], writes=[bt])
                k.op(pool, lambda e: e.tensor_tensor(out=X[:, t, half * 512:(half + 1) * 512], in0=X[:, t, half * 512:(half + 1) * 512],
                                                     in1=tmp, op=ALU.add),
                     reads=[bt], writes=[bX[t]])

        def mlp(i, bH):
            hid = [SL[:, q, 0:4 * T].rearrange("p (f t) -> p f t", f=4) for q in range(2)]
            bhid = [[Buf() for _ in range(NCH)] for _ in range(2)]
            wi = [slot_bf(2 + q, 0, KC * 512).rearrange("p (k n) -> p k n", k=KC) for q in range(3)]
            wo = [slot_bf(2 + q, KC * 512, 4 * D).rearrange("p (f n) -> p f n", f=4) for q in range(3)]
            bwi = [Buf() for _ in range(3)]
            bwo = [Buf() for _ in range(3)]
            rr = [slot_f32(5, 512 * q, 512) for q in range(2)]
            brr = [Buf(), Buf()]
            wiv = wmi_d[i].rearrange("(k p) f -> p k f", p=128)
            wov = wmo_d[i].rearrange("(f p) d -> p f d", p=128)
            res = Resid(1)
            NG = DFF // 512

            def load(g):
                s = g % 3
                k.dma(pool, d_w, wi[s], wiv[:, :, g * 512:(g + 1) * 512], writes=[bwi[s]])
                k.dma(pool, d_w, wo[s], wov[:, g * 4:(g + 1) * 4, :], writes=[bwo[s]])

            load(0)
            load(1)
            nps = 0
            nr = 0
            for g in range(NG):
                s = g % 3
                hb = g % 2
                if g + 2 < NG:
                    load(g + 2)
                for c in range(NCH):
                    for fc in range(4):
                        ps, bps = PS[nps % 3], bPS[nps % 3]
                        nps += 1
                        for kc in range(KC):
                            k.op(pe, lambda e, kc=kc, fc=fc: e.matmul(ps[:, :], wi[s][:, kc, fc * 128:(fc + 1) * 128], R1[:, kc, c * 512:(c + 1) * 512],
                                                                      start=(kc == 0), stop=(kc == KC - 1)),
                                 reads=[bwi[s], bH[c]], writes=[bps], inc=(kc == KC - 1))
                        r, br = rr[nr % 2], brr[nr % 2]
                        nr += 1
                        k.op(act, lambda e: e.activation(out=r, in_=ps[:, :], func=AF.Relu), reads=[bps], writes=[br])
                        k.op(dve, lambda e, fc=fc: e.tensor_tensor(out=hid[hb][:, fc, c * 512:(c + 1) * 512], in0=r, in1=r, op=ALU.mult),
                             reads=[br], writes=[bhid[hb][c]])
                for t in range(NT):
                    for half in range(2):
                        ps, bps = PS[3 + nps % 3], bPS[3 + nps % 3]
                        nps += 1
                        for fc in range(4):
                            k.op(pe, lambda e, fc=fc: e.matmul(ps[:, :], hid[hb][:, fc, t * 128:(t + 1) * 128], wo[s][:, fc, half * 512:(half + 1) * 512],
                                                               start=(fc == 0), stop=(fc == 3)),
                                 reads=[bwo[s], bhid[hb][t // 4]], writes=[bps], inc=(fc == 3))
                        res.add(ps, bps, t, half)

        def outproj(w_dram, bH, gate, wslot, bias_row=None):
            w = slot_bf(wslot, 0, KC * D).rearrange("p (k n) -> p k n", k=KC)
            bw = Buf()
            k.dma(pool, d_w, w, w_dram.rearrange("(k p) n -> p k n", p=128), writes=[bw])
            res = Resid(gate)
            n = 0
            for t in range(NT):
                for half in range(2):
                    ps, bps = PS[n % 3], bPS[n % 3]
                    n += 1
                    for kc in range(KC):
                        last = (kc == KC - 1) and bias_row is None
                        k.op(pe, lambda e, kc=kc: e.matmul(ps[:, :], R1[:, kc, t * 128:(t + 1) * 128], w[:, kc, half * 512:(half + 1) * 512],
                                                           start=(kc == 0), stop=last),
                             reads=[bw, bH[t // 4]], writes=[bps], inc=last)
                    if bias_row is not None:
                        brow, bbrow = bias_row
                        k.op(pe, lambda e: e.matmul(ps[:, :], onesf[0:1, :], brow[0:1, half * 512:(half + 1) * 512], start=False, stop=True),
                             reads=[bbrow, b_const], writes=[bps])
                    res.add(ps, bps, t, half)

        def fox(j, bH):
            NFC = NT * NH
            lgf = slot_f32(4, 0, NFC).rearrange("p (t h) -> p t h", h=NH)
            nf = slot_f32(4, NFC, NFC).rearrange("p (t h) -> p t h", h=NH)
            kbp = slot_f32(4, 2 * NFC, NFC).rearrange("p (t h) -> p t h", h=NH)
            ftmp = slot_f32(4, 4 * NFC, NFC).rearrange("p (t h) -> p t h", h=NH)
            fs3 = slot_bf(4, 10 * NFC, 3 * NFC).rearrange("p (t h r) -> p t h r", h=NH, r=3)
            bfb = slot_f32(4, 7 * NFC, NH)
            qgb = slot_f32(4, 7 * NFC + 16, HD)
            kgb = slot_f32(4, 7 * NFC + 80, HD)
            b_f = Buf()
            b_fs3 = Buf()
            b_g = Buf()
            k.dma(sp, d_in, bfb, fbf_d[j], writes=[b_g])
            k.dma(sp, d_in, qgb, fqg_d[j], writes=[b_g])
            k.dma(sp, d_in, kgb, fkg_d[j], writes=[b_g])
            k.op(dve, lambda e: e.scalar_tensor_tensor(out=qgb, in0=qgb, scalar=HD ** -0.5, in1=kgb, op0=ALU.mult, op1=ALU.mult),
                 reads=[], writes=[b_g])
            wv = fwin_d[j].rearrange("(k p) n -> p k n", p=128)
            wring = [slot_bf(q, 0, KC * 512).rearrange("p (k n) -> p k n", k=KC) for q in range(3)]
            bwr = [Buf() for _ in range(3)]
            chunks = [("f", 3 * D, NH)] + [("q", qc * 512, 512) for qc in range(2)] + [("k", D + qc * 512, 512) for qc in range(2)] + \
                     [("v", 2 * D + qc * 512, 512) for qc in range(2)]

            def loadw(ci):
                kind, off, n = chunks[ci]
                s = ci % 3
                k.dma(pool, d_w, wring[s][:, :, 0:n], wv[:, :, off:off + n], writes=[bwr[s]])

            loadw(0)
            loadw(1)
            sq = [slot_f32(3, 512 * q, 512) for q in range(2)]
            bsq = [Buf(), Buf()]
            kf = [slot_f32(3, 1024 + 512 * q, 512) for q in range(2)]
            bkf = [Buf(), Buf()]
            ssh = [slot_f32(3, 2048 + 16 * q, 8) for q in range(2)]
            bssh = [Buf(), Buf()]
            qa = [slot_bf(3, 4224 + 8 * HA * q, 8 * HA).rearrange("p (h r) -> p h r", r=HA) for q in range(2)]
            bqa = [Buf(), Buf()]
            vst = [slot_bf(3, 4224 + 16 * HA + 512 * q, 512) for q in range(2)]
            bvst = [Buf(), Buf()]
            stg = [slot_bf(5, 4096 * q, 4096).rearrange("p (h t) -> p h t", h=8) for q in range(2)]
            bstg = [Buf(), Buf()]
            qtv = qt_t.ap().rearrange("(h r) t -> r h t", r=HA)
            ktv = kto_t.ap().rearrange("(c h r) t -> c r h t", c=NCH, r=HA)
            nps = 0
            nu = 0
            nstg = 0
            for ci, (kind, off, n) in enumerate(chunks):
                s = ci % 3
                if ci + 2 < len(chunks):
                    loadw(ci + 2)
                w = wring[s]
                for t in range(NT):
                    ps, bps = PS[nps % 4], bPS[nps % 4]
                    nps += 1
                    for kc in range(KC):
                        k.op(pe, lambda e, kc=kc: e.matmul(ps[:, 0:n], R1[:, kc, t * 128:(t + 1) * 128], w[:, kc, 0:n],
                                                           start=(kc == 0), stop=(kc == KC - 1)),
                             reads=[bwr[s], bH[t // 4]], writes=[bps], inc=(kc == KC - 1))
                    u = nu % 2
                    nu += 1
                    if kind == "f":
                        k.op(dve, lambda e, t=t: e.tensor_tensor(out=lgf[:, t, :], in0=ps[:, 0:NH], in1=bfb, op=ALU.add),
                             reads=[bps, b_g], writes=[b_f])
                    elif kind == "v":
                        qc = (off - 2 * D) // 512
                        k.op(act, lambda e: e.copy(out=vst[u], in_=ps[:, :]), reads=[bps], writes=[bvst[u]])
                        k.dma(sp, d_st, vo_t.ap()[t * 128:(t + 1) * 128, qc * 512:(qc + 1) * 512], vst[u], reads=[bvst[u]], writes=[b_vo])
                    else:
                        qc = (off % D) // 512
                        k.op(act, lambda e: e.activation(out=sq[u], in_=ps[:, :], func=AF.Square), reads=[bps], writes=[bsq[u]])
                        k.op(dve, lambda e: e.tensor_reduce(out=ssh[u], in_=sq[u].rearrange("p (h d) -> p h d", d=HD), axis=AX.X, op=ALU.add),
                             reads=[bsq[u]], writes=[bssh[u]])
                        k.op(act, lambda e: e.activation(out=ssh[u], in_=ssh[u], func=AF.Sqrt, bias=epsc[:, 0:1], scale=1.0 / HD),
                             reads=[b_const], writes=[bssh[u]])
                        k.op(dve, lambda e: e.reciprocal(out=ssh[u], in_=ssh[u]), writes=[bssh[u]])
                        rb = ssh[u].unsqueeze(2).to_broadcast([128, 8, HD])
                        psv = ps[:, :].rearrange("p (h d) -> p h d", d=HD)
                        if kind == "q":
                            k.op(dve, lambda e: e.tensor_tensor(out=qa[u][:, :, 0:HD], in0=psv, in1=rb, op=ALU.mult),
                                 reads=[bps, bssh[u]], writes=[bqa[u]])
                            k.op(pool, lambda e, t=t, qc=qc: e.tensor_copy(out=qa[u][:, :, HD:HA], in_=fs3[:, t, qc * 8:(qc + 1) * 8, :]),
                                 reads=[b_fs3], writes=[bqa[u]])
                        else:
                            kfv = kf[u].rearrange("p (h d) -> p h d", d=HD)
                            k.op(dve, lambda e: e.tensor_tensor(out=kfv, in0=psv, in1=rb, op=ALU.mult),
                                 reads=[bps, bssh[u]], writes=[bkf[u]])
                            k.op(pool, lambda e: e.tensor_tensor(out=qa[u][:, :, 0:HD], in0=kfv, in1=qgb.unsqueeze(1).to_broadcast([128, 8, HD]), op=ALU.mult),
                                 reads=[bkf[u], b_g], writes=[bqa[u]])
                            k.op(pool, lambda e: e.memset(qa[u][:, :, HD:HA], 1.0), writes=[bqa[u]])
                        pt, bpt = PT[u], bPT[u]
                        for h in range(8):
                            k.op(pe, lambda e, h=h: e.transpose(pt[0:HA, h * 128:(h + 1) * 128], qa[u][:, h, :], ident[:]),
                                 reads=[bqa[u], b_const], writes=[bpt], inc=(h == 7))
                        sg_ = nstg % 2
                        tq = t % 4
                        k.op(act, lambda e, tq=tq: e.copy(out=stg[sg_][0:HA, :, tq * 128:(tq + 1) * 128],
                                                          in_=pt[0:HA, :].rearrange("p (h t) -> p h t", h=8)),
                             reads=[bpt], writes=[bstg[sg_]])
                        if tq == 3:
                            c = t // 4
                            if kind == "q":
                                dst = qtv[:, qc * 8:(qc + 1) * 8, c * 512:(c + 1) * 512]
                            else:
                                dst = ktv[c][:, qc * 8:(qc + 1) * 8, :]
                            k.dma(sp, d_st, dst, stg[sg_][0:HA, :, :], reads=[bstg[sg_]], writes=[b_qt if kind == "q" else b_kto])
                            nstg += 1
                if kind == "f":
                    k.op(act, lambda e: e.activation(out=lgf, in_=lgf, func=AF.Exp, scale=-1.0), writes=[b_f])
                    k.op(act, lambda e: e.activation(out=lgf, in_=lgf, func=AF.Ln, bias=1.0, scale=1.0), writes=[b_f])
                    for t in range(NT):
                        ps, bps = PS[4 + t % 2], bPS[4 + t % 2]
                        for t2 in range(t + 1):
                            k.op(pe, lambda e, t2=t2, t=t: e.matmul(ps[:, 0:NH], triu[:] if t2 == t else onesf[:], lgf[:, t2, :],
                                                                    start=(t2 == 0), stop=(t2 == t)),
                                 reads=[b_f, b_const], writes=[bps], inc=(t2 == t))
                        k.op(act, lambda e, t=t: e.copy(out=nf[:, t, :], in_=ps[:, 0:NH]), reads=[bps], writes=[b_f])
                    ps, bps = PS[4], bPS[4]
                    for t2 in range(NT):
                        k.op(pe, lambda e, t2=t2: e.matmul(ps[:, 0:NH], onesf[:], lgf[:, t2, :], start=(t2 == 0), stop=(t2 == NT - 1)),
                             reads=[b_f, b_const], writes=[bps], inc=(t2 == NT - 1))
                    k.op(dve, lambda e: e.tensor_tensor(out=kbp, in0=nf, in1=ps[:, 0:NH].unsqueeze(1).to_broadcast([128, NT, NH]), op=ALU.subtract),
                         reads=[bps], writes=[b_f])
                    k.dma(sp, d_st, kbo_t.ap(), kbp.rearrange("p t h -> p (t h)"), reads=[b_f], writes=[b_kbo])
                    k.op(dve, lambda e: e.tensor_scalar(out=ftmp, in0=nf, scalar1=-1.0, scalar2=None, op0=ALU.mult), writes=[b_f])
                    for r in range(3):
                        k.op(dve, lambda e, r=r: e.tensor_copy(out=fs3[:, :, :, r], in_=ftmp), writes=[b_f, b_fs3])
                        if r < 2:
                            k.op(dve, lambda e, r=r: e.tensor_tensor(out=ftmp, in0=ftmp, in1=fs3[:, :, :, r], op=ALU.subtract), writes=[b_f, b_fs3])
            KR = NH * HA
            for c in range(NCH):
                k.coll(c_sem, "AllGather", GROUPS, kto_t.ap()[c * KR:(c + 1) * KR, :].opt(), kta_t.ap()[2 * c * KR:2 * (c + 1) * KR, :].opt(),
                       reads=[b_kto], writes=[b_kta[c]])
                k.coll(c_sem, "AllGather", GROUPS, vo_t.ap()[c * 512:(c + 1) * 512, :].opt(), va_t.ap()[c * 1024:(c + 1) * 1024, :].opt(),
                       reads=[b_vo], writes=[b_va[c]])
            k.coll(c_sem, "AllGather", GROUPS, kbo_t.ap().opt(), kba_t.ap().opt(), reads=[b_kbo], writes=[b_kba])
            k.barrier()
            kbias = slot_f32(4, 2 * NFC, 2 * NFC).rearrange("p (s t h) -> p s t h", s=2, h=NH)
            b_kb = Buf()
            k.op(dve, lambda e: e.tensor_copy(out=kbias[:, 1], in_=nf), writes=[b_kb])
            k.dma(sp, d_in, kbias[:, 0], kba_t.ap()[0:128, :].rearrange("p (t h) -> p t h", h=NH), reads=[b_kba], writes=[b_kb])
            k.op(dve, lambda e: e.tensor_scalar(out=kbias[:, 0], in0=kbias[:, 0], scalar1=mbias[:, 0:1], scalar2=None, op0=ALU.add),
                 reads=[b_const], writes=[b_kb])
            bO = [Buf() for _ in range(NCH)]
            ktav = kta_t.ap().rearrange("(c s h r) t -> r c s h t", c=NCH, s=2, r=HA)
            ktov = kto_t.ap().rearrange("(c h r) t -> r c h t", c=NCH, r=HA)
            vav = va_t.ap().rearrange("(c s q p) d -> p c s q d", c=NCH, s=2, p=128)
            vov = vo_t.ap().rearrange("(t p) d -> p t d", p=128)
            KTb = [SL[:, 2 * q, 0:4 * T].rearrange("p (h s t) -> p h s t", h=2, s=2) for q in range(2)]
            QTb = [SL[:, 2 * q + 1, 0:2 * T].rearrange("p (h t) -> p h t", h=2) for q in range(2)]
            Vb = [SL[:, 2 * q + 1, 2 * T:2 * T + 2 * NT * 128].rearrange("p (s t d) -> p s t d", s=2, d=128) for q in range(2)]
            bKQV = [Buf(), Buf()]
            Pb = [slot_bf(5, 512 * q, 512) for q in range(4)]
            bP = [Buf() for _ in range(4)]
            rc = [slot_f32(5, 1024 + 512 * q, 512) for q in range(2)]
            brc = [Buf(), Buf()]

            def loadpair(jp):
                q = jp % 2
                wr = [bKQV[q]]
                for hh in range(2):
                    h = 2 * jp + hh
                    k.dma(sp, d_in, KTb[q][0:HA, hh, 0, :].rearrange("r (c t) -> r c t", t=512), ktav[:, :, 0, h, :], reads=b_kta, writes=wr)
                    k.dma(sp, d_in, KTb[q][0:HA, hh, 1, :].rearrange("r (c t) -> r c t", t=512), ktov[:, :, h, :], reads=[b_kto], writes=wr)
                    k.dma(sp, d_in, QTb[q][0:HA, hh, :], qtv[:, h, :], reads=[b_qt], writes=wr)
                for c in range(NCH):
                    k.dma(sp, d_in, Vb[q][:, 0, 4 * c:4 * c + 4, :], vav[:, c, 0, :, jp * 128:(jp + 1) * 128], reads=[b_va[c]], writes=wr)
                k.dma(sp, d_in, Vb[q][:, 1], vov[:, :, jp * 128:(jp + 1) * 128], reads=[b_vo], writes=wr)

            Sb = [PS[0], PS[1], PT[0][:, :].bitcast(F32), PT[1][:, :].bitcast(F32)]
            bSb = [bPS[0], bPS[1], bPT[0], bPT[1]]
            NSB = 4
            SDEPTH = 3
            units = []
            item = 0
            for jp in range(8):
                for hh in range(2):
                    for c in range(NCH):
                        blocks = [(0, kb) for kb in range(NT)] + [(1, kb) for kb in range(4 * c + 4)]
                        for bi, (src, kb) in enumerate(blocks):
                            units.append((jp, hh, c, bi, len(blocks), src, kb, item))
                        item += 1
            loaded = set()

            def geom(u):
                jp, hh, c, bi, nb, src, kb, item = u
                diag = (src == 1 and kb >= 4 * c)
                i = kb - 4 * c if diag else 0
                return diag, i, c * 512 + i * 128, 512 - i * 128

            def emit_S(ui):
                u = units[ui]
                jp, hh, c, bi, nb, src, kb, item = u
                q = jp % 2
                diag, i, q0, n = geom(u)
                ps, bps = Sb[ui % NSB], bSb[ui % NSB]
                k.op(pe, lambda e: e.matmul(ps[:, 0:n], KTb[q][0:HA, hh, src, kb * 128:(kb + 1) * 128], QTb[q][0:HA, hh, q0:q0 + n],
                                            start=True, stop=(not diag)),
                     reads=[bKQV[q]], writes=[bps], inc=(not diag))
                if diag:
                    k.op(pe, lambda e: e.matmul(ps[:, 0:128], ident[:], maskT[:], start=False, stop=True),
                         reads=[b_const], writes=[bps])

            def emit_rest(ui):
                u = units[ui]
                jp, hh, c, bi, nb, src, kb, item = u
                q = jp % 2
                h = 2 * jp + hh
                rows = slice(hh * 64, hh * 64 + 64)
                diag, i, q0, n = geom(u)
                ps, bps = Sb[ui % NSB], bSb[ui % NSB]
                po, bpo = PS[2 + item % 2], bPS[2 + item % 2]
                pm, bpm = PS[4 + item % 2], bPS[4 + item % 2]
                pb, bpb = Pb[ui % 4], bP[ui % 4]
                k.op(act, lambda e: e.activation(out=pb[:, 0:n], in_=ps[:, 0:n], func=AF.Exp, bias=kbias[:, src, kb, h:h + 1], scale=1.0),
                     reads=[bps, b_kb], writes=[bpb])
                first = (bi == 0)
                last = (bi == nb - 1)
                k.op(pe, lambda e: e.matmul(po[rows, i * 128:512], Vb[q][:, src, kb, hh * 64:(hh + 1) * 64], pb[:, 0:n], start=first, stop=last),
                     reads=[bpb, bKQV[q]], writes=[bpo], inc=False)
                k.op(pe, lambda e: e.matmul(pm[rows, i * 128:512], onesb[:, 0:64], pb[:, 0:n], start=first, stop=last),
                     reads=[bpb, b_const], writes=[bpm], inc=True)
                if last:
                    uu = item % 2
                    k.op(dve, lambda e: e.reciprocal(out=rc[uu][rows, :], in_=pm[rows, :]), reads=[bpm], writes=[brc[uu]])
                    k.op(dve, lambda e: e.tensor_tensor(out=R1[rows, jp, c * 512:(c + 1) * 512], in0=po[rows, :], in1=rc[uu][rows, :], op=ALU.mult),
                         reads=[bpo, brc[uu]], writes=[bO[c]])
                    if hh == 1 and c == NCH - 1 and jp + 2 < 8:
                        loadpair(jp + 2)

            loadpair(0)
            loadpair(1)
            for ui in range(len(units) + SDEPTH):
                if ui < len(units):
                    emit_S(ui)
                if ui - SDEPTH >= 0:
                    emit_rest(ui - SDEPTH)
            k.barrier()
            outproj(fwo_d[j], bO, 0, 0)

        def gmlp(bH):
            w_in = SL[:, 0:2, :].rearrange("p s n -> p (s n)")[:, 0:KC * 2 * D].rearrange("p (k n) -> p k n", k=KC)
            bw = Buf()
            k.dma(pool, d_w, w_in, swin_d.rearrange("(k p) n -> p k n", p=128), writes=[bw])
            w_o = slot_bf(2, 0, KC * D).rearrange("p (k n) -> p k n", k=KC)
            bwo = Buf()
            k.dma(pool, d_w, w_o, swo_d.rearrange("(k p) n -> p k n", p=128), writes=[bwo])
            lng = slot_f32(3, 0, D)
            lnb = slot_f32(3, D, D)
            wsf = slot_f32(3, 2 * D, D).rearrange("p (g s) -> p g s", g=8)
            wsb = slot_bf(3, 6 * D, D).rearrange("p (g s) -> p g s", g=8)
            wsT = slot_bf(3, 7 * D, D).rearrange("p (g t) -> p g t", g=8)
            bsc = slot_f32(5, 100, 8)
            b_p = Buf()
            k.dma(sp, d_in, lng, slng_d[:, :], writes=[b_p])
            k.dma(sp, d_in, lnb, slnb_d[:, :], writes=[b_p])
            k.dma(sp, d_in, wsf, sws_d.rearrange("g t s -> t g s"), writes=[b_p])
            k.dma(sp, d_in, bsc, sbs_d[:, :], writes=[b_p])
            k.op(dve, lambda e: e.tensor_copy(out=wsb, in_=wsf), writes=[b_p])
            for g in range(8):
                k.op(pe, lambda e, g=g: e.transpose(PT[0][:, g * 128:(g + 1) * 128], wsb[:, g, :], ident[:]),
                     reads=[b_p, b_const], writes=[bPT[0]], inc=(g == 7))
            k.op(dve, lambda e: e.tensor_copy(out=wsT, in_=PT[0][:, :].rearrange("p (g t) -> p g t", g=8)), reads=[bPT[0]], writes=[b_p])
            k.op(dve, lambda e: e.memset(wsT[64:128, :, 0:64], 0.0), writes=[b_p])
            ub = [slot_bf(4, D * q, D) for q in range(2)]
            vf = [slot_f32(4, D + D * q, D) for q in range(2)]
            vnb = [slot_bf(4, 6 * D + D * q, D) for q in range(2)]
            st6 = [slot_f32(5, 16 * q, 12) for q in range(2)]
            mv = [slot_f32(5, 64 + 16 * q, 2) for q in range(2)]
            pbuf = [slot_bf(5, 256 + D * q, D) for q in range(2)]
            pT = [slot_bf(5, 256 + 2 * D + D * q, D).rearrange("p (k t) -> p k t", k=KC) for q in range(2)]
            bu, bv, bvn, bst, bpb, bpT = ([Buf(), Buf()] for _ in range(6))
            res = Resid(0)
            for t in range(NT):
                u = t % 2
                for n4 in range(4):
                    ps, bps = PS[n4], bPS[n4]
                    for kc in range(KC):
                        k.op(pe, lambda e, kc=kc, n4=n4: e.matmul(ps[:, :], R1[:, kc, t * 128:(t + 1) * 128], w_in[:, kc, n4 * 512:(n4 + 1) * 512],
                                                                  start=(kc == 0), stop=(kc == KC - 1)),
                             reads=[bw, bH[t // 4]], writes=[bps], inc=(kc == KC - 1))
                    if n4 < 2:
                        k.op(act, lambda e, n4=n4: e.activation(out=ub[u][:, n4 * 512:(n4 + 1) * 512], in_=ps[:, :], func=AF.Gelu_apprx_tanh),
                             reads=[bps], writes=[bu[u]])
                    else:
                        k.op(act, lambda e, n4=n4: e.activation(out=vf[u][:, (n4 - 2) * 512:(n4 - 1) * 512], in_=ps[:, :], func=AF.Gelu_apprx_tanh),
                             reads=[bps], writes=[bv[u]])
                for hf in range(2):
                    k.op(dve, lambda e, hf=hf: e.bn_stats(out=st6[u][:, hf * 6:(hf + 1) * 6], in_=vf[u][:, hf * 512:(hf + 1) * 512]),
                         reads=[bv[u]], writes=[bst[u]])
                k.op(dve, lambda e: e.bn_aggr(out=mv[u], in_=st6[u].rearrange("p (c s) -> p c s", s=6)), writes=[bst[u]])
                k.op(act, lambda e: e.activation(out=mv[u][:, 1:2], in_=mv[u][:, 1:2], func=AF.Sqrt, bias=epsc[:, 0:1], scale=1.0),
                     reads=[b_const], writes=[bst[u]])
                k.op(dve, lambda e: e.reciprocal(out=mv[u][:, 1:2], in_=mv[u][:, 1:2]), writes=[bst[u]])
                k.op(dve, lambda e: e.tensor_scalar(out=vf[u], in0=vf[u], scalar1=mv[u][:, 0:1], scalar2=mv[u][:, 1:2],
                                                    op0=ALU.subtract, op1=ALU.mult), reads=[bst[u]], writes=[bv[u]])
                k.op(pool, lambda e: e.tensor_tensor(out=vf[u], in0=vf[u], in1=lng, op=ALU.mult), reads=[b_p], writes=[bv[u]])
                k.op(pool, lambda e: e.tensor_tensor(out=vnb[u], in0=vf[u], in1=lnb, op=ALU.add), reads=[b_p, bv[u]], writes=[bvn[u]])
                for hf in range(2):
                    ps, bps = PS[4 + hf], bPS[4 + hf]
                    for gg in range(4):
                        g = hf * 4 + gg
                        k.op(pe, lambda e, g=g, gg=gg: e.matmul(ps[:, gg * 128:(gg + 1) * 128], wsT[:, g, :], vnb[u][:, g * 128:(g + 1) * 128],
                                                                start=True, stop=True),
                             reads=[b_p, bvn[u]], writes=[bps], inc=(gg == 3))
                    for gg in range(4):
                        g = hf * 4 + gg
                        k.op(dve, lambda e, g=g, gg=gg: e.scalar_tensor_tensor(out=pbuf[u][:, g * 128:(g + 1) * 128], in0=ps[:, gg * 128:(gg + 1) * 128],
                                                                               scalar=bsc[:, g:g + 1], in1=ub[u][:, g * 128:(g + 1) * 128],
                                                                               op0=ALU.add, op1=ALU.mult),
                             reads=[bps, bu[u], b_p], writes=[bpb[u]])
                pt, bpt = PT[u], bPT[u]
                for kc in range(KC):
                    k.op(pe, lambda e, kc=kc: e.transpose(pt[:, kc * 128:(kc + 1) * 128], pbuf[u][:, kc * 128:(kc + 1) * 128], ident[:]),
                         reads=[bpb[u], b_const], writes=[bpt], inc=(kc == KC - 1))
                k.op(act, lambda e: e.copy(out=pT[u], in_=pt[:, :].rearrange("p (k t) -> p k t", k=KC)), reads=[bpt], writes=[bpT[u]])
                for half in range(2):
                    ps, bps = PS[half], bPS[half]
                    for kc in range(KC):
                        k.op(pe, lambda e, kc=kc, half=half: e.matmul(ps[:, :], pT[u][:, kc, :], w_o[:, kc, half * 512:(half + 1) * 512],
                                                                      start=(kc == 0), stop=(kc == KC - 1)),
                             reads=[bwo, bpT[u]], writes=[bps], inc=(kc == KC - 1))
                    res.add(ps, bps, t, half)

        def conv(bH):
            YW = 32 + T
            Y = SL[:, 0:3, :].rearrange("p s n -> p (s n)")[:, 0:KC * YW].rearrange("p (k t) -> p k t", k=KC)
            bY = [Buf() for _ in range(KC)]
            bHalo = Buf()
            cols = slot_f32(3, 0, 16 + 4 * KC + KC * CW)
            b1c = cols[:, 0:16]
            bdc = cols[:, 16:16 + KC]
            lgc = cols[:, 16 + KC:16 + 2 * KC]
            lbc = cols[:, 16 + 2 * KC:16 + 3 * KC]
            wdc = cols[:, 16 + 4 * KC:16 + 4 * KC + KC * CW].rearrange("p (k w) -> p k w", k=KC)
            b2row = slot_f32(3, 512, D)
            b_p = Buf()
            k.dma(sp, d_in, b1c, cb1_d[:, :], writes=[b_p])
            k.dma(sp, d_in, bdc, cbd_d[:, :], writes=[b_p])
            k.dma(sp, d_in, lgc, clg_d[:, :], writes=[b_p])
            k.dma(sp, d_in, lbc, clb_d[:, :], writes=[b_p])
            k.dma(sp, d_in, wdc, cwd_d.rearrange("p (k w) -> p k w", k=KC), writes=[b_p])
            k.dma(sp, d_in, b2row[0:1, :], cb2_d[:, :], writes=[b_p])
            wv = cw1_d.rearrange("(k p) n -> p k n", p=128)
            wr = [slot_bf(4, 2048 * q, 2048).rearrange("p (k h n) -> p k h n", k=KC, h=2) for q in range(3)]
            bwr = [Buf() for _ in range(3)]
            sgm = [slot_f32(5, 512 * q, 512) for q in range(2)]
            bsg = [Buf(), Buf()]
            hst = slot_bf(5, 4096, KC * 32).rearrange("p (k t) -> p k t", k=KC)
            hin = slot_bf(5, 4096 + KC * 32, KC * 32).rearrange("p (k t) -> p k t", k=KC)
            b_h = Buf()

            def loadw(jc):
                s = jc % 3
                k.dma(pool, d_w, wr[s][:, :, 0, :], wv[:, :, jc * 128:(jc + 1) * 128], writes=[bwr[s]])
                k.dma(pool, d_w, wr[s][:, :, 1, :], wv[:, :, D + jc * 128:D + (jc + 1) * 128], writes=[bwr[s]])

            loadw(0)
            loadw(1)
            n = 0
            corder = [NCH - 1] + list(range(NCH - 1))
            for jc in range(KC):
                s = jc % 3
                if jc + 2 < KC:
                    loadw(jc + 2)
                for c in corder:
                    pa, bpa = PS[(2 * n) % 4], bPS[(2 * n) % 4]
                    pg, bpg = PS[(2 * n + 1) % 4], bPS[(2 * n + 1) % 4]
                    u = n % 2
                    n += 1
                    for hf, (pp, bpp) in enumerate(((pa, bpa), (pg, bpg))):
                        for kc in range(KC):
                            k.op(pe, lambda e, kc=kc, hf=hf, pp=pp: e.matmul(pp[:, :], wr[s][:, kc, hf, :], R1[:, kc, c * 512:(c + 1) * 512],
                                                                             start=(kc == 0), stop=(kc == KC - 1)),
                                 reads=[bwr[s], bH[c]], writes=[bpp], inc=(kc == KC - 1))
                    k.op(act, lambda e, jc=jc: e.activation(out=sgm[u], in_=pg[:, :], func=AF.Sigmoid, bias=b1c[:, 8 + jc:9 + jc], scale=1.0),
                         reads=[bpg, b_p], writes=[bsg[u]])
                    k.op(dve, lambda e, jc=jc, c=c: e.scalar_tensor_tensor(out=Y[:, jc, 32 + c * 512:32 + (c + 1) * 512], in0=pa[:, :],
                                                                           scalar=b1c[:, jc:jc + 1], in1=sgm[u], op0=ALU.add, op1=ALU.mult),
                         reads=[bpa, bsg[u], b_p], writes=[bY[jc]])
            k.op(dve, lambda e: e.tensor_copy(out=hst, in_=Y[:, :, T:T + 32]), reads=bY, writes=[b_h])
            k.dma(sp, d_st, hlo_t.ap(), hst.rearrange("p k t -> p (k t)"), reads=[b_h], writes=[b_hlo])
            k.coll(c_sem, "AllGather", GROUPS, hlo_t.ap().opt(), hla_t.ap().opt(), reads=[b_hlo], writes=[b_hla])
            k.dma(sp, d_in, hin.rearrange("p k t -> p (k t)"), hla_t.ap()[0:128, :], reads=[b_hla], writes=[b_h])
            k.op(dve, lambda e: e.tensor_scalar(out=Y[:, :, 0:32], in0=hin, scalar1=flagc[:, 0:1], scalar2=None, op0=ALU.mult),
                 reads=[b_h, b_const], writes=bY)
            k.barrier()
            dgb = [SL[:, 4, 0:CW * 128].rearrange("p (w m) -> p w m", w=CW), SL[:, 5, 0:CW * 128].rearrange("p (w m) -> p w m", w=CW)]
            bdg = [Buf(), Buf()]
            bC = [Buf() for _ in range(NCH)]
            n = 0
            for jc in range(KC):
                u = jc % 2
                for w in range(CW):
                    E = dve if w % 2 == 0 else pool
                    k.op(E, lambda e, w=w, jc=jc: e.tensor_scalar(out=dgb[u][:, w, :], in0=ident[:], scalar1=wdc[:, jc, w:w + 1], scalar2=None, op0=ALU.mult),
                         reads=[b_p, b_const], writes=[bdg[u]])
                for c in range(NCH):
                    ps, bps = PS[n % 4], bPS[n % 4]
                    n += 1
                    for w in range(CW):
                        o = 32 + c * 512 - (CW - 1) + w
                        k.op(pe, lambda e, w=w, o=o, jc=jc: e.matmul(ps[:, :], dgb[u][:, w, :], Y[:, jc, o:o + 512], start=(w == 0), stop=(w == CW - 1)),
                             reads=[bdg[u], bY[jc]], writes=[bps], inc=(w == CW - 1))
                    k.op(act, lambda e, jc=jc, c=c: e.activation(out=R1[:, jc, c * 512:(c + 1) * 512], in_=ps[:, :], func=AF.Identity,
                                                                 bias=bdc[:, jc:jc + 1], scale=1.0),
                         reads=[bps, b_p], writes=[bC[c]])
            k.barrier()
            w2 = slot_bf(0, 0, KC * D).rearrange("p (k n) -> p k n", k=KC)
            bw2 = Buf()
            k.dma(pool, d_w, w2, cw2_d.rearrange("(k p) n -> p k n", p=128), writes=[bw2])
            ysq = [slot_bf(1, 4096 * q, 4096).rearrange("p (k t) -> p k t", k=KC) for q in range(2)]
            bys = [Buf(), Buf()]
            zT = [slot_bf(2, 4096 * q, 4096).rearrange("p (k t) -> p k t", k=KC) for q in range(2)]
            bz = [Buf(), Buf()]
            Rr = [slot_f32(4, 1024 * q, 512) for q in range(2)]
            MR = [slot_f32(4, 1024 * q + 512, 512) for q in range(2)]
            bR = [Buf(), Buf()]
            zt = [slot_f32(5, 512 * q, 512) for q in range(2)]
            bzt = [Buf(), Buf()]
            res = Resid(0)
            nz = 0
            nres = 0
            for c in range(NCH):
                u = c % 2
                for jc in range(KC):
                    k.op(pool, lambda e, jc=jc: e.tensor_tensor(out=ysq[u][:, jc, :], in0=R1[:, jc, c * 512:(c + 1) * 512], in1=R1[:, jc, c * 512:(c + 1) * 512], op=ALU.mult),
                         reads=[bC[c]], writes=[bys[u]])
                pm, bpm = PS[4], bPS[4]
                pq, bpq = PS[5], bPS[5]
                for jc in range(KC):
                    k.op(pe, lambda e, jc=jc: e.matmul(pm[:, :], invd[:], R1[:, jc, c * 512:(c + 1) * 512], start=(jc == 0), stop=(jc == KC - 1)),
                         reads=[bC[c], b_const], writes=[bpm], inc=(jc == KC - 1))
                for jc in range(KC):
                    k.op(pe, lambda e, jc=jc: e.matmul(pq[:, :], invd[:], ysq[u][:, jc, :], start=(jc == 0), stop=(jc == KC - 1)),
                         reads=[bys[u], b_const], writes=[bpq], inc=(jc == KC - 1))
                k.op(act, lambda e: e.activation(out=MR[u], in_=pm[:, :], func=AF.Square), reads=[bpm], writes=[bR[u]])
                k.op(dve, lambda e: e.tensor_tensor(out=Rr[u], in0=pq[:, :], in1=MR[u], op=ALU.subtract), reads=[bpq], writes=[bR[u]])
                k.op(act, lambda e: e.activation(out=Rr[u], in_=Rr[u], func=AF.Sqrt, bias=epsc[:, 0:1], scale=1.0), reads=[b_const], writes=[bR[u]])
                k.op(dve, lambda e: e.reciprocal(out=Rr[u], in_=Rr[u]), writes=[bR[u]])
                k.op(dve, lambda e: e.tensor_tensor(out=MR[u], in0=pm[:, :], in1=Rr[u], op=ALU.mult), reads=[bpm], writes=[bR[u]])
                for jc in range(KC):
                    v = nz % 2
                    nz += 1
                    k.op(dve, lambda e, jc=jc: e.tensor_tensor(out=zt[v], in0=R1[:, jc, c * 512:(c + 1) * 512], in1=Rr[u], op=ALU.mult),
                         reads=[bC[c], bR[u]], writes=[bzt[v]])
                    k.op(pool, lambda e: e.tensor_tensor(out=zt[v], in0=zt[v], in1=MR[u], op=ALU.subtract), reads=[bR[u]], writes=[bzt[v]])
                    k.op(act, lambda e, jc=jc: e.activation(out=zT[u][:, jc, :], in_=zt[v], func=AF.Silu, bias=lbc[:, jc:jc + 1], scale=lgc[:, jc:jc + 1]),
                         reads=[bzt[v], b_p], writes=[bz[u]])
                for tq in range(4):
                    t = 4 * c + tq
                    for half in range(2):
                        ps, bps = PS[nres % 4], bPS[nres % 4]
                        nres += 1
                        for kc in range(KC):
                            k.op(pe, lambda e, kc=kc, tq=tq, half=half: e.matmul(ps[:, :], zT[u][:, kc, tq * 128:(tq + 1) * 128], w2[:, kc, half * 512:(half + 1) * 512],
                                                                               start=(kc == 0), stop=False),
                                 reads=[bw2, bz[u]], writes=[bps], inc=False)
                        k.op(pe, lambda e, half=half: e.matmul(ps[:, :], onesf[0:1, :], b2row[0:1, half * 512:(half + 1) * 512], start=False, stop=True),
                             reads=[b_p, b_const], writes=[bps])
                        res.add(ps, bps, t, half)

        consts()
        load_x()
        for i in layers:
            kind = i % 3
            j = i // 3
            k.barrier()
            mods(i)
            k.barrier()
            bH = [Buf() for _ in range(NCH)]
            norm_to_hT(0, bH)
            k.barrier()
            if "mix" not in skip:
                if kind == 0:
                    fox(j, bH)
                elif kind == 1:
                    gmlp(bH)
                else:
                    conv(bH)
            k.barrier()
            bH = [Buf() for _ in range(NCH)]
            norm_to_hT(1, bH)
            k.barrier()
            if "mlp" not in skip:
                mlp(i, bH)
        k.barrier()
        ov = out_d.rearrange("(t p) d -> p t d", p=128)
        for t0 in range(0, NT, 4):
            k.dma(sp, d_st, ov[:, t0:t0 + 4, :], X[:, t0:t0 + 4, :], reads=bX[t0:t0 + 4])
        for E in k.engs:
            E.wait((d_st, d_st.cnt))
    return nc


def make_in_maps(inputs, S):
    T = S // 2
    f = lambda a: np.ascontiguousarray(np.asarray(a, dtype=np.float32))
    x = f(inputs["x"])
    c = f(inputs["c"])
    B = x.shape[0]
    shared = {
        "norm_mix": f(inputs["norm_mix"]), "norm_mlp": f(inputs["norm_mlp"]),
        "w_ada": f(inputs["w_ada"]), "b_ada": f(inputs["b_ada"]),
        "w_mlp_in": f(inputs["w_mlp_in"]), "w_mlp_out": f(inputs["w_mlp_out"]),
        "fox_w_in": f(inputs["fox_w_in"]),
        "fox_bf_bc": f(np.broadcast_to(np.asarray(inputs["fox_b_f"])[:, None, :], (2, 128, NH))),
        "fox_qg_bc": f(np.broadcast_to(np.asarray(inputs["fox_q_norm"])[:, None, :], (2, 128, HD))),
        "fox_kg_bc": f(np.broadcast_to(np.asarray(inputs["fox_k_norm"])[:, None, :], (2, 128, HD))),
        "fox_w_out": f(inputs["fox_w_out"]),
        "sg_w_in": f(inputs["sg_w_in"])[0],
        "sg_lng_bc": f(np.broadcast_to(np.asarray(inputs["sg_ln_g"])[0][None, :], (128, D))),
        "sg_lnb_bc": f(np.broadcast_to(np.asarray(inputs["sg_ln_b"])[0][None, :], (128, D))),
        "sg_w_s": f(inputs["sg_w_s"])[0],
        "sg_bs_c": f(np.asarray(inputs["sg_b_s"])[0].T),
        "sg_w_out": f(inputs["sg_w_out"])[0],
        "cv_w_pw1": f(inputs["cv_w_pw1"])[0],
        "cv_bpw1_c": f(np.asarray(inputs["cv_b_pw1"])[0].reshape(16, 128).T),
        "cv_wdw_c": f(np.asarray(inputs["cv_w_dw"])[0].reshape(CW, KC, 128).transpose(2, 1, 0).reshape(128, KC * CW)),
        "cv_bdw_c": f(np.asarray(inputs["cv_b_dw"])[0].reshape(KC, 128).T),
        "cv_lng_c": f(np.asarray(inputs["cv_ln_g"])[0].reshape(KC, 128).T),
        "cv_lnb_c": f(np.asarray(inputs["cv_ln_b"])[0].reshape(KC, 128).T),
        "cv_w_pw2": f(inputs["cv_w_pw2"])[0],
        "cv_bpw2": f(np.asarray(inputs["cv_b_pw2"])[0][None, :]),
    }
    maps = []
    for core in range(2 * B):
        b, r = core // 2, core % 2
        m = dict(shared)
        m["x"] = np.ascontiguousarray(x[b, r * T:(r + 1) * T, :])
        m["cT"] = np.ascontiguousarray(c[b].reshape(KC, 128).T)
        m["flag"] = np.full((128, 1), float(r), np.float32)
        maps.append(m)
    return maps


_NC_CACHE = {}


def run(inputs, S, layers=(0, 1, 2, 3), skip=(), trace=False):
    NT = S // 256
    key = (NT, tuple(layers), tuple(skip))
    if key not in _NC_CACHE:
        _NC_CACHE[key] = build(NT, layers, skip)
    nc = _NC_CACHE[key]
    maps = make_in_maps(inputs, S)
    res = run_bass_kernel_spmd(nc, maps, core_ids=list(range(8)), **({"trace": True} if trace else {}))
    B = 4
    T = S // 2
    out = np.empty((B, S, D), np.float32)
    for core in range(8):
        b, r = core // 2, core % 2
        out[b, r * T:(r + 1) * T, :] = res.results[core]["out"]
    return out, res


def kernel(**inputs):
    out, _ = run(inputs, 4096)
    return out
```

```python
import os
import numpy as np
from contextlib import ExitStack
import concourse.bass as bass
import concourse.mybir as mybir
from concourse.bass_utils import run_bass_kernel_spmd

F32 = mybir.dt.float32
BF16 = mybir.dt.bfloat16
AF = mybir.ActivationFunctionType
ALU = mybir.AluOpType
AX = mybir.AxisListType

D = 1024
DFF = 4096
NH = 16
HD = 64
KC = 8
DEPTH = 4
EPS = 1e-6
HA = 67
CW = 31
NEG = -30000.0
SLOT = 8192
NSLOT = 6


class Buf:
    __slots__ = ("name", "w", "r", "ds")

    def __init__(self, name=""):
        self.name = name
        self.w = None
        self.r = {}
        self.ds = None


class SemC:
    def __init__(self, k, name):
        self.name = name
        self.sem = k.es.enter_context(k.nc.semaphore(name))
        self.cnt = 0


class Eng:
    def __init__(self, k, e, name, in_order_self=False):
        self.e = e
        self.name = name
        self.sc = SemC(k, "s_" + name)
        self.seen = {}
        self.in_order_self = in_order_self

    def wait(self, tk):
        if tk is None:
            return
        sc, val = tk
        if val <= 0:
            return
        if sc is self.sc and self.in_order_self:
            return
        if self.seen.get(sc, 0) >= val:
            return
        self.e.wait_ge(sc.sem, val)
        self.seen[sc] = val


class K:
    def __init__(self, nc, es):
        self.nc = nc
        self.es = es
        self.pe = Eng(self, nc.tensor, "pe", in_order_self=True)
        self.act = Eng(self, nc.scalar, "act")
        self.dve = Eng(self, nc.vector, "dve")
        self.pool = Eng(self, nc.gpsimd, "pool")
        self.sp = Eng(self, nc.sync, "sp")
        self.engs = [self.pe, self.act, self.dve, self.pool, self.sp]
        self.bar_sems = []
        self.dpool = []
        self.dnext = 0
        self.NDPOOL = 56

    def _buf_sem(self, writes):
        b = writes[0]
        if b.ds is None:
            if len(self.dpool) < self.NDPOOL:
                self.dpool.append(self.dsem("dp%d" % len(self.dpool)))
            b.ds = self.dpool[self.dnext % self.NDPOOL]
            self.dnext += 1
        return b.ds

    def sb(self, name, shape, dt):
        return self.es.enter_context(self.nc.sbuf_tensor(name, list(shape), dt))

    def ps(self, name, shape, dt):
        return self.es.enter_context(self.nc.psum_tensor(name, list(shape), dt))

    def dsem(self, name, barrier=True):
        s = SemC(self, name)
        if barrier:
            self.bar_sems.append(s)
        return s

    def _deps(self, E, reads, writes):
        for b in reads:
            E.wait(b.w)
        for b in writes:
            E.wait(b.w)
            for sc, v in b.r.items():
                E.wait((sc, v))

    def _record(self, tk, reads, writes):
        sc, v = tk
        for b in reads:
            if b.r.get(sc, 0) < v:
                b.r[sc] = v
        for b in writes:
            b.w = tk
            b.r = {}

    def op(self, E, emit, reads=(), writes=(), inc=True):
        self._deps(E, reads, writes)
        ins = emit(E.e)
        if inc:
            E.sc.cnt += 1
            ins.then_inc(E.sc.sem, 1)
            tk = (E.sc, E.sc.cnt)
        else:
            tk = (E.sc, E.sc.cnt + 1)
        self._record(tk, reads, writes)
        return tk

    def dma(self, Q, dsem, out, in_, reads=(), writes=()):
        if writes:
            dsem = self._buf_sem(writes)
        self._deps(Q, reads, writes)
        ins = Q.e.dma_start(out=out, in_=in_)
        dsem.cnt += 16
        ins.then_inc(dsem.sem, 16)
        tk = (dsem, dsem.cnt)
        self._record(tk, reads, writes)
        return tk

    def coll(self, csem, kind, groups, in_ap, out_ap, reads=(), writes=()):
        Q = self.pool
        b = writes[0]
        if b.ds is None:
            b.ds = self.dsem("cs_" + b.name)
        csem = b.ds
        self._deps(Q, reads, writes)
        ins = Q.e.collective_compute(kind, ALU.bypass, replica_groups=groups, ins=[in_ap], outs=[out_ap])
        csem.cnt += 1
        ins.then_inc(csem.sem)
        tk = (csem, csem.cnt)
        self._record(tk, reads, writes)
        return tk

    def barrier(self):
        for E in self.engs:
            for E2 in self.engs:
                if E2 is not E:
                    E.wait((E2.sc, E2.sc.cnt))
            for s in self.bar_sems:
                E.wait((s, s.cnt))


def build(NT, layers=(0, 1, 2, 3), skip=()):
    T = NT * 128
    NCH = NT // 4
    assert NT % 4 == 0
    nc = bass.Bass("TRN2", target_bir_lowering=False)
    es = ExitStack()
    with es:
        k = K(nc, es)
        pe, act, dve, pool, sp = k.pe, k.act, k.dve, k.pool, k.sp

        def din(name, shape, dt=F32):
            return nc.dram_tensor(name, list(shape), dt, kind="ExternalInput").ap()

        x_d = din("x", [T, D])
        cT_d = din("cT", [128, KC])
        flag_d = din("flag", [128, 1])
        nmix_d = din("norm_mix", [DEPTH, D])
        nmlp_d = din("norm_mlp", [DEPTH, D])
        wada_d = din("w_ada", [DEPTH, D, 6 * D])
        bada_d = din("b_ada", [DEPTH, 6 * D])
        wmi_d = din("w_mlp_in", [DEPTH, D, DFF])
        wmo_d = din("w_mlp_out", [DEPTH, DFF, D])
        fwin_d = din("fox_w_in", [2, D, 3 * D + NH])
        fbf_d = din("fox_bf_bc", [2, 128, NH])
        fqg_d = din("fox_qg_bc", [2, 128, HD])
        fkg_d = din("fox_kg_bc", [2, 128, HD])
        fwo_d = din("fox_w_out", [2, D, D])
        swin_d = din("sg_w_in", [D, 2 * D])
        slng_d = din("sg_lng_bc", [128, D])
        slnb_d = din("sg_lnb_bc", [128, D])
        sws_d = din("sg_w_s", [8, 128, 128])
        sbs_d = din("sg_bs_c", [128, 8])
        swo_d = din("sg_w_out", [D, D])
        cw1_d = din("cv_w_pw1", [D, 2 * D])
        cb1_d = din("cv_bpw1_c", [128, 16])
        cwd_d = din("cv_wdw_c", [128, KC * CW])
        cbd_d = din("cv_bdw_c", [128, KC])
        clg_d = din("cv_lng_c", [128, KC])
        clb_d = din("cv_lnb_c", [128, KC])
        cw2_d = din("cv_w_pw2", [D, D])
        cb2_d = din("cv_bpw2", [1, D])
        out_d = nc.dram_tensor("out", [T, D], F32, kind="ExternalOutput").ap()

        qt_t = nc.dram_tensor("qt_d", [NH * HA, T], BF16)
        kto_t = nc.dram_tensor("kt_own_d", [NCH * NH * HA, 512], BF16)
        kta_t = nc.dram_tensor("kt_all_d", [NCH * 2 * NH * HA, 512], BF16)
        vo_t = nc.dram_tensor("v_own_d", [T, D], BF16)
        va_t = nc.dram_tensor("v_all_d", [NCH * 2 * 512, D], BF16)
        kbo_t = nc.dram_tensor("kb_own_d", [128, NT * NH], F32)
        kba_t = nc.dram_tensor("kb_all_d", [256, NT * NH], F32)
        hlo_t = nc.dram_tensor("hl_own_d", [128, KC * 32], BF16)
        hla_t = nc.dram_tensor("hl_all_d", [256, KC * 32], BF16)
        b_qt, b_kto, b_vo = Buf("qt"), Buf("kto"), Buf("vo")
        b_kta = [Buf("kta%d" % c) for c in range(NCH)]
        b_va = [Buf("va%d" % c) for c in range(NCH)]
        b_kbo, b_kba, b_hlo, b_hla = Buf("kbo"), Buf("kba"), Buf("hlo"), Buf("hla")
        GROUPS = [[0, 1], [2, 3], [4, 5], [6, 7]]

        X = k.sb("X", [128, NT, D], F32)
        bX = [Buf("X%d" % t) for t in range(NT)]
        R1 = k.sb("R1", [128, KC, T], BF16)
        SL = k.sb("SL", [128, NSLOT, SLOT], BF16)
        ident = k.sb("ident", [128, 128], BF16)
        identf = k.sb("identf", [128, 128], F32)
        triu = k.sb("triu", [128, 128], F32)
        onesf = k.sb("onesf", [128, 128], F32)
        onesb = k.sb("onesb", [128, 128], BF16)
        invd = k.sb("invd", [128, 128], BF16)
        maskT = k.sb("maskT", [128, 128], BF16)
        gbc = k.sb("gbc", [128, 2, D], F32)
        modc = k.sb("modc", [128, 4, KC], F32)
        cact = k.sb("cact", [128, KC], BF16)
        cin = k.sb("cin", [128, KC], F32)
        flagc = k.sb("flagc", [128, 1], F32)
        mbias = k.sb("mbias", [128, 1], F32)
        epsc = k.sb("epsc", [128, 1], F32)
        rtmp = k.sb("rtmp", [128, 2, 512], F32)
        ssq = k.sb("ssq", [128, NT], F32)
        rstd = k.sb("rstd", [128, NT], F32)
        b_const, b_gbc, b_modc, b_cact, b_rowb, b_nrow, b_ssq = Buf(), Buf(), Buf(), Buf(), [Buf(), Buf(), Buf()], Buf(), Buf()

        PS = [k.ps("ps%d" % i, [128, 512], F32) for i in range(6)]
        bPS = [Buf("ps%d" % i) for i in range(6)]
        PT = [k.ps("pt%d" % i, [128, 1024], BF16) for i in range(2)]
        bPT = [Buf("pt%d" % i) for i in range(2)]

        for b_ in (b_qt, b_kto, b_vo, b_kbo, b_hlo):
            b_.ds = k.dsem("dd_" + b_.name)
        d_x = k.dsem("d_x")
        for b_ in bX:
            b_.ds = d_x
        d_in = k.dsem("d_in")
        d_w = k.dsem("d_w")
        d_st = k.dsem("d_st")
        c_sem = k.dsem("c_sem")

        def slot_bf(i, off, n):
            return SL[:, i, off:off + n]

        def slot_f32(i, off, n):
            return SL[:, i, 2 * off:2 * off + 2 * n].bitcast(F32)

        rowb = slot_f32(2, 0, 1536).rearrange("p (r n) -> p r n", r=3)
        nrow = slot_f32(2, 1536, 2 * D).rearrange("p (r n) -> p r n", r=2)

        def consts():
            w = [b_const]
            k.op(dve, lambda e: e.memset(identf[:], 0.0), writes=w)
            k.op(pool, lambda e: e.affine_select(out=identf[:], in_=identf[:], compare_op=ALU.not_equal, fill=1.0,
                                                 base=0, pattern=[[-1, 128]], channel_multiplier=1), writes=w)
            k.op(dve, lambda e: e.tensor_copy(out=ident[:], in_=identf[:]), writes=w)
            k.op(dve, lambda e: e.memset(onesf[:], 1.0), writes=w)
            k.op(dve, lambda e: e.memset(onesb[:], 1.0), writes=w)
            k.op(dve, lambda e: e.memset(invd[:], 1.0 / D), writes=w)
            k.op(dve, lambda e: e.memset(epsc[:], EPS), writes=w)
            k.op(pool, lambda e: e.affine_select(out=triu[:], in_=onesf[:], compare_op=ALU.is_ge, fill=0.0,
                                                 base=0, pattern=[[1, 128]], channel_multiplier=-1), writes=w)
            k.op(dve, lambda e: e.tensor_scalar(out=identf[:], in0=triu[:], scalar1=-1.0, scalar2=-NEG,
                                                op0=ALU.add, op1=ALU.mult), writes=w)
            k.op(dve, lambda e: e.tensor_copy(out=maskT[:], in_=identf[:]), writes=w)
            k.dma(sp, d_in, flagc[:], flag_d[:, :], writes=w)
            k.dma(sp, d_in, cin[:], cT_d[:, :], writes=w)
            k.op(dve, lambda e: e.tensor_scalar(out=mbias[:], in0=flagc[:], scalar1=-1.0, scalar2=-NEG,
                                                op0=ALU.add, op1=ALU.mult), writes=w)
            k.op(act, lambda e: e.activation(out=cact[:], in_=cin[:], func=AF.Silu), writes=w)

        def load_x():
            xv = x_d.rearrange("(t p) d -> p t d", p=128)
            for t0 in range(0, NT, 4):
                k.dma(sp, d_in, X[:, t0:t0 + 4, :], xv[:, t0:t0 + 4, :], writes=bX[t0:t0 + 4])

        def mods(i):
            wring = [slot_bf(0, 0, KC * 512).rearrange("p (k n) -> p k n", k=KC),
                     slot_bf(1, 0, KC * 512).rearrange("p (k n) -> p k n", k=KC)]
            bw = [Buf(), Buf()]
            wv = wada_d[i].rearrange("(k p) n -> p k n", p=128)
            k.dma(sp, d_in, nrow[0:1, 0, :], nmix_d[i:i + 1, :], writes=[b_nrow])
            k.dma(sp, d_in, nrow[0:1, 1, :], nmlp_d[i:i + 1, :], writes=[b_nrow])
            for n in range(12):
                kind = n // 2
                half = n % 2
                wb, bwb = wring[n % 2], bw[n % 2]
                k.dma(pool, d_w, wb, wv[:, :, n * 512:(n + 1) * 512], writes=[bwb])
                rb = n % 3
                k.dma(sp, d_in, rowb[0:1, rb, :], bada_d[i:i + 1, n * 512:(n + 1) * 512], writes=[b_rowb[rb]])
                ps = PS[n % 2]
                bps = bPS[n % 2]
                for kc in range(KC):
                    k.op(pe, lambda e, kc=kc: e.matmul(ps[0:1, :], cact[:, kc:kc + 1], wb[:, kc, :],
                                                       start=(kc == 0), stop=(kc == KC - 1)),
                         reads=[bwb, b_const], writes=[bps], inc=(kc == KC - 1))
                k.op(dve, lambda e: e.tensor_tensor(out=rowb[0:1, rb, :], in0=ps[0:1, :], in1=rowb[0:1, rb, :], op=ALU.add),
                     reads=[bps], writes=[b_rowb[rb]])
                if kind in (1, 4):
                    g = 0 if kind == 1 else 1
                    k.op(dve, lambda e: e.scalar_tensor_tensor(out=rowb[0:1, rb, :], in0=rowb[0:1, rb, :], scalar=1.0,
                                                               in1=nrow[0:1, g, half * 512:(half + 1) * 512],
                                                               op0=ALU.add, op1=ALU.mult),
                         reads=[b_nrow], writes=[b_rowb[rb]])
                if kind in (2, 5):
                    g = 0 if kind == 2 else 1
                    pb = PS[2 + n % 2]
                    bpb = bPS[2 + n % 2]
                    k.op(pe, lambda e: e.matmul(pb[:, :], onesf[0:1, :], rowb[0:1, rb, :], start=True, stop=True),
                         reads=[b_rowb[rb], b_const], writes=[bpb])
                    k.op(act, lambda e: e.copy(out=gbc[:, g, half * 512:(half + 1) * 512], in_=pb[:, :]),
                         reads=[bpb], writes=[b_gbc])
                else:
                    col = {0: 0, 1: 1, 3: 2, 4: 3}[kind]
                    pb = PS[2 + n % 2]
                    bpb = bPS[2 + n % 2]
                    for q in range(4):
                        k.op(pe, lambda e, q=q: e.matmul(pb[:, q:q + 1], rowb[0:1, rb, q * 128:(q + 1) * 128], onesf[0:1, 0:1],
                                                         start=True, stop=True),
                             reads=[b_rowb[rb], b_const], writes=[bpb], inc=(q == 3))
                    k.op(dve, lambda e: e.tensor_copy(out=modc[:, col, half * 4:(half + 1) * 4], in_=pb[:, 0:4]),
                         reads=[bpb], writes=[b_modc])

        def norm_to_hT(which, bH):
            shc, gc = (0, 1) if which == 0 else (2, 3)
            junk = slot_bf(5, 0, D)
            bj = Buf()
            xs = [slot_bf(5, D * (1 + q), D) for q in range(4)]
            bxs = [Buf() for _ in range(4)]
            for t in range(NT):
                k.op(act, lambda e, t=t: e.activation(out=junk, in_=X[:, t, :], func=AF.Square, accum_out=ssq[:, t:t + 1]),
                     reads=[bX[t]], writes=[bj, b_ssq])
            k.op(act, lambda e: e.activation(out=rstd[:], in_=ssq[:], func=AF.Sqrt, bias=epsc[:, 0:1], scale=1.0 / D),
                 reads=[b_ssq, b_const], writes=[b_ssq])
            k.op(dve, lambda e: e.reciprocal(out=rstd[:], in_=rstd[:]), reads=[b_ssq], writes=[b_ssq])
            for c in range(NCH):
                for q in range(4):
                    t = 4 * c + q
                    k.op(dve, lambda e, t=t, q=q: e.tensor_scalar(out=xs[q], in0=X[:, t, :], scalar1=rstd[:, t:t + 1], scalar2=None,
                                                                  op0=ALU.mult),
                         reads=[bX[t], b_ssq], writes=[bxs[q]])
                for kp in range(4):
                    pt, bpt = PT[kp % 2], bPT[kp % 2]
                    for kk in range(2):
                        kc = 2 * kp + kk
                        for q in range(4):
                            k.op(pe, lambda e, kc=kc, kk=kk, q=q: e.transpose(pt[:, kk * 512 + q * 128: kk * 512 + (q + 1) * 128],
                                                                              xs[q][:, kc * 128:(kc + 1) * 128], ident[:]),
                                 reads=[bxs[q], b_const], writes=[bpt], inc=(kk == 1 and q == 3))
                    for kk in range(2):
                        kc = 2 * kp + kk
                        E = act if kk == 0 else dve
                        if E is act:
                            k.op(act, lambda e, kc=kc, kk=kk: e.activation(out=R1[:, kc, c * 512:(c + 1) * 512], in_=pt[:, kk * 512:(kk + 1) * 512],
                                                                           func=AF.Identity, bias=modc[:, shc, kc:kc + 1], scale=modc[:, gc, kc:kc + 1]),
                                 reads=[bpt, b_modc], writes=[bH[c]])
                        else:
                            k.op(dve, lambda e, kc=kc, kk=kk: e.tensor_scalar(out=R1[:, kc, c * 512:(c + 1) * 512], in0=pt[:, kk * 512:(kk + 1) * 512],
                                                                              scalar1=modc[:, gc, kc:kc + 1], scalar2=modc[:, shc, kc:kc + 1],
                                                                              op0=ALU.mult, op1=ALU.add),
                                 reads=[bpt, b_modc], writes=[bH[c]])

        class Resid:
            def __init__(self, g):
                self.g = g
                self.tmp = [rtmp[:, q, :] for q in range(2)]
                self.bt = [Buf(), Buf()]
                self.n = 0

            def add(self, ps, bps, t, half):
                q = self.n % 2
                self.n += 1
                tmp, bt = self.tmp[q], self.bt[q]
                k.op(dve, lambda e: e.tensor_tensor(out=tmp, in0=ps[:, :], in1=gbc[:, self.g, half * 512:(half + 1) * 512], op=ALU.mult),
                     reads=[bps, b_gbc], writes=[bt])
                k.op(pool, lambda e: e.tensor_tensor(out=X[:, t, half * 512:(half + 1) * 512], in0=X[:, t, half * 512:(half + 1) * 512],
                                                     in1=tmp, op=ALU.add),
                     reads=[bt], writes=[bX[t]])

        def mlp(i, bH):
            hid = [SL[:, q, 0:4 * T].rearrange("p (f t) -> p f t", f=4) for q in range(2)]
            bhid = [[Buf() for _ in range(NCH)] for _ in range(2)]
            wi = [slot_bf(2 + q, 0, KC * 512).rearrange("p (k n) -> p k n", k=KC) for q in range(3)]
            wo = [slot_bf(2 + q, KC * 512, 4 * D).rearrange("p (f n) -> p f n", f=4) for q in range(3)]
            bwi = [Buf() for _ in range(3)]
            bwo = [Buf() for _ in range(3)]
            rr = [slot_f32(5, 512 * q, 512) for q in range(2)]
            brr = [Buf(), Buf()]
            wiv = wmi_d[i].rearrange("(k p) f -> p k f", p=128)
            wov = wmo_d[i].rearrange("(f p) d -> p f d", p=128)
            res = Resid(1)
            NG = DFF // 512

            def load(g):
                s = g % 3
                k.dma(pool, d_w, wi[s], wiv[:, :, g * 512:(g + 1) * 512], writes=[bwi[s]])
                k.dma(pool, d_w, wo[s], wov[:, g * 4:(g + 1) * 4, :], writes=[bwo[s]])

            load(0)
            load(1)
            nps = 0
            nr = 0
            for g in range(NG):
                s = g % 3
                hb = g % 2
                if g + 2 < NG:
                    load(g + 2)
                for c in range(NCH):
                    for fc in range(4):
                        ps, bps = PS[nps % 3], bPS[nps % 3]
                        nps += 1
                        for kc in range(KC):
                            k.op(pe, lambda e, kc=kc, fc=fc: e.matmul(ps[:, :], wi[s][:, kc, fc * 128:(fc + 1) * 128], R1[:, kc, c * 512:(c + 1) * 512],
                                                                      start=(kc == 0), stop=(kc == KC - 1)),
                                 reads=[bwi[s], bH[c]], writes=[bps], inc=(kc == KC - 1))
                        r, br = rr[nr % 2], brr[nr % 2]
                        nr += 1
                        k.op(act, lambda e: e.activation(out=r, in_=ps[:, :], func=AF.Relu), reads=[bps], writes=[br])
                        k.op(dve, lambda e, fc=fc: e.tensor_tensor(out=hid[hb][:, fc, c * 512:(c + 1) * 512], in0=r, in1=r, op=ALU.mult),
                             reads=[br], writes=[bhid[hb][c]])
                for t in range(NT):
                    for half in range(2):
                        ps, bps = PS[3 + nps % 3], bPS[3 + nps % 3]
                        nps += 1
                        for fc in range(4):
                            k.op(pe, lambda e, fc=fc: e.matmul(ps[:, :], hid[hb][:, fc, t * 128:(t + 1) * 128], wo[s][:, fc, half * 512:(half + 1) * 512],
                                                               start=(fc == 0), stop=(fc == 3)),
                                 reads=[bwo[s], bhid[hb][t // 4]], writes=[bps], inc=(fc == 3))
                        res.add(ps, bps, t, half)

        def outproj(w_dram, bH, gate, wslot, bias_row=None):
            w = slot_bf(wslot, 0, KC * D).rearrange("p (k n) -> p k n", k=KC)
            bw = Buf()
            k.dma(pool, d_w, w, w_dram.rearrange("(k p) n -> p k n", p=128), writes=[bw])
            res = Resid(gate)
            n = 0
            for t in range(NT):
                for half in range(2):
                    ps, bps = PS[n % 3], bPS[n % 3]
                    n += 1
                    for kc in range(KC):
                        last = (kc == KC - 1) and bias_row is None
                        k.op(pe, lambda e, kc=kc: e.matmul(ps[:, :], R1[:, kc, t * 128:(t + 1) * 128], w[:, kc, half * 512:(half + 1) * 512],
                                                           start=(kc == 0), stop=last),
                             reads=[bw, bH[t // 4]], writes=[bps], inc=last)
                    if bias_row is not None:
                        brow, bbrow = bias_row
                        k.op(pe, lambda e: e.matmul(ps[:, :], onesf[0:1, :], brow[0:1, half * 512:(half + 1) * 512], start=False, stop=True),
                             reads=[bbrow, b_const], writes=[bps])
                    res.add(ps, bps, t, half)

        def fox(j, bH):
            NFC = NT * NH
            lgf = slot_f32(4, 0, NFC).rearrange("p (t h) -> p t h", h=NH)
            nf = slot_f32(4, NFC, NFC).rearrange("p (t h) -> p t h", h=NH)
            kbp = slot_f32(4, 2 * NFC, NFC).rearrange("p (t h) -> p t h", h=NH)
            ftmp = slot_f32(4, 4 * NFC, NFC).rearrange("p (t h) -> p t h", h=NH)
            fs3 = slot_bf(4, 10 * NFC, 3 * NFC).rearrange("p (t h r) -> p t h r", h=NH, r=3)
            bfb = slot_f32(4, 7 * NFC, NH)
            qgb = slot_f32(4, 7 * NFC + 16, HD)
            kgb = slot_f32(4, 7 * NFC + 80, HD)
            b_f = Buf()
            b_fs3 = Buf()
            b_g = Buf()
            k.dma(sp, d_in, bfb, fbf_d[j], writes=[b_g])
            k.dma(sp, d_in, qgb, fqg_d[j], writes=[b_g])
            k.dma(sp, d_in, kgb, fkg_d[j], writes=[b_g])
            k.op(dve, lambda e: e.scalar_tensor_tensor(out=qgb, in0=qgb, scalar=HD ** -0.5, in1=kgb, op0=ALU.mult, op1=ALU.mult),
                 reads=[], writes=[b_g])
            wv = fwin_d[j].rearrange("(k p) n -> p k n", p=128)
            wring = [slot_bf(q, 0, KC * 512).rearrange("p (k n) -> p k n", k=KC) for q in range(3)]
            bwr = [Buf() for _ in range(3)]
            chunks = [("f", 3 * D, NH)] + [("q", qc * 512, 512) for qc in range(2)] + [("k", D + qc * 512, 512) for qc in range(2)] + \
                     [("v", 2 * D + qc * 512, 512) for qc in range(2)]

            def loadw(ci):
                kind, off, n = chunks[ci]
                s = ci % 3
                k.dma(pool, d_w, wring[s][:, :, 0:n], wv[:, :, off:off + n], writes=[bwr[s]])

            loadw(0)
            loadw(1)
            sq = [slot_f32(3, 512 * q, 512) for q in range(2)]
            bsq = [Buf(), Buf()]
            kf = [slot_f32(3, 1024 + 512 * q, 512) for q in range(2)]
            bkf = [Buf(), Buf()]
            ssh = [slot_f32(3, 2048 + 16 * q, 8) for q in range(2)]
            bssh = [Buf(), Buf()]
            qa = [slot_bf(3, 4224 + 8 * HA * q, 8 * HA).rearrange("p (h r) -> p h r", r=HA) for q in range(4)]
            bqa = [Buf() for _ in range(4)]
            vst = [slot_bf(3, 4224 + 32 * HA + 512 * q, 512) for q in range(2)]
            bvst = [Buf(), Buf()]
            stg = [slot_bf(5, 4096 * q, 4096).rearrange("p (h t) -> p h t", h=8) for q in range(2)]
            bstg = [Buf(), Buf()]
            qtv = qt_t.ap().rearrange("(h r) t -> r h t", r=HA)
            ktv = kto_t.ap().rearrange("(c h r) t -> c r h t", c=NCH, r=HA)
            nps = 0
            nu = 0
            nqa = 0
            st = {"nstg": 0, "npt": 0}
            pending = []

            def stage_b(kind, qc, t, qi):
                pt, bpt = PT[st["npt"] % 2], bPT[st["npt"] % 2]
                st["npt"] += 1
                for h in range(8):
                    k.op(pe, lambda e, h=h: e.transpose(pt[0:HA, h * 128:(h + 1) * 128], qa[qi][:, h, :], ident[:]),
                         reads=[bqa[qi], b_const], writes=[bpt], inc=(h == 7))
                sg_ = st["nstg"] % 2
                tq = t % 4
                k.op(act, lambda e: e.copy(out=stg[sg_][0:HA, :, tq * 128:(tq + 1) * 128],
                                           in_=pt[0:HA, :].rearrange("p (h t) -> p h t", h=8)),
                     reads=[bpt], writes=[bstg[sg_]])
                if tq == 3:
                    c = t // 4
                    if kind == "q":
                        dst = qtv[:, qc * 8:(qc + 1) * 8, c * 512:(c + 1) * 512]
                    else:
                        dst = ktv[c][:, qc * 8:(qc + 1) * 8, :]
                    k.dma(sp, d_st, dst, stg[sg_][0:HA, :, :], reads=[bstg[sg_]], writes=[b_qt if kind == "q" else b_kto])
                    st["nstg"] += 1

            for ci, (kind, off, n) in enumerate(chunks):
                s = ci % 3
                if ci + 2 < len(chunks):
                    loadw(ci + 2)
                w = wring[s]
                for t in range(NT):
                    ps, bps = PS[nps % 4], bPS[nps % 4]
                    nps += 1
                    for kc in range(KC):
                        k.op(pe, lambda e, kc=kc: e.matmul(ps[:, 0:n], R1[:, kc, t * 128:(t + 1) * 128], w[:, kc, 0:n],
                                                           start=(kc == 0), stop=(kc == KC - 1)),
                             reads=[bwr[s], bH[t // 4]], writes=[bps], inc=(kc == KC - 1))
                    u = nu % 2
                    nu += 1
                    if kind == "f":
                        k.op(dve, lambda e, t=t: e.tensor_tensor(out=lgf[:, t, :], in0=ps[:, 0:NH], in1=bfb, op=ALU.add),
                             reads=[bps, b_g], writes=[b_f])
                    elif kind == "v":
                        qc = (off - 2 * D) // 512
                        k.op(act, lambda e: e.copy(out=vst[u], in_=ps[:, :]), reads=[bps], writes=[bvst[u]])
                        k.dma(sp, d_st, vo_t.ap()[t * 128:(t + 1) * 128, qc * 512:(qc + 1) * 512], vst[u], reads=[bvst[u]], writes=[b_vo])
                        if pending:
                            stage_b(*pending.pop(0))
                    else:
                        qc = (off % D) // 512
                        qi = nqa % 4
                        nqa += 1
                        k.op(act, lambda e: e.activation(out=sq[u], in_=ps[:, :], func=AF.Square), reads=[bps], writes=[bsq[u]])
                        k.op(dve, lambda e: e.tensor_reduce(out=ssh[u], in_=sq[u].rearrange("p (h d) -> p h d", d=HD), axis=AX.X, op=ALU.add),
                             reads=[bsq[u]], writes=[bssh[u]])
                        k.op(act, lambda e: e.activation(out=ssh[u], in_=ssh[u], func=AF.Sqrt, bias=epsc[:, 0:1], scale=1.0 / HD),
                             reads=[b_const], writes=[bssh[u]])
                        k.op(dve, lambda e: e.reciprocal(out=ssh[u], in_=ssh[u]), writes=[bssh[u]])
                        rb = ssh[u].unsqueeze(2).to_broadcast([128, 8, HD])
                        psv = ps[:, :].rearrange("p (h d) -> p h d", d=HD)
                        if kind == "q":
                            k.op(dve, lambda e: e.tensor_tensor(out=qa[qi][:, :, 0:HD], in0=psv, in1=rb, op=ALU.mult),
                                 reads=[bps, bssh[u]], writes=[bqa[qi]])
                            k.op(pool, lambda e, t=t, qc=qc: e.tensor_copy(out=qa[qi][:, :, HD:HA], in_=fs3[:, t, qc * 8:(qc + 1) * 8, :]),
                                 reads=[b_fs3], writes=[bqa[qi]])
                        else:
                            kfv = kf[u].rearrange("p (h d) -> p h d", d=HD)
                            k.op(dve, lambda e: e.tensor_tensor(out=kfv, in0=psv, in1=rb, op=ALU.mult),
                                 reads=[bps, bssh[u]], writes=[bkf[u]])
                            k.op(pool, lambda e: e.tensor_tensor(out=qa[qi][:, :, 0:HD], in0=kfv, in1=qgb.unsqueeze(1).to_broadcast([128, 8, HD]), op=ALU.mult),
                                 reads=[bkf[u], b_g], writes=[bqa[qi]])
                            k.op(pool, lambda e: e.memset(qa[qi][:, :, HD:HA], 1.0), writes=[bqa[qi]])
                        pending.append((kind, qc, t, qi))
                        if len(pending) > 2:
                            stage_b(*pending.pop(0))
                if ci == len(chunks) - 1:
                    while pending:
                        stage_b(*pending.pop(0))
                if kind == "f":
                    k.op(act, lambda e: e.activation(out=lgf, in_=lgf, func=AF.Exp, scale=-1.0), writes=[b_f])
                    k.op(act, lambda e: e.activation(out=lgf, in_=lgf, func=AF.Ln, bias=1.0, scale=1.0), writes=[b_f])
                    for t in range(NT):
                        ps, bps = PS[4 + t % 2], bPS[4 + t % 2]
                        for t2 in range(t + 1):
                            k.op(pe, lambda e, t2=t2, t=t: e.matmul(ps[:, 0:NH], triu[:] if t2 == t else onesf[:], lgf[:, t2, :],
                                                                    start=(t2 == 0), stop=(t2 == t)),
                                 reads=[b_f, b_const], writes=[bps], inc=(t2 == t))
                        k.op(act, lambda e, t=t: e.copy(out=nf[:, t, :], in_=ps[:, 0:NH]), reads=[bps], writes=[b_f])
                    ps, bps = PS[4], bPS[4]
                    for t2 in range(NT):
                        k.op(pe, lambda e, t2=t2: e.matmul(ps[:, 0:NH], onesf[:], lgf[:, t2, :], start=(t2 == 0), stop=(t2 == NT - 1)),
                             reads=[b_f, b_const], writes=[bps], inc=(t2 == NT - 1))
                    k.op(dve, lambda e: e.tensor_tensor(out=kbp, in0=nf, in1=ps[:, 0:NH].unsqueeze(1).to_broadcast([128, NT, NH]), op=ALU.subtract),
                         reads=[bps], writes=[b_f])
                    k.dma(sp, d_st, kbo_t.ap(), kbp.rearrange("p t h -> p (t h)"), reads=[b_f], writes=[b_kbo])
                    k.op(dve, lambda e: e.tensor_scalar(out=ftmp, in0=nf, scalar1=-1.0, scalar2=None, op0=ALU.mult), writes=[b_f])
                    for r in range(3):
                        k.op(dve, lambda e, r=r: e.tensor_copy(out=fs3[:, :, :, r], in_=ftmp), writes=[b_f, b_fs3])
                        if r < 2:
                            k.op(dve, lambda e, r=r: e.tensor_tensor(out=ftmp, in0=ftmp, in1=fs3[:, :, :, r], op=ALU.subtract), writes=[b_f, b_fs3])
            KR = NH * HA
            for c in range(NCH):
                k.coll(c_sem, "AllGather", GROUPS, kto_t.ap()[c * KR:(c + 1) * KR, :].opt(), kta_t.ap()[2 * c * KR:2 * (c + 1) * KR, :].opt(),
                       reads=[b_kto], writes=[b_kta[c]])
                k.coll(c_sem, "AllGather", GROUPS, vo_t.ap()[c * 512:(c + 1) * 512, :].opt(), va_t.ap()[c * 1024:(c + 1) * 1024, :].opt(),
                       reads=[b_vo], writes=[b_va[c]])
            k.coll(c_sem, "AllGather", GROUPS, kbo_t.ap().opt(), kba_t.ap().opt(), reads=[b_kbo], writes=[b_kba])
            k.barrier()
            kbias = slot_f32(4, 2 * NFC, 2 * NFC).rearrange("p (s t h) -> p s t h", s=2, h=NH)
            b_kb = Buf()
            k.op(dve, lambda e: e.tensor_copy(out=kbias[:, 1], in_=nf), writes=[b_kb])
            k.dma(sp, d_in, kbias[:, 0], kba_t.ap()[0:128, :].rearrange("p (t h) -> p t h", h=NH), reads=[b_kba], writes=[b_kb])
            k.op(dve, lambda e: e.tensor_scalar(out=kbias[:, 0], in0=kbias[:, 0], scalar1=mbias[:, 0:1], scalar2=None, op0=ALU.add),
                 reads=[b_const], writes=[b_kb])
            bO = [Buf() for _ in range(NCH)]
            ktav = kta_t.ap().rearrange("(c s h r) t -> r c s h t", c=NCH, s=2, r=HA)
            ktov = kto_t.ap().rearrange("(c h r) t -> r c h t", c=NCH, r=HA)
            vav = va_t.ap().rearrange("(c s q p) d -> p c s q d", c=NCH, s=2, p=128)
            vov = vo_t.ap().rearrange("(t p) d -> p t d", p=128)
            KTb = [SL[:, 2 * q, 0:4 * T].rearrange("p (h s t) -> p h s t", h=2, s=2) for q in range(2)]
            QTb = [SL[:, 2 * q + 1, 0:2 * T].rearrange("p (h t) -> p h t", h=2) for q in range(2)]
            Vb = [SL[:, 2 * q + 1, 2 * T:2 * T + 2 * NT * 128].rearrange("p (s t d) -> p s t d", s=2, d=128) for q in range(2)]
            bKQV = [Buf(), Buf()]
            Pb = [slot_bf(5, 512 * q, 512) for q in range(4)]
            bP = [Buf() for _ in range(4)]
            rc = [slot_f32(5, 1024 + 512 * q, 512) for q in range(2)]
            brc = [Buf(), Buf()]

            def loadpair(jp):
                q = jp % 2
                wr = [bKQV[q]]
                for hh in range(2):
                    h = 2 * jp + hh
                    k.dma(sp, d_in, KTb[q][0:HA, hh, 0, :].rearrange("r (c t) -> r c t", t=512), ktav[:, :, 0, h, :], reads=b_kta, writes=wr)
                    k.dma(sp, d_in, KTb[q][0:HA, hh, 1, :].rearrange("r (c t) -> r c t", t=512), ktov[:, :, h, :], reads=[b_kto], writes=wr)
                    k.dma(sp, d_in, QTb[q][0:HA, hh, :], qtv[:, h, :], reads=[b_qt], writes=wr)
                for c in range(NCH):
                    k.dma(sp, d_in, Vb[q][:, 0, 4 * c:4 * c + 4, :], vav[:, c, 0, :, jp * 128:(jp + 1) * 128], reads=[b_va[c]], writes=wr)
                k.dma(sp, d_in, Vb[q][:, 1], vov[:, :, jp * 128:(jp + 1) * 128], reads=[b_vo], writes=wr)

            Sb = [PS[0], PS[1], PT[0][:, :].bitcast(F32), PT[1][:, :].bitcast(F32)]
            bSb = [bPS[0], bPS[1], bPT[0], bPT[1]]
            NSB = 4
            SDEPTH = 3
            units = []
            item = 0
            for jp in range(8):
                for hh in range(2):
                    for c in range(NCH):
                        blocks = [(0, kb) for kb in range(NT)] + [(1, kb) for kb in range(4 * c + 4)]
                        for bi, (src, kb) in enumerate(blocks):
                            units.append((jp, hh, c, bi, len(blocks), src, kb, item))
                        item += 1
            loaded = set()

            def geom(u):
                jp, hh, c, bi, nb, src, kb, item = u
                diag = (src == 1 and kb >= 4 * c)
                i = kb - 4 * c if diag else 0
                return diag, i, c * 512 + i * 128, 512 - i * 128

            def emit_S(ui):
                u = units[ui]
                jp, hh, c, bi, nb, src, kb, item = u
                q = jp % 2
                diag, i, q0, n = geom(u)
                ps, bps = Sb[ui % NSB], bSb[ui % NSB]
                k.op(pe, lambda e: e.matmul(ps[:, 0:n], KTb[q][0:HA, hh, src, kb * 128:(kb + 1) * 128], QTb[q][0:HA, hh, q0:q0 + n],
                                            start=True, stop=(not diag)),
                     reads=[bKQV[q]], writes=[bps], inc=(not diag))
                if diag:
                    k.op(pe, lambda e: e.matmul(ps[:, 0:128], ident[:], maskT[:], start=False, stop=True),
                         reads=[b_const], writes=[bps])

            def emit_rest(ui):
                u = units[ui]
                jp, hh, c, bi, nb, src, kb, item = u
                q = jp % 2
                h = 2 * jp + hh
                rows = slice(hh * 64, hh * 64 + 64)
                diag, i, q0, n = geom(u)
                ps, bps = Sb[ui % NSB], bSb[ui % NSB]
                po, bpo = PS[2 + item % 2], bPS[2 + item % 2]
                pm, bpm = PS[4 + item % 2], bPS[4 + item % 2]
                pb, bpb = Pb[ui % 4], bP[ui % 4]
                k.op(act, lambda e: e.activation(out=pb[:, 0:n], in_=ps[:, 0:n], func=AF.Exp, bias=kbias[:, src, kb, h:h + 1], scale=1.0),
                     reads=[bps, b_kb], writes=[bpb])
                first = (bi == 0)
                last = (bi == nb - 1)
                k.op(pe, lambda e: e.matmul(po[rows, i * 128:512], Vb[q][:, src, kb, hh * 64:(hh + 1) * 64], pb[:, 0:n], start=first, stop=last),
                     reads=[bpb, bKQV[q]], writes=[bpo], inc=False)
                k.op(pe, lambda e: e.matmul(pm[rows, i * 128:512], onesb[:, 0:64], pb[:, 0:n], start=first, stop=last),
                     reads=[bpb, b_const], writes=[bpm], inc=True)
                if last:
                    uu = item % 2
                    k.op(dve, lambda e: e.reciprocal(out=rc[uu][rows, :], in_=pm[rows, :]), reads=[bpm], writes=[brc[uu]])
                    k.op(dve, lambda e: e.tensor_tensor(out=R1[rows, jp, c * 512:(c + 1) * 512], in0=po[rows, :], in1=rc[uu][rows, :], op=ALU.mult),
                         reads=[bpo, brc[uu]], writes=[bO[c]])
                    if hh == 1 and c == NCH - 1 and jp + 2 < 8:
                        loadpair(jp + 2)

            loadpair(0)
            loadpair(1)
            for ui in range(len(units) + SDEPTH):
                if ui < len(units):
                    emit_S(ui)
                if ui - SDEPTH >= 0:
                    emit_rest(ui - SDEPTH)
            k.barrier()
            outproj(fwo_d[j], bO, 0, 0)

        def gmlp(bH):
            w_in = SL[:, 0:2, :].rearrange("p s n -> p (s n)")[:, 0:KC * 2 * D].rearrange("p (k n) -> p k n", k=KC)
            bw = Buf()
            k.dma(pool, d_w, w_in, swin_d.rearrange("(k p) n -> p k n", p=128), writes=[bw])
            w_o = slot_bf(2, 0, KC * D).rearrange("p (k n) -> p k n", k=KC)
            bwo = Buf()
            k.dma(pool, d_w, w_o, swo_d.rearrange("(k p) n -> p k n", p=128), writes=[bwo])
            lng = slot_f32(3, 0, D)
            lnb = slot_f32(3, D, D)
            wsf = slot_f32(3, 2 * D, D).rearrange("p (g s) -> p g s", g=8)
            wsb = slot_bf(3, 6 * D, D).rearrange("p (g s) -> p g s", g=8)
            wsT = slot_bf(3, 7 * D, D).rearrange("p (g t) -> p g t", g=8)
            bsc = slot_f32(5, 100, 8)
            b_p = Buf()
            k.dma(sp, d_in, lng, slng_d[:, :], writes=[b_p])
            k.dma(sp, d_in, lnb, slnb_d[:, :], writes=[b_p])
            k.dma(sp, d_in, wsf, sws_d.rearrange("g t s -> t g s"), writes=[b_p])
            k.dma(sp, d_in, bsc, sbs_d[:, :], writes=[b_p])
            k.op(dve, lambda e: e.tensor_copy(out=wsb, in_=wsf), writes=[b_p])
            for g in range(8):
                k.op(pe, lambda e, g=g: e.transpose(PT[0][:, g * 128:(g + 1) * 128], wsb[:, g, :], ident[:]),
                     reads=[b_p, b_const], writes=[bPT[0]], inc=(g == 7))
            k.op(dve, lambda e: e.tensor_copy(out=wsT, in_=PT[0][:, :].rearrange("p (g t) -> p g t", g=8)), reads=[bPT[0]], writes=[b_p])
            k.op(dve, lambda e: e.memset(wsT[64:128, :, 0:64], 0.0), writes=[b_p])
            ub = [slot_bf(4, D * q, D) for q in range(2)]
            vf = [slot_f32(4, D + D * q, D) for q in range(2)]
            vnb = [slot_bf(4, 6 * D + D * q, D) for q in range(2)]
            st6 = [slot_f32(5, 16 * q, 12) for q in range(2)]
            mv = [slot_f32(5, 64 + 16 * q, 2) for q in range(2)]
            pbuf = [slot_bf(5, 256 + D * q, D) for q in range(2)]
            pT = [slot_bf(5, 256 + 2 * D + D * q, D).rearrange("p (k t) -> p k t", k=KC) for q in range(2)]
            bu, bv, bvn, bst, bpb, bpT = ([Buf(), Buf()] for _ in range(6))
            res = Resid(0)
            for t in range(NT):
                u = t % 2
                for n4 in range(4):
                    ps, bps = PS[n4], bPS[n4]
                    for kc in range(KC):
                        k.op(pe, lambda e, kc=kc, n4=n4: e.matmul(ps[:, :], R1[:, kc, t * 128:(t + 1) * 128], w_in[:, kc, n4 * 512:(n4 + 1) * 512],
                                                                  start=(kc == 0), stop=(kc == KC - 1)),
                             reads=[bw, bH[t // 4]], writes=[bps], inc=(kc == KC - 1))
                    if n4 < 2:
                        k.op(act, lambda e, n4=n4: e.activation(out=ub[u][:, n4 * 512:(n4 + 1) * 512], in_=ps[:, :], func=AF.Gelu_apprx_tanh),
                             reads=[bps], writes=[bu[u]])
                    else:
                        k.op(act, lambda e, n4=n4: e.activation(out=vf[u][:, (n4 - 2) * 512:(n4 - 1) * 512], in_=ps[:, :], func=AF.Gelu_apprx_tanh),
                             reads=[bps], writes=[bv[u]])
                for hf in range(2):
                    k.op(dve, lambda e, hf=hf: e.bn_stats(out=st6[u][:, hf * 6:(hf + 1) * 6], in_=vf[u][:, hf * 512:(hf + 1) * 512]),
                         reads=[bv[u]], writes=[bst[u]])
                k.op(dve, lambda e: e.bn_aggr(out=mv[u], in_=st6[u].rearrange("p (c s) -> p c s", s=6)), writes=[bst[u]])
                k.op(act, lambda e: e.activation(out=mv[u][:, 1:2], in_=mv[u][:, 1:2], func=AF.Sqrt, bias=epsc[:, 0:1], scale=1.0),
                     reads=[b_const], writes=[bst[u]])
                k.op(dve, lambda e: e.reciprocal(out=mv[u][:, 1:2], in_=mv[u][:, 1:2]), writes=[bst[u]])
                k.op(dve, lambda e: e.tensor_scalar(out=vf[u], in0=vf[u], scalar1=mv[u][:, 0:1], scalar2=mv[u][:, 1:2],
                                                    op0=ALU.subtract, op1=ALU.mult), reads=[bst[u]], writes=[bv[u]])
                k.op(pool, lambda e: e.tensor_tensor(out=vf[u], in0=vf[u], in1=lng, op=ALU.mult), reads=[b_p], writes=[bv[u]])
                k.op(pool, lambda e: e.tensor_tensor(out=vnb[u], in0=vf[u], in1=lnb, op=ALU.add), reads=[b_p, bv[u]], writes=[bvn[u]])
                for hf in range(2):
                    ps, bps = PS[4 + hf], bPS[4 + hf]
                    for gg in range(4):
                        g = hf * 4 + gg
                        k.op(pe, lambda e, g=g, gg=gg: e.matmul(ps[:, gg * 128:(gg + 1) * 128], wsT[:, g, :], vnb[u][:, g * 128:(g + 1) * 128],
                                                                start=True, stop=True),
                             reads=[b_p, bvn[u]], writes=[bps], inc=(gg == 3))
                    for gg in range(4):
                        g = hf * 4 + gg
                        k.op(dve, lambda e, g=g, gg=gg: e.scalar_tensor_tensor(out=pbuf[u][:, g * 128:(g + 1) * 128], in0=ps[:, gg * 128:(gg + 1) * 128],
                                                                               scalar=bsc[:, g:g + 1], in1=ub[u][:, g * 128:(g + 1) * 128],
                                                                               op0=ALU.add, op1=ALU.mult),
                             reads=[bps, bu[u], b_p], writes=[bpb[u]])
                pt, bpt = PT[u], bPT[u]
                for kc in range(KC):
                    k.op(pe, lambda e, kc=kc: e.transpose(pt[:, kc * 128:(kc + 1) * 128], pbuf[u][:, kc * 128:(kc + 1) * 128], ident[:]),
                         reads=[bpb[u], b_const], writes=[bpt], inc=(kc == KC - 1))
                k.op(act, lambda e: e.copy(out=pT[u], in_=pt[:, :].rearrange("p (k t) -> p k t", k=KC)), reads=[bpt], writes=[bpT[u]])
                for half in range(2):
                    ps, bps = PS[half], bPS[half]
                    for kc in range(KC):
                        k.op(pe, lambda e, kc=kc, half=half: e.matmul(ps[:, :], pT[u][:, kc, :], w_o[:, kc, half * 512:(half + 1) * 512],
                                                                      start=(kc == 0), stop=(kc == KC - 1)),
                             reads=[bwo, bpT[u]], writes=[bps], inc=(kc == KC - 1))
                    res.add(ps, bps, t, half)

        def conv(bH):
            YW = 32 + T
            Y = SL[:, 0:3, :].rearrange("p s n -> p (s n)")[:, 0:KC * YW].rearrange("p (k t) -> p k t", k=KC)
            bY = [Buf() for _ in range(KC)]
            bHalo = Buf()
            cols = slot_f32(3, 0, 16 + 4 * KC + KC * CW)
            b1c = cols[:, 0:16]
            bdc = cols[:, 16:16 + KC]
            lgc = cols[:, 16 + KC:16 + 2 * KC]
            lbc = cols[:, 16 + 2 * KC:16 + 3 * KC]
            wdc = cols[:, 16 + 4 * KC:16 + 4 * KC + KC * CW].rearrange("p (k w) -> p k w", k=KC)
            b2row = slot_f32(3, 512, D)
            b_p = Buf()
            k.dma(sp, d_in, b1c, cb1_d[:, :], writes=[b_p])
            k.dma(sp, d_in, bdc, cbd_d[:, :], writes=[b_p])
            k.dma(sp, d_in, lgc, clg_d[:, :], writes=[b_p])
            k.dma(sp, d_in, lbc, clb_d[:, :], writes=[b_p])
            k.dma(sp, d_in, wdc, cwd_d.rearrange("p (k w) -> p k w", k=KC), writes=[b_p])
            k.dma(sp, d_in, b2row[0:1, :], cb2_d[:, :], writes=[b_p])
            wv = cw1_d.rearrange("(k p) n -> p k n", p=128)
            wr = [slot_bf(4, 2048 * q, 2048).rearrange("p (k h n) -> p k h n", k=KC, h=2) for q in range(3)]
            bwr = [Buf() for _ in range(3)]
            sgm = [slot_f32(5, 512 * q, 512) for q in range(2)]
            bsg = [Buf(), Buf()]
            hst = slot_bf(5, 4096, KC * 32).rearrange("p (k t) -> p k t", k=KC)
            hin = slot_bf(5, 4096 + KC * 32, KC * 32).rearrange("p (k t) -> p k t", k=KC)
            b_h = Buf()

            def loadw(jc):
                s = jc % 3
                k.dma(pool, d_w, wr[s][:, :, 0, :], wv[:, :, jc * 128:(jc + 1) * 128], writes=[bwr[s]])
                k.dma(pool, d_w, wr[s][:, :, 1, :], wv[:, :, D + jc * 128:D + (jc + 1) * 128], writes=[bwr[s]])

            loadw(0)
            loadw(1)
            n = 0
            corder = [NCH - 1] + list(range(NCH - 1))
            for jc in range(KC):
                s = jc % 3
                if jc + 2 < KC:
                    loadw(jc + 2)
                for c in corder:
                    pa, bpa = PS[(2 * n) % 4], bPS[(2 * n) % 4]
                    pg, bpg = PS[(2 * n + 1) % 4], bPS[(2 * n + 1) % 4]
                    u = n % 2
                    n += 1
                    for hf, (pp, bpp) in enumerate(((pa, bpa), (pg, bpg))):
                        for kc in range(KC):
                            k.op(pe, lambda e, kc=kc, hf=hf, pp=pp: e.matmul(pp[:, :], wr[s][:, kc, hf, :], R1[:, kc, c * 512:(c + 1) * 512],
                                                                             start=(kc == 0), stop=(kc == KC - 1)),
                                 reads=[bwr[s], bH[c]], writes=[bpp], inc=(kc == KC - 1))
                    k.op(act, lambda e, jc=jc: e.activation(out=sgm[u], in_=pg[:, :], func=AF.Sigmoid, bias=b1c[:, 8 + jc:9 + jc], scale=1.0),
                         reads=[bpg, b_p], writes=[bsg[u]])
                    k.op(dve, lambda e, jc=jc, c=c: e.scalar_tensor_tensor(out=Y[:, jc, 32 + c * 512:32 + (c + 1) * 512], in0=pa[:, :],
                                                                           scalar=b1c[:, jc:jc + 1], in1=sgm[u], op0=ALU.add, op1=ALU.mult),
                         reads=[bpa, bsg[u], b_p], writes=[bY[jc]])
            k.op(dve, lambda e: e.tensor_copy(out=hst, in_=Y[:, :, T:T + 32]), reads=bY, writes=[b_h])
            k.dma(sp, d_st, hlo_t.ap(), hst.rearrange("p k t -> p (k t)"), reads=[b_h], writes=[b_hlo])
            k.coll(c_sem, "AllGather", GROUPS, hlo_t.ap().opt(), hla_t.ap().opt(), reads=[b_hlo], writes=[b_hla])
            k.dma(sp, d_in, hin.rearrange("p k t -> p (k t)"), hla_t.ap()[0:128, :], reads=[b_hla], writes=[b_h])
            k.op(dve, lambda e: e.tensor_scalar(out=Y[:, :, 0:32], in0=hin, scalar1=flagc[:, 0:1], scalar2=None, op0=ALU.mult),
                 reads=[b_h, b_const], writes=bY)
            k.barrier()
            dgb = [SL[:, 4, 0:CW * 128].rearrange("p (w m) -> p w m", w=CW), SL[:, 5, 0:CW * 128].rearrange("p (w m) -> p w m", w=CW)]
            bdg = [Buf(), Buf()]
            bC = [Buf() for _ in range(NCH)]
            n = 0
            for jc in range(KC):
                u = jc % 2
                for w in range(CW):
                    E = dve if w % 2 == 0 else pool
                    k.op(E, lambda e, w=w, jc=jc: e.tensor_scalar(out=dgb[u][:, w, :], in0=ident[:], scalar1=wdc[:, jc, w:w + 1], scalar2=None, op0=ALU.mult),
                         reads=[b_p, b_const], writes=[bdg[u]])
                for c in range(NCH):
                    ps, bps = PS[n % 4], bPS[n % 4]
                    n += 1
                    for w in range(CW):
                        o = 32 + c * 512 - (CW - 1) + w
                        k.op(pe, lambda e, w=w, o=o, jc=jc: e.matmul(ps[:, :], dgb[u][:, w, :], Y[:, jc, o:o + 512], start=(w == 0), stop=(w == CW - 1)),
                             reads=[bdg[u], bY[jc]], writes=[bps], inc=(w == CW - 1))
                    k.op(act, lambda e, jc=jc, c=c: e.activation(out=R1[:, jc, c * 512:(c + 1) * 512], in_=ps[:, :], func=AF.Identity,
                                                                 bias=bdc[:, jc:jc + 1], scale=1.0),
                         reads=[bps, b_p], writes=[bC[c]])
            k.barrier()
            w2 = slot_bf(0, 0, KC * D).rearrange("p (k n) -> p k n", k=KC)
            bw2 = Buf()
            k.dma(pool, d_w, w2, cw2_d.rearrange("(k p) n -> p k n", p=128), writes=[bw2])
            ysq = [slot_bf(1, 4096 * q, 4096).rearrange("p (k t) -> p k t", k=KC) for q in range(2)]
            bys = [Buf(), Buf()]
            zT = [slot_bf(2, 4096 * q, 4096).rearrange("p (k t) -> p k t", k=KC) for q in range(2)]
            bz = [Buf(), Buf()]
            Rr = [slot_f32(4, 1024 * q, 512) for q in range(2)]
            MR = [slot_f32(4, 1024 * q + 512, 512) for q in range(2)]
            bR = [Buf(), Buf()]
            zt = [slot_f32(5, 512 * q, 512) for q in range(2)]
            bzt = [Buf(), Buf()]
            res = Resid(0)
            nz = 0
            nres = 0
            for c in range(NCH):
                u = c % 2
                for jc in range(KC):
                    k.op(pool, lambda e, jc=jc: e.tensor_tensor(out=ysq[u][:, jc, :], in0=R1[:, jc, c * 512:(c + 1) * 512], in1=R1[:, jc, c * 512:(c + 1) * 512], op=ALU.mult),
                         reads=[bC[c]], writes=[bys[u]])
                pm, bpm = PS[4], bPS[4]
                pq, bpq = PS[5], bPS[5]
                for jc in range(KC):
                    k.op(pe, lambda e, jc=jc: e.matmul(pm[:, :], invd[:], R1[:, jc, c * 512:(c + 1) * 512], start=(jc == 0), stop=(jc == KC - 1)),
                         reads=[bC[c], b_const], writes=[bpm], inc=(jc == KC - 1))
                for jc in range(KC):
                    k.op(pe, lambda e, jc=jc: e.matmul(pq[:, :], invd[:], ysq[u][:, jc, :], start=(jc == 0), stop=(jc == KC - 1)),
                         reads=[bys[u], b_const], writes=[bpq], inc=(jc == KC - 1))
                k.op(act, lambda e: e.activation(out=MR[u], in_=pm[:, :], func=AF.Square), reads=[bpm], writes=[bR[u]])
                k.op(dve, lambda e: e.tensor_tensor(out=Rr[u], in0=pq[:, :], in1=MR[u], op=ALU.subtract), reads=[bpq], writes=[bR[u]])
                k.op(act, lambda e: e.activation(out=Rr[u], in_=Rr[u], func=AF.Sqrt, bias=epsc[:, 0:1], scale=1.0), reads=[b_const], writes=[bR[u]])
                k.op(dve, lambda e: e.reciprocal(out=Rr[u], in_=Rr[u]), writes=[bR[u]])
                k.op(dve, lambda e: e.tensor_tensor(out=MR[u], in0=pm[:, :], in1=Rr[u], op=ALU.mult), reads=[bpm], writes=[bR[u]])
                for jc in range(KC):
                    v = nz % 2
                    nz += 1
                    k.op(dve, lambda e, jc=jc: e.tensor_tensor(out=zt[v], in0=R1[:, jc, c * 512:(c + 1) * 512], in1=Rr[u], op=ALU.mult),
                         reads=[bC[c], bR[u]], writes=[bzt[v]])
                    k.op(pool, lambda e: e.tensor_tensor(out=zt[v], in0=zt[v], in1=MR[u], op=ALU.subtract), reads=[bR[u]], writes=[bzt[v]])
                    k.op(act, lambda e, jc=jc: e.activation(out=zT[u][:, jc, :], in_=zt[v], func=AF.Silu, bias=lbc[:, jc:jc + 1], scale=lgc[:, jc:jc + 1]),
                         reads=[bzt[v], b_p], writes=[bz[u]])
                for tq in range(4):
                    t = 4 * c + tq
                    for half in range(2):
                        ps, bps = PS[nres % 4], bPS[nres % 4]
                        nres += 1
                        for kc in range(KC):
                            k.op(pe, lambda e, kc=kc, tq=tq, half=half: e.matmul(ps[:, :], zT[u][:, kc, tq * 128:(tq + 1) * 128], w2[:, kc, half * 512:(half + 1) * 512],
                                                                               start=(kc == 0), stop=False),
                                 reads=[bw2, bz[u]], writes=[bps], inc=False)
                        k.op(pe, lambda e, half=half: e.matmul(ps[:, :], onesf[0:1, :], b2row[0:1, half * 512:(half + 1) * 512], start=False, stop=True),
                             reads=[b_p, b_const], writes=[bps])
                        res.add(ps, bps, t, half)

        consts()
        load_x()
        for i in layers:
            kind = i % 3
            j = i // 3
            k.barrier()
            mods(i)
            k.barrier()
            bH = [Buf() for _ in range(NCH)]
            norm_to_hT(0, bH)
            k.barrier()
            if "mix" not in skip:
                if kind == 0:
                    fox(j, bH)
                elif kind == 1:
                    gmlp(bH)
                else:
                    conv(bH)
            k.barrier()
            bH = [Buf() for _ in range(NCH)]
            norm_to_hT(1, bH)
            k.barrier()
            if "mlp" not in skip:
                mlp(i, bH)
        k.barrier()
        ov = out_d.rearrange("(t p) d -> p t d", p=128)
        for t0 in range(0, NT, 4):
            k.dma(sp, d_st, ov[:, t0:t0 + 4, :], X[:, t0:t0 + 4, :], reads=bX[t0:t0 + 4])
        for E in k.engs:
            E.wait((d_st, d_st.cnt))
    return nc


def make_in_maps(inputs, S):
    T = S // 2
    f = lambda a: np.ascontiguousarray(np.asarray(a, dtype=np.float32))
    x = f(inputs["x"])
    c = f(inputs["c"])
    B = x.shape[0]
    shared = {
        "norm_mix": f(inputs["norm_mix"]), "norm_mlp": f(inputs["norm_mlp"]),
        "w_ada": f(inputs["w_ada"]), "b_ada": f(inputs["b_ada"]),
        "w_mlp_in": f(inputs["w_mlp_in"]), "w_mlp_out": f(inputs["w_mlp_out"]),
        "fox_w_in": f(inputs["fox_w_in"]),
        "fox_bf_bc": f(np.broadcast_to(np.asarray(inputs["fox_b_f"])[:, None, :], (2, 128, NH))),
        "fox_qg_bc": f(np.broadcast_to(np.asarray(inputs["fox_q_norm"])[:, None, :], (2, 128, HD))),
        "fox_kg_bc": f(np.broadcast_to(np.asarray(inputs["fox_k_norm"])[:, None, :], (2, 128, HD))),
        "fox_w_out": f(inputs["fox_w_out"]),
        "sg_w_in": f(inputs["sg_w_in"])[0],
        "sg_lng_bc": f(np.broadcast_to(np.asarray(inputs["sg_ln_g"])[0][None, :], (128, D))),
        "sg_lnb_bc": f(np.broadcast_to(np.asarray(inputs["sg_ln_b"])[0][None, :], (128, D))),
        "sg_w_s": f(inputs["sg_w_s"])[0],
        "sg_bs_c": f(np.asarray(inputs["sg_b_s"])[0].T),
        "sg_w_out": f(inputs["sg_w_out"])[0],
        "cv_w_pw1": f(inputs["cv_w_pw1"])[0],
        "cv_bpw1_c": f(np.asarray(inputs["cv_b_pw1"])[0].reshape(16, 128).T),
        "cv_wdw_c": f(np.asarray(inputs["cv_w_dw"])[0].reshape(CW, KC, 128).transpose(2, 1, 0).reshape(128, KC * CW)),
        "cv_bdw_c": f(np.asarray(inputs["cv_b_dw"])[0].reshape(KC, 128).T),
        "cv_lng_c": f(np.asarray(inputs["cv_ln_g"])[0].reshape(KC, 128).T),
        "cv_lnb_c": f(np.asarray(inputs["cv_ln_b"])[0].reshape(KC, 128).T),
        "cv_w_pw2": f(inputs["cv_w_pw2"])[0],
        "cv_bpw2": f(np.asarray(inputs["cv_b_pw2"])[0][None, :]),
    }
    maps = []
    for core in range(2 * B):
        b, r = core // 2, core % 2
        m = dict(shared)
        m["x"] = np.ascontiguousarray(x[b, r * T:(r + 1) * T, :])
        m["cT"] = np.ascontiguousarray(c[b].reshape(KC, 128).T)
        m["flag"] = np.full((128, 1), float(r), np.float32)
        maps.append(m)
    return maps


_NC_CACHE = {}


def run(inputs, S, layers=(0, 1, 2, 3), skip=(), trace=False):
    NT = S // 256
    key = (NT, tuple(layers), tuple(skip))
    if key not in _NC_CACHE:
        _NC_CACHE[key] = build(NT, layers, skip)
    nc = _NC_CACHE[key]
    maps = make_in_maps(inputs, S)
    res = run_bass_kernel_spmd(nc, maps, core_ids=list(range(8)), **({"trace": True} if trace else {}))
    B = 4
    T = S // 2
    out = np.empty((B, S, D), np.float32)
    for core in range(8):
        b, r = core // 2, core % 2
        out[b, r * T:(r + 1) * T, :] = res.results[core]["out"]
    return out, res


def kernel(**inputs):
    out, _ = run(inputs, 4096)
    return out
```

```python
import os
import numpy as np
from contextlib import ExitStack
import concourse.bass as bass
import concourse.mybir as mybir
from concourse.bass_utils import run_bass_kernel_spmd

F32 = mybir.dt.float32
BF16 = mybir.dt.bfloat16
AF = mybir.ActivationFunctionType
ALU = mybir.AluOpType
AX = mybir.AxisListType

D = 1024
DFF = 4096
NH = 16
HD = 64
KC = 8
DEPTH = 4
EPS = 1e-6
HA = 67
CW = 31
NEG = -30000.0
SLOT = 8192
NSLOT = 6


class Buf:
    __slots__ = ("name", "w", "r", "ds")

    def __init__(self, name=""):
        self.name = name
        self.w = None
        self.r = {}
        self.ds = None


class SemC:
    def __init__(self, k, name):
        self.name = name
        self.sem = k.es.enter_context(k.nc.semaphore(name))
        self.cnt = 0


class Eng:
    def __init__(self, k, e, name, in_order_self=False):
        self.e = e
        self.name = name
        self.sc = SemC(k, "s_" + name)
        self.seen = {}
        self.in_order_self = in_order_self

    def wait(self, tk):
        if tk is None:
            return
        sc, val = tk
        if val <= 0:
            return
        if sc is self.sc and self.in_order_self:
            return
        if self.seen.get(sc, 0) >= val:
            return
        self.e.wait_ge(sc.sem, val)
        self.seen[sc] = val


class K:
    def __init__(self, nc, es):
        self.nc = nc
        self.es = es
        self.pe = Eng(self, nc.tensor, "pe", in_order_self=True)
        self.act = Eng(self, nc.scalar, "act")
        self.dve = Eng(self, nc.vector, "dve")
        self.pool = Eng(self, nc.gpsimd, "pool")
        self.sp = Eng(self, nc.sync, "sp")
        self.engs = [self.pe, self.act, self.dve, self.pool, self.sp]
        self.bar_sems = []
        self.dpool = []
        self.dnext = 0
        self.NDPOOL = 56

    def _buf_sem(self, writes):
        b = writes[0]
        if b.ds is None:
            if len(self.dpool) < self.NDPOOL:
                self.dpool.append(self.dsem("dp%d" % len(self.dpool)))
            b.ds = self.dpool[self.dnext % self.NDPOOL]
            self.dnext += 1
        return b.ds

    def sb(self, name, shape, dt):
        return self.es.enter_context(self.nc.sbuf_tensor(name, list(shape), dt))

    def ps(self, name, shape, dt):
        return self.es.enter_context(self.nc.psum_tensor(name, list(shape), dt))

    def dsem(self, name, barrier=True):
        s = SemC(self, name)
        if barrier:
            self.bar_sems.append(s)
        return s

    def _deps(self, E, reads, writes):
        for b in reads:
            E.wait(b.w)
        for b in writes:
            E.wait(b.w)
            for sc, v in b.r.items():
                E.wait((sc, v))

    def _record(self, tk, reads, writes):
        sc, v = tk
        for b in reads:
            if b.r.get(sc, 0) < v:
                b.r[sc] = v
        for b in writes:
            b.w = tk
            b.r = {}

    def op(self, E, emit, reads=(), writes=(), inc=True):
        self._deps(E, reads, writes)
        ins = emit(E.e)
        if inc:
            E.sc.cnt += 1
            ins.then_inc(E.sc.sem, 1)
            tk = (E.sc, E.sc.cnt)
        else:
            tk = (E.sc, E.sc.cnt + 1)
        self._record(tk, reads, writes)
        return tk

    def dma(self, Q, dsem, out, in_, reads=(), writes=()):
        if writes:
            dsem = self._buf_sem(writes)
        self._deps(Q, reads, writes)
        ins = Q.e.dma_start(out=out, in_=in_)
        dsem.cnt += 16
        ins.then_inc(dsem.sem, 16)
        tk = (dsem, dsem.cnt)
        self._record(tk, reads, writes)
        return tk

    def dma_rows(self, Q, out, in_, reads=(), writes=()):
        n = out.shape[0]
        self.dma(Q, None, out[0:64], in_[0:64], reads=reads, writes=writes)
        self.dma(Q, None, out[64:n], in_[64:n], reads=reads, writes=writes)

    def coll(self, csem, kind, groups, in_ap, out_ap, reads=(), writes=()):
        Q = self.pool
        b = writes[0]
        if b.ds is None:
            b.ds = self.dsem("cs_" + b.name)
        csem = b.ds
        self._deps(Q, reads, writes)
        ins = Q.e.collective_compute(kind, ALU.bypass, replica_groups=groups, ins=[in_ap], outs=[out_ap])
        csem.cnt += 1
        ins.then_inc(csem.sem)
        tk = (csem, csem.cnt)
        self._record(tk, reads, writes)
        return tk

    def barrier(self):
        for E in self.engs:
            for E2 in self.engs:
                if E2 is not E:
                    E.wait((E2.sc, E2.sc.cnt))
            for s in self.bar_sems:
                E.wait((s, s.cnt))


def build(NT, layers=(0, 1, 2, 3), skip=()):
    T = NT * 128
    NCH = NT // 4
    assert NT % 4 == 0
    nc = bass.Bass("TRN2", target_bir_lowering=False)
    es = ExitStack()
    with es:
        k = K(nc, es)
        pe, act, dve, pool, sp = k.pe, k.act, k.dve, k.pool, k.sp

        def din(name, shape, dt=F32):
            return nc.dram_tensor(name, list(shape), dt, kind="ExternalInput").ap()

        x_d = din("x", [T, D])
        cT_d = din("cT", [128, KC])
        flag_d = din("flag", [128, 1])
        nmix_d = din("norm_mix", [DEPTH, D])
        nmlp_d = din("norm_mlp", [DEPTH, D])
        wada_d = din("w_ada", [DEPTH, D, 6 * D])
        bada_d = din("b_ada", [DEPTH, 6 * D])
        wmi_d = din("w_mlp_in", [DEPTH, D, DFF])
        wmo_d = din("w_mlp_out", [DEPTH, DFF, D])
        fwin_d = din("fox_w_in", [2, D, 3 * D + NH])
        fbf_d = din("fox_bf_bc", [2, 128, NH])
        fqg_d = din("fox_qg_bc", [2, 128, HD])
        fkg_d = din("fox_kg_bc", [2, 128, HD])
        fwo_d = din("fox_w_out", [2, D, D])
        swin_d = din("sg_w_in", [D, 2 * D])
        slng_d = din("sg_lng_bc", [128, D])
        slnb_d = din("sg_lnb_bc", [128, D])
        sws_d = din("sg_w_s", [8, 128, 128])
        sbs_d = din("sg_bs_c", [128, 8])
        swo_d = din("sg_w_out", [D, D])
        cw1_d = din("cv_w_pw1", [D, 2 * D])
        cb1_d = din("cv_bpw1_c", [128, 16])
        cwd_d = din("cv_wdw_c", [128, KC * CW])
        cbd_d = din("cv_bdw_c", [128, KC])
        clg_d = din("cv_lng_c", [128, KC])
        clb_d = din("cv_lnb_c", [128, KC])
        cw2_d = din("cv_w_pw2", [D, D])
        cb2_d = din("cv_bpw2", [1, D])
        out_d = nc.dram_tensor("out", [T, D], F32, kind="ExternalOutput").ap()

        qt_t = nc.dram_tensor("qt_d", [NH * HA, T], BF16)
        kto_t = nc.dram_tensor("kt_own_d", [NCH * NH * HA, 512], BF16)
        kta_t = nc.dram_tensor("kt_all_d", [NCH * 2 * NH * HA, 512], BF16)
        vo_t = nc.dram_tensor("v_own_d", [T, D], BF16)
        va_t = nc.dram_tensor("v_all_d", [NCH * 2 * 512, D], BF16)
        kbo_t = nc.dram_tensor("kb_own_d", [128, NT * NH], F32)
        kba_t = nc.dram_tensor("kb_all_d", [256, NT * NH], F32)
        hlo_t = nc.dram_tensor("hl_own_d", [128, KC * 32], BF16)
        hla_t = nc.dram_tensor("hl_all_d", [256, KC * 32], BF16)
        b_qt, b_kto, b_vo = Buf("qt"), Buf("kto"), Buf("vo")
        b_kta = [Buf("kta%d" % c) for c in range(NCH)]
        b_va = [Buf("va%d" % c) for c in range(NCH)]
        b_kbo, b_kba, b_hlo, b_hla = Buf("kbo"), Buf("kba"), Buf("hlo"), Buf("hla")
        GROUPS = [[0, 1], [2, 3], [4, 5], [6, 7]]

        X = k.sb("X", [128, NT, D], F32)
        bX = [Buf("X%d" % t) for t in range(NT)]
        R1 = k.sb("R1", [128, KC, T], BF16)
        SL = k.sb("SL", [128, NSLOT, SLOT], BF16)
        ident = k.sb("ident", [128, 128], BF16)
        identf = k.sb("identf", [128, 128], F32)
        triu = k.sb("triu", [128, 128], F32)
        onesf = k.sb("onesf", [128, 128], F32)
        onesb = k.sb("onesb", [128, 128], BF16)
        invd = k.sb("invd", [128, 128], BF16)
        maskT = k.sb("maskT", [128, 128], BF16)
        gbc = k.sb("gbc", [128, 2, D], F32)
        modc = k.sb("modc", [128, 4, KC], F32)
        cact = k.sb("cact", [128, KC], BF16)
        cin = k.sb("cin", [128, KC], F32)
        flagc = k.sb("flagc", [128, 1], F32)
        mbias = k.sb("mbias", [128, 1], F32)
        epsc = k.sb("epsc", [128, 1], F32)
        rtmp = k.sb("rtmp", [128, 2, 512], F32)
        ssq = k.sb("ssq", [128, NT], F32)
        rstd = k.sb("rstd", [128, NT], F32)
        b_const, b_gbc, b_modc, b_cact, b_rowb, b_nrow, b_ssq = Buf(), Buf(), Buf(), Buf(), [Buf(), Buf(), Buf()], Buf(), Buf()

        PS = [k.ps("ps%d" % i, [128, 512], F32) for i in range(6)]
        bPS = [Buf("ps%d" % i) for i in range(6)]
        PT = [k.ps("pt%d" % i, [128, 1024], BF16) for i in range(2)]
        bPT = [Buf("pt%d" % i) for i in range(2)]

        for b_ in (b_qt, b_kto, b_vo, b_kbo, b_hlo):
            b_.ds = k.dsem("dd_" + b_.name)
        d_x = k.dsem("d_x")
        for b_ in bX:
            b_.ds = d_x
        d_in = k.dsem("d_in")
        d_w = k.dsem("d_w")
        d_st = k.dsem("d_st")
        c_sem = k.dsem("c_sem")

        def slot_bf(i, off, n):
            return SL[:, i, off:off + n]

        def slot_f32(i, off, n):
            return SL[:, i, 2 * off:2 * off + 2 * n].bitcast(F32)

        rowb = slot_f32(2, 0, 1536).rearrange("p (r n) -> p r n", r=3)
        nrow = slot_f32(2, 1536, 2 * D).rearrange("p (r n) -> p r n", r=2)

        def consts():
            w = [b_const]
            k.op(dve, lambda e: e.memset(identf[:], 0.0), writes=w)
            k.op(pool, lambda e: e.affine_select(out=identf[:], in_=identf[:], compare_op=ALU.not_equal, fill=1.0,
                                                 base=0, pattern=[[-1, 128]], channel_multiplier=1), writes=w)
            k.op(dve, lambda e: e.tensor_copy(out=ident[:], in_=identf[:]), writes=w)
            k.op(dve, lambda e: e.memset(onesf[:], 1.0), writes=w)
            k.op(dve, lambda e: e.memset(onesb[:], 1.0), writes=w)
            k.op(dve, lambda e: e.memset(invd[:], 1.0 / D), writes=w)
            k.op(dve, lambda e: e.memset(epsc[:], EPS), writes=w)
            k.op(pool, lambda e: e.affine_select(out=triu[:], in_=onesf[:], compare_op=ALU.is_ge, fill=0.0,
                                                 base=0, pattern=[[1, 128]], channel_multiplier=-1), writes=w)
            k.op(dve, lambda e: e.tensor_scalar(out=identf[:], in0=triu[:], scalar1=-1.0, scalar2=-NEG,
                                                op0=ALU.add, op1=ALU.mult), writes=w)
            k.op(dve, lambda e: e.tensor_copy(out=maskT[:], in_=identf[:]), writes=w)
            k.dma(sp, d_in, flagc[:], flag_d[:, :], writes=w)
            k.dma(sp, d_in, cin[:], cT_d[:, :], writes=w)
            k.op(dve, lambda e: e.tensor_scalar(out=mbias[:], in0=flagc[:], scalar1=-1.0, scalar2=-NEG,
                                                op0=ALU.add, op1=ALU.mult), writes=w)
            k.op(act, lambda e: e.activation(out=cact[:], in_=cin[:], func=AF.Silu), writes=w)

        def load_x():
            xv = x_d.rearrange("(t p) d -> p t d", p=128)
            for t0 in range(0, NT, 4):
                k.dma(sp, d_in, X[:, t0:t0 + 4, :], xv[:, t0:t0 + 4, :], writes=bX[t0:t0 + 4])

        def mods(i):
            wring = [slot_bf(0, 0, KC * 512).rearrange("p (k n) -> p k n", k=KC),
                     slot_bf(1, 0, KC * 512).rearrange("p (k n) -> p k n", k=KC)]
            bw = [Buf(), Buf()]
            wv = wada_d[i].rearrange("(k p) n -> p k n", p=128)
            k.dma(sp, d_in, nrow[0:1, 0, :], nmix_d[i:i + 1, :], writes=[b_nrow])
            k.dma(sp, d_in, nrow[0:1, 1, :], nmlp_d[i:i + 1, :], writes=[b_nrow])
            for n in range(12):
                kind = n // 2
                half = n % 2
                wb, bwb = wring[n % 2], bw[n % 2]
                k.dma(pool, d_w, wb, wv[:, :, n * 512:(n + 1) * 512], writes=[bwb])
                rb = n % 3
                k.dma(sp, d_in, rowb[0:1, rb, :], bada_d[i:i + 1, n * 512:(n + 1) * 512], writes=[b_rowb[rb]])
                ps = PS[n % 2]
                bps = bPS[n % 2]
                for kc in range(KC):
                    k.op(pe, lambda e, kc=kc: e.matmul(ps[0:1, :], cact[:, kc:kc + 1], wb[:, kc, :],
                                                       start=(kc == 0), stop=(kc == KC - 1)),
                         reads=[bwb, b_const], writes=[bps], inc=(kc == KC - 1))
                k.op(dve, lambda e: e.tensor_tensor(out=rowb[0:1, rb, :], in0=ps[0:1, :], in1=rowb[0:1, rb, :], op=ALU.add),
                     reads=[bps], writes=[b_rowb[rb]])
                if kind in (1, 4):
                    g = 0 if kind == 1 else 1
                    k.op(dve, lambda e: e.scalar_tensor_tensor(out=rowb[0:1, rb, :], in0=rowb[0:1, rb, :], scalar=1.0,
                                                               in1=nrow[0:1, g, half * 512:(half + 1) * 512],
                                                               op0=ALU.add, op1=ALU.mult),
                         reads=[b_nrow], writes=[b_rowb[rb]])
                if kind in (2, 5):
                    g = 0 if kind == 2 else 1
                    pb = PS[2 + n % 2]
                    bpb = bPS[2 + n % 2]
                    k.op(pe, lambda e: e.matmul(pb[:, :], onesf[0:1, :], rowb[0:1, rb, :], start=True, stop=True),
                         reads=[b_rowb[rb], b_const], writes=[bpb])
                    k.op(act, lambda e: e.copy(out=gbc[:, g, half * 512:(half + 1) * 512], in_=pb[:, :]),
                         reads=[bpb], writes=[b_gbc])
                else:
                    col = {0: 0, 1: 1, 3: 2, 4: 3}[kind]
                    pb = PS[2 + n % 2]
                    bpb = bPS[2 + n % 2]
                    for q in range(4):
                        k.op(pe, lambda e, q=q: e.matmul(pb[:, q:q + 1], rowb[0:1, rb, q * 128:(q + 1) * 128], onesf[0:1, 0:1],
                                                         start=True, stop=True),
                             reads=[b_rowb[rb], b_const], writes=[bpb], inc=(q == 3))
                    k.op(dve, lambda e: e.tensor_copy(out=modc[:, col, half * 4:(half + 1) * 4], in_=pb[:, 0:4]),
                         reads=[bpb], writes=[b_modc])

        def norm_to_hT(which, bH):
            shc, gc = (0, 1) if which == 0 else (2, 3)
            junk = slot_bf(5, 0, D)
            bj = Buf()
            xs = [slot_bf(5, D * (1 + q), D) for q in range(4)]
            bxs = [Buf() for _ in range(4)]
            for t in range(NT):
                k.op(act, lambda e, t=t: e.activation(out=junk, in_=X[:, t, :], func=AF.Square, accum_out=ssq[:, t:t + 1]),
                     reads=[bX[t]], writes=[bj, b_ssq])
            k.op(act, lambda e: e.activation(out=rstd[:], in_=ssq[:], func=AF.Sqrt, bias=epsc[:, 0:1], scale=1.0 / D),
                 reads=[b_ssq, b_const], writes=[b_ssq])
            k.op(dve, lambda e: e.reciprocal(out=rstd[:], in_=rstd[:]), reads=[b_ssq], writes=[b_ssq])
            for c in range(NCH):
                for q in range(4):
                    t = 4 * c + q
                    k.op(dve, lambda e, t=t, q=q: e.tensor_scalar(out=xs[q], in0=X[:, t, :], scalar1=rstd[:, t:t + 1], scalar2=None,
                                                                  op0=ALU.mult),
                         reads=[bX[t], b_ssq], writes=[bxs[q]])
                for kp in range(4):
                    pt, bpt = PT[kp % 2], bPT[kp % 2]
                    for kk in range(2):
                        kc = 2 * kp + kk
                        for q in range(4):
                            k.op(pe, lambda e, kc=kc, kk=kk, q=q: e.transpose(pt[:, kk * 512 + q * 128: kk * 512 + (q + 1) * 128],
                                                                              xs[q][:, kc * 128:(kc + 1) * 128], ident[:]),
                                 reads=[bxs[q], b_const], writes=[bpt], inc=(kk == 1 and q == 3))
                    for kk in range(2):
                        kc = 2 * kp + kk
                        E = act if kk == 0 else dve
                        if E is act:
                            k.op(act, lambda e, kc=kc, kk=kk: e.activation(out=R1[:, kc, c * 512:(c + 1) * 512], in_=pt[:, kk * 512:(kk + 1) * 512],
                                                                           func=AF.Identity, bias=modc[:, shc, kc:kc + 1], scale=modc[:, gc, kc:kc + 1]),
                                 reads=[bpt, b_modc], writes=[bH[c]])
                        else:
                            k.op(dve, lambda e, kc=kc, kk=kk: e.tensor_scalar(out=R1[:, kc, c * 512:(c + 1) * 512], in0=pt[:, kk * 512:(kk + 1) * 512],
                                                                              scalar1=modc[:, gc, kc:kc + 1], scalar2=modc[:, shc, kc:kc + 1],
                                                                              op0=ALU.mult, op1=ALU.add),
                                 reads=[bpt, b_modc], writes=[bH[c]])

        class Resid:
            def __init__(self, g):
                self.g = g
                self.tmp = [rtmp[:, q, :] for q in range(2)]
                self.bt = [Buf(), Buf()]
                self.n = 0

            def add(self, ps, bps, t, half):
                q = self.n % 2
                self.n += 1
                tmp, bt = self.tmp[q], self.bt[q]
                k.op(dve, lambda e: e.tensor_tensor(out=tmp, in0=ps[:, :], in1=gbc[:, self.g, half * 512:(half + 1) * 512], op=ALU.mult),
                     reads=[bps, b_gbc], writes=[bt])
                k.op(pool, lambda e: e.tensor_tensor(out=X[:, t, half * 512:(half + 1) * 512], in0=X[:, t, half * 512:(half + 1) * 512],
                                                     in1=tmp, op=ALU.add),
                     reads=[bt], writes=[bX[t]])

        def mlp(i, bH):
            hid = [SL[:, q, 0:4 * T].rearrange("p (f t) -> p f t", f=4) for q in range(2)]
            bhid = [[Buf() for _ in range(NCH)] for _ in range(2)]
            wi = [slot_bf(2 + q, 0, KC * 512).rearrange("p (k n) -> p k n", k=KC) for q in range(3)]
            wo = [slot_bf(2 + q, KC * 512, 4 * D).rearrange("p (f n) -> p f n", f=4) for q in range(3)]
            bwi = [Buf() for _ in range(3)]
            bwo = [Buf() for _ in range(3)]
            rr = [slot_f32(5, 512 * q, 512) for q in range(2)]
            brr = [Buf(), Buf()]
            wiv = wmi_d[i].rearrange("(k p) f -> p k f", p=128)
            wov = wmo_d[i].rearrange("(f p) d -> p f d", p=128)
            res = Resid(1)
            NG = DFF // 512

            def load(g):
                s = g % 3
                k.dma(pool, d_w, wi[s], wiv[:, :, g * 512:(g + 1) * 512], writes=[bwi[s]])
                k.dma(pool, d_w, wo[s], wov[:, g * 4:(g + 1) * 4, :], writes=[bwo[s]])

            load(0)
            load(1)
            nps = 0
            nr = 0
            for g in range(NG):
                s = g % 3
                hb = g % 2
                if g + 2 < NG:
                    load(g + 2)
                for c in range(NCH):
                    for fc in range(4):
                        ps, bps = PS[nps % 3], bPS[nps % 3]
                        nps += 1
                        for kc in range(KC):
                            k.op(pe, lambda e, kc=kc, fc=fc: e.matmul(ps[:, :], wi[s][:, kc, fc * 128:(fc + 1) * 128], R1[:, kc, c * 512:(c + 1) * 512],
                                                                      start=(kc == 0), stop=(kc == KC - 1)),
                                 reads=[bwi[s], bH[c]], writes=[bps], inc=(kc == KC - 1))
                        r, br = rr[nr % 2], brr[nr % 2]
                        nr += 1
                        k.op(act, lambda e: e.activation(out=r, in_=ps[:, :], func=AF.Relu), reads=[bps], writes=[br])
                        k.op(dve, lambda e, fc=fc: e.tensor_tensor(out=hid[hb][:, fc, c * 512:(c + 1) * 512], in0=r, in1=r, op=ALU.mult),
                             reads=[br], writes=[bhid[hb][c]])
                for t in range(NT):
                    for half in range(2):
                        ps, bps = PS[3 + nps % 3], bPS[3 + nps % 3]
                        nps += 1
                        for fc in range(4):
                            k.op(pe, lambda e, fc=fc: e.matmul(ps[:, :], hid[hb][:, fc, t * 128:(t + 1) * 128], wo[s][:, fc, half * 512:(half + 1) * 512],
                                                               start=(fc == 0), stop=(fc == 3)),
                                 reads=[bwo[s], bhid[hb][t // 4]], writes=[bps], inc=(fc == 3))
                        res.add(ps, bps, t, half)

        def outproj(w_dram, bH, gate, wslot, bias_row=None):
            w = slot_bf(wslot, 0, KC * D).rearrange("p (k n) -> p k n", k=KC)
            bw = Buf()
            k.dma(pool, d_w, w, w_dram.rearrange("(k p) n -> p k n", p=128), writes=[bw])
            res = Resid(gate)
            n = 0
            for t in range(NT):
                for half in range(2):
                    ps, bps = PS[n % 3], bPS[n % 3]
                    n += 1
                    for kc in range(KC):
                        last = (kc == KC - 1) and bias_row is None
                        k.op(pe, lambda e, kc=kc: e.matmul(ps[:, :], R1[:, kc, t * 128:(t + 1) * 128], w[:, kc, half * 512:(half + 1) * 512],
                                                           start=(kc == 0), stop=last),
                             reads=[bw, bH[t // 4]], writes=[bps], inc=last)
                    if bias_row is not None:
                        brow, bbrow = bias_row
                        k.op(pe, lambda e: e.matmul(ps[:, :], onesf[0:1, :], brow[0:1, half * 512:(half + 1) * 512], start=False, stop=True),
                             reads=[bbrow, b_const], writes=[bps])
                    res.add(ps, bps, t, half)

        def fox(j, bH):
            NFC = NT * NH
            lgf = slot_f32(4, 0, NFC).rearrange("p (t h) -> p t h", h=NH)
            nf = slot_f32(4, NFC, NFC).rearrange("p (t h) -> p t h", h=NH)
            kbp = slot_f32(4, 2 * NFC, NFC).rearrange("p (t h) -> p t h", h=NH)
            ftmp = slot_f32(4, 4 * NFC, NFC).rearrange("p (t h) -> p t h", h=NH)
            fs3 = slot_bf(4, 10 * NFC, 3 * NFC).rearrange("p (t h r) -> p t h r", h=NH, r=3)
            bfb = slot_f32(4, 7 * NFC, NH)
            qgb = slot_f32(4, 7 * NFC + 16, HD)
            kgb = slot_f32(4, 7 * NFC + 80, HD)
            b_f = Buf()
            b_fs3 = Buf()
            b_g = Buf()
            k.dma(sp, d_in, bfb, fbf_d[j], writes=[b_g])
            k.dma(sp, d_in, qgb, fqg_d[j], writes=[b_g])
            k.dma(sp, d_in, kgb, fkg_d[j], writes=[b_g])
            k.op(dve, lambda e: e.scalar_tensor_tensor(out=qgb, in0=qgb, scalar=HD ** -0.5, in1=kgb, op0=ALU.mult, op1=ALU.mult),
                 reads=[], writes=[b_g])
            wv = fwin_d[j].rearrange("(k p) n -> p k n", p=128)
            wring = [slot_bf(q, 0, KC * 512).rearrange("p (k n) -> p k n", k=KC) for q in range(3)]
            bwr = [Buf() for _ in range(3)]
            chunks = [("f", 3 * D, NH)] + [("k", D + qc * 512, 512) for qc in range(2)] + \
                     [("v", 2 * D + qc * 512, 512) for qc in range(2)] + [("q", qc * 512, 512) for qc in range(2)]

            def loadw(ci):
                kind, off, n = chunks[ci]
                s = ci % 3
                k.dma(pool, d_w, wring[s][:, :, 0:n], wv[:, :, off:off + n], writes=[bwr[s]])

            loadw(0)
            loadw(1)
            sq = [slot_f32(3, 512 * q, 512) for q in range(2)]
            bsq = [Buf(), Buf()]
            kf = [slot_f32(3, 1024 + 512 * q, 512) for q in range(2)]
            bkf = [Buf(), Buf()]
            ssh = [slot_f32(3, 2048 + 16 * q, 8) for q in range(2)]
            bssh = [Buf(), Buf()]
            qa = [slot_bf(3, 4224 + 8 * HA * q, 8 * HA).rearrange("p (h r) -> p h r", r=HA) for q in range(4)]
            bqa = [Buf() for _ in range(4)]
            vst = [slot_bf(3, 4224 + 32 * HA + 512 * q, 512) for q in range(2)]
            bvst = [Buf(), Buf()]
            stg = [slot_bf(5, 4096 * q, 4096).rearrange("p (h t) -> p h t", h=8) for q in range(2)]
            bstg = [Buf(), Buf()]
            qtv = qt_t.ap().rearrange("(h r) t -> r h t", r=HA)
            ktv = kto_t.ap().rearrange("(c h r) t -> c r h t", c=NCH, r=HA)
            nps = 0
            nu = 0
            nqa = 0
            st = {"nstg": 0, "npt": 0}
            pending = []

            def stage_b(kind, qc, t, qi):
                pt, bpt = PT[st["npt"] % 2], bPT[st["npt"] % 2]
                st["npt"] += 1
                for h in range(8):
                    k.op(pe, lambda e, h=h: e.transpose(pt[0:HA, h * 128:(h + 1) * 128], qa[qi][:, h, :], ident[:]),
                         reads=[bqa[qi], b_const], writes=[bpt], inc=(h == 7))
                sg_ = st["nstg"] % 2
                tq = t % 4
                k.op(act, lambda e: e.copy(out=stg[sg_][0:HA, :, tq * 128:(tq + 1) * 128],
                                           in_=pt[0:HA, :].rearrange("p (h t) -> p h t", h=8)),
                     reads=[bpt], writes=[bstg[sg_]])
                if tq == 3:
                    c = t // 4
                    if kind == "q":
                        dst = qtv[:, qc * 8:(qc + 1) * 8, c * 512:(c + 1) * 512]
                    else:
                        dst = ktv[c][:, qc * 8:(qc + 1) * 8, :]
                    k.dma_rows(sp, dst, stg[sg_][0:HA, :, :], reads=[bstg[sg_]], writes=[b_qt if kind == "q" else b_kto])
                    st["nstg"] += 1

            for ci, (kind, off, n) in enumerate(chunks):
                s = ci % 3
                if ci + 2 < len(chunks):
                    loadw(ci + 2)
                w = wring[s]
                for t in range(NT):
                    ps, bps = PS[nps % 4], bPS[nps % 4]
                    nps += 1
                    for kc in range(KC):
                        k.op(pe, lambda e, kc=kc: e.matmul(ps[:, 0:n], R1[:, kc, t * 128:(t + 1) * 128], w[:, kc, 0:n],
                                                           start=(kc == 0), stop=(kc == KC - 1)),
                             reads=[bwr[s], bH[t // 4]], writes=[bps], inc=(kc == KC - 1))
                    u = nu % 2
                    nu += 1
                    if kind == "f":
                        k.op(dve, lambda e, t=t: e.tensor_tensor(out=lgf[:, t, :], in0=ps[:, 0:NH], in1=bfb, op=ALU.add),
                             reads=[bps, b_g], writes=[b_f])
                    elif kind == "v":
                        qc = (off - 2 * D) // 512
                        k.op(act, lambda e: e.copy(out=vst[u], in_=ps[:, :]), reads=[bps], writes=[bvst[u]])
                        k.dma(sp, d_st, vo_t.ap()[t * 128:(t + 1) * 128, qc * 512:(qc + 1) * 512], vst[u], reads=[bvst[u]], writes=[b_vo])
                        if pending:
                            stage_b(*pending.pop(0))
                    else:
                        qc = (off % D) // 512
                        qi = nqa % 4
                        nqa += 1
                        k.op(act, lambda e: e.activation(out=sq[u], in_=ps[:, :], func=AF.Square), reads=[bps], writes=[bsq[u]])
                        k.op(dve, lambda e: e.tensor_reduce(out=ssh[u], in_=sq[u].rearrange("p (h d) -> p h d", d=HD), axis=AX.X, op=ALU.add),
                             reads=[bsq[u]], writes=[bssh[u]])
                        k.op(act, lambda e: e.activation(out=ssh[u], in_=ssh[u], func=AF.Sqrt, bias=epsc[:, 0:1], scale=1.0 / HD),
                             reads=[b_const], writes=[bssh[u]])
                        k.op(dve, lambda e: e.reciprocal(out=ssh[u], in_=ssh[u]), writes=[bssh[u]])
                        rb = ssh[u].unsqueeze(2).to_broadcast([128, 8, HD])
                        psv = ps[:, :].rearrange("p (h d) -> p h d", d=HD)
                        if kind == "q":
                            k.op(dve, lambda e: e.tensor_tensor(out=qa[qi][:, :, 0:HD], in0=psv, in1=rb, op=ALU.mult),
                                 reads=[bps, bssh[u]], writes=[bqa[qi]])
                            k.op(pool, lambda e, t=t, qc=qc: e.tensor_copy(out=qa[qi][:, :, HD:HA], in_=fs3[:, t, qc * 8:(qc + 1) * 8, :]),
                                 reads=[b_fs3], writes=[bqa[qi]])
                        else:
                            kfv = kf[u].rearrange("p (h d) -> p h d", d=HD)
                            k.op(dve, lambda e: e.tensor_tensor(out=kfv, in0=psv, in1=rb, op=ALU.mult),
                                 reads=[bps, bssh[u]], writes=[bkf[u]])
                            k.op(pool, lambda e: e.tensor_tensor(out=qa[qi][:, :, 0:HD], in0=kfv, in1=qgb.unsqueeze(1).to_broadcast([128, 8, HD]), op=ALU.mult),
                                 reads=[bkf[u], b_g], writes=[bqa[qi]])
                            k.op(pool, lambda e: e.memset(qa[qi][:, :, HD:HA], 1.0), writes=[bqa[qi]])
                        pending.append((kind, qc, t, qi))
                        if len(pending) > 2:
                            stage_b(*pending.pop(0))
                if ci == len(chunks) - 1:
                    while pending:
                        stage_b(*pending.pop(0))
                if ci == 4:
                    KR = NH * HA
                    for c in range(NCH):
                        k.coll(c_sem, "AllGather", GROUPS, kto_t.ap()[c * KR:(c + 1) * KR, :].opt(), kta_t.ap()[2 * c * KR:2 * (c + 1) * KR, :].opt(),
                               reads=[b_kto], writes=[b_kta[c]])
                        k.coll(c_sem, "AllGather", GROUPS, vo_t.ap()[c * 512:(c + 1) * 512, :].opt(), va_t.ap()[c * 1024:(c + 1) * 1024, :].opt(),
                               reads=[b_vo], writes=[b_va[c]])
                    k.coll(c_sem, "AllGather", GROUPS, kbo_t.ap().opt(), kba_t.ap().opt(), reads=[b_kbo], writes=[b_kba])
                if kind == "f":
                    k.op(act, lambda e: e.activation(out=lgf, in_=lgf, func=AF.Exp, scale=-1.0), writes=[b_f])
                    k.op(act, lambda e: e.activation(out=lgf, in_=lgf, func=AF.Ln, bias=1.0, scale=1.0), writes=[b_f])
                    for t in range(NT):
                        ps, bps = PS[4 + t % 2], bPS[4 + t % 2]
                        for t2 in range(t + 1):
                            k.op(pe, lambda e, t2=t2, t=t: e.matmul(ps[:, 0:NH], triu[:] if t2 == t else onesf[:], lgf[:, t2, :],
                                                                    start=(t2 == 0), stop=(t2 == t)),
                                 reads=[b_f, b_const], writes=[bps], inc=(t2 == t))
                        k.op(act, lambda e, t=t: e.copy(out=nf[:, t, :], in_=ps[:, 0:NH]), reads=[bps], writes=[b_f])
                    ps, bps = PS[4], bPS[4]
                    for t2 in range(NT):
                        k.op(pe, lambda e, t2=t2: e.matmul(ps[:, 0:NH], onesf[:], lgf[:, t2, :], start=(t2 == 0), stop=(t2 == NT - 1)),
                             reads=[b_f, b_const], writes=[bps], inc=(t2 == NT - 1))
                    k.op(dve, lambda e: e.tensor_tensor(out=kbp, in0=nf, in1=ps[:, 0:NH].unsqueeze(1).to_broadcast([128, NT, NH]), op=ALU.subtract),
                         reads=[bps], writes=[b_f])
                    k.dma(sp, d_st, kbo_t.ap(), kbp.rearrange("p t h -> p (t h)"), reads=[b_f], writes=[b_kbo])
                    k.op(dve, lambda e: e.tensor_scalar(out=ftmp, in0=nf, scalar1=-1.0, scalar2=None, op0=ALU.mult), writes=[b_f])
                    for r in range(3):
                        k.op(dve, lambda e, r=r: e.tensor_copy(out=fs3[:, :, :, r], in_=ftmp), writes=[b_f, b_fs3])
                        if r < 2:
                            k.op(dve, lambda e, r=r: e.tensor_tensor(out=ftmp, in0=ftmp, in1=fs3[:, :, :, r], op=ALU.subtract), writes=[b_f, b_fs3])
            k.barrier()
            kbias = slot_f32(4, 2 * NFC, 2 * NFC).rearrange("p (s t h) -> p s t h", s=2, h=NH)
            b_kb = Buf()
            k.op(dve, lambda e: e.tensor_copy(out=kbias[:, 1], in_=nf), writes=[b_kb])
            k.dma(sp, d_in, kbias[:, 0], kba_t.ap()[0:128, :].rearrange("p (t h) -> p t h", h=NH), reads=[b_kba], writes=[b_kb])
            k.op(dve, lambda e: e.tensor_scalar(out=kbias[:, 0], in0=kbias[:, 0], scalar1=mbias[:, 0:1], scalar2=None, op0=ALU.add),
                 reads=[b_const], writes=[b_kb])
            bO = [Buf() for _ in range(NCH)]
            ktav = kta_t.ap().rearrange("(c s h r) t -> r c s h t", c=NCH, s=2, r=HA)
            ktov = kto_t.ap().rearrange("(c h r) t -> r c h t", c=NCH, r=HA)
            vav = va_t.ap().rearrange("(c s q p) d -> p c s q d", c=NCH, s=2, p=128)
            vov = vo_t.ap().rearrange("(t p) d -> p t d", p=128)
            KTb = [SL[:, 2 * q, 0:4 * T].rearrange("p (h s t) -> p h s t", h=2, s=2) for q in range(2)]
            QTb = [SL[:, 2 * q + 1, 0:2 * T].rearrange("p (h t) -> p h t", h=2) for q in range(2)]
            Vb = [SL[:, 2 * q + 1, 2 * T:2 * T + 2 * NT * 128].rearrange("p (s t d) -> p s t d", s=2, d=128) for q in range(2)]
            bKQV = [Buf(), Buf()]
            Pb = [slot_bf(5, 512 * q, 512) for q in range(4)]
            bP = [Buf() for _ in range(4)]
            rc = [slot_f32(5, 1024 + 512 * q, 512) for q in range(2)]
            brc = [Buf(), Buf()]

            def loadpair(jp):
                q = jp % 2
                wr = [bKQV[q]]
                for hh in range(2):
                    h = 2 * jp + hh
                    k.dma_rows(sp, KTb[q][0:HA, hh, 0, :].rearrange("r (c t) -> r c t", t=512), ktav[:, :, 0, h, :], reads=b_kta, writes=wr)
                    k.dma_rows(sp, KTb[q][0:HA, hh, 1, :].rearrange("r (c t) -> r c t", t=512), ktov[:, :, h, :], reads=[b_kto], writes=wr)
                    k.dma_rows(sp, QTb[q][0:HA, hh, :], qtv[:, h, :], reads=[b_qt], writes=wr)
                for c in range(NCH):
                    k.dma(sp, d_in, Vb[q][:, 0, 4 * c:4 * c + 4, :], vav[:, c, 0, :, jp * 128:(jp + 1) * 128], reads=[b_va[c]], writes=wr)
                k.dma(sp, d_in, Vb[q][:, 1], vov[:, :, jp * 128:(jp + 1) * 128], reads=[b_vo], writes=wr)

            Sb = [PS[0], PS[1], PT[0][:, :].bitcast(F32), PT[1][:, :].bitcast(F32)]
            bSb = [bPS[0], bPS[1], bPT[0], bPT[1]]
            NSB = 4
            SDEPTH = 3
            units = []
            item = 0
            for jp in range(8):
                for hh in range(2):
                    for c in range(NCH):
                        blocks = [(0, kb) for kb in range(NT)] + [(1, kb) for kb in range(4 * c + 4)]
                        for bi, (src, kb) in enumerate(blocks):
                            units.append((jp, hh, c, bi, len(blocks), src, kb, item))
                        item += 1
            loaded = set()

            def geom(u):
                jp, hh, c, bi, nb, src, kb, item = u
                diag = (src == 1 and kb >= 4 * c)
                i = kb - 4 * c if diag else 0
                return diag, i, c * 512 + i * 128, 512 - i * 128

            def emit_S(ui):
                u = units[ui]
                jp, hh, c, bi, nb, src, kb, item = u
                q = jp % 2
                diag, i, q0, n = geom(u)
                ps, bps = Sb[ui % NSB], bSb[ui % NSB]
                k.op(pe, lambda e: e.matmul(ps[:, 0:n], KTb[q][0:HA, hh, src, kb * 128:(kb + 1) * 128], QTb[q][0:HA, hh, q0:q0 + n],
                                            start=True, stop=(not diag)),
                     reads=[bKQV[q]], writes=[bps], inc=(not diag))
                if diag:
                    k.op(pe, lambda e: e.matmul(ps[:, 0:128], ident[:], maskT[:], start=False, stop=True),
                         reads=[b_const], writes=[bps])

            def emit_rest(ui):
                u = units[ui]
                jp, hh, c, bi, nb, src, kb, item = u
                q = jp % 2
                h = 2 * jp + hh
                rows = slice(hh * 64, hh * 64 + 64)
                diag, i, q0, n = geom(u)
                ps, bps = Sb[ui % NSB], bSb[ui % NSB]
                po, bpo = PS[2 + item % 2], bPS[2 + item % 2]
                pm, bpm = PS[4 + item % 2], bPS[4 + item % 2]
                pb, bpb = Pb[ui % 4], bP[ui % 4]
                k.op(act, lambda e: e.activation(out=pb[:, 0:n], in_=ps[:, 0:n], func=AF.Exp, bias=kbias[:, src, kb, h:h + 1], scale=1.0),
                     reads=[bps, b_kb], writes=[bpb])
                first = (bi == 0)
                last = (bi == nb - 1)
                k.op(pe, lambda e: e.matmul(po[rows, i * 128:512], Vb[q][:, src, kb, hh * 64:(hh + 1) * 64], pb[:, 0:n], start=first, stop=last),
                     reads=[bpb, bKQV[q]], writes=[bpo], inc=False)
                k.op(pe, lambda e: e.matmul(pm[rows, i * 128:512], onesb[:, 0:64], pb[:, 0:n], start=first, stop=last),
                     reads=[bpb, b_const], writes=[bpm], inc=True)
                if last:
                    uu = item % 2
                    k.op(dve, lambda e: e.reciprocal(out=rc[uu][rows, :], in_=pm[rows, :]), reads=[bpm], writes=[brc[uu]])
                    k.op(dve, lambda e: e.tensor_tensor(out=R1[rows, jp, c * 512:(c + 1) * 512], in0=po[rows, :], in1=rc[uu][rows, :], op=ALU.mult),
                         reads=[bpo, brc[uu]], writes=[bO[c]])
                    if hh == 1 and c == NCH - 1 and jp + 2 < 8:
                        loadpair(jp + 2)

            loadpair(0)
            loadpair(1)
            for ui in range(len(units) + SDEPTH):
                if ui < len(units):
                    emit_S(ui)
                if ui - SDEPTH >= 0:
                    emit_rest(ui - SDEPTH)
            k.barrier()
            outproj(fwo_d[j], bO, 0, 0)

        def gmlp(bH):
            w_in = SL[:, 0:2, :].rearrange("p s n -> p (s n)")[:, 0:KC * 2 * D].rearrange("p (k n) -> p k n", k=KC)
            bw = Buf()
            k.dma(pool, d_w, w_in, swin_d.rearrange("(k p) n -> p k n", p=128), writes=[bw])
            w_o = slot_bf(2, 0, KC * D).rearrange("p (k n) -> p k n", k=KC)
            bwo = Buf()
            k.dma(pool, d_w, w_o, swo_d.rearrange("(k p) n -> p k n", p=128), writes=[bwo])
            lng = slot_f32(3, 0, D)
            lnb = slot_f32(3, D, D)
            wsf = slot_f32(3, 2 * D, D).rearrange("p (g s) -> p g s", g=8)
            wsb = slot_bf(3, 6 * D, D).rearrange("p (g s) -> p g s", g=8)
            wsT = slot_bf(3, 7 * D, D).rearrange("p (g t) -> p g t", g=8)
            bsc = slot_f32(5, 100, 8)
            b_p = Buf()
            k.dma(sp, d_in, lng, slng_d[:, :], writes=[b_p])
            k.dma(sp, d_in, lnb, slnb_d[:, :], writes=[b_p])
            k.dma(sp, d_in, wsf, sws_d.rearrange("g t s -> t g s"), writes=[b_p])
            k.dma(sp, d_in, bsc, sbs_d[:, :], writes=[b_p])
            k.op(dve, lambda e: e.tensor_copy(out=wsb, in_=wsf), writes=[b_p])
            for g in range(8):
                k.op(pe, lambda e, g=g: e.transpose(PT[0][:, g * 128:(g + 1) * 128], wsb[:, g, :], ident[:]),
                     reads=[b_p, b_const], writes=[bPT[0]], inc=(g == 7))
            k.op(dve, lambda e: e.tensor_copy(out=wsT, in_=PT[0][:, :].rearrange("p (g t) -> p g t", g=8)), reads=[bPT[0]], writes=[b_p])
            k.op(dve, lambda e: e.memset(wsT[64:128, :, 0:64], 0.0), writes=[b_p])
            ub = [slot_bf(4, D * q, D) for q in range(2)]
            vf = [slot_f32(4, D + D * q, D) for q in range(2)]
            vnb = [slot_bf(4, 6 * D + D * q, D) for q in range(2)]
            st6 = [slot_f32(5, 16 * q, 12) for q in range(2)]
            mv = [slot_f32(5, 64 + 16 * q, 2) for q in range(2)]
            pbuf = [slot_bf(5, 256 + D * q, D) for q in range(2)]
            pT = [slot_bf(5, 256 + 2 * D + D * q, D).rearrange("p (k t) -> p k t", k=KC) for q in range(2)]
            bu, bv, bvn, bst, bpb, bpT = ([Buf(), Buf()] for _ in range(6))
            res = Resid(0)
            for t in range(NT):
                u = t % 2
                for n4 in range(4):
                    ps, bps = PS[n4], bPS[n4]
                    for kc in range(KC):
                        k.op(pe, lambda e, kc=kc, n4=n4: e.matmul(ps[:, :], R1[:, kc, t * 128:(t + 1) * 128], w_in[:, kc, n4 * 512:(n4 + 1) * 512],
                                                                  start=(kc == 0), stop=(kc == KC - 1)),
                             reads=[bw, bH[t // 4]], writes=[bps], inc=(kc == KC - 1))
                    if n4 < 2:
                        k.op(act, lambda e, n4=n4: e.activation(out=ub[u][:, n4 * 512:(n4 + 1) * 512], in_=ps[:, :], func=AF.Gelu_apprx_tanh),
                             reads=[bps], writes=[bu[u]])
                    else:
                        k.op(act, lambda e, n4=n4: e.activation(out=vf[u][:, (n4 - 2) * 512:(n4 - 1) * 512], in_=ps[:, :], func=AF.Gelu_apprx_tanh),
                             reads=[bps], writes=[bv[u]])
                for hf in range(2):
                    k.op(dve, lambda e, hf=hf: e.bn_stats(out=st6[u][:, hf * 6:(hf + 1) * 6], in_=vf[u][:, hf * 512:(hf + 1) * 512]),
                         reads=[bv[u]], writes=[bst[u]])
                k.op(dve, lambda e: e.bn_aggr(out=mv[u], in_=st6[u].rearrange("p (c s) -> p c s", s=6)), writes=[bst[u]])
                k.op(act, lambda e: e.activation(out=mv[u][:, 1:2], in_=mv[u][:, 1:2], func=AF.Sqrt, bias=epsc[:, 0:1], scale=1.0),
                     reads=[b_const], writes=[bst[u]])
                k.op(dve, lambda e: e.reciprocal(out=mv[u][:, 1:2], in_=mv[u][:, 1:2]), writes=[bst[u]])
                k.op(dve, lambda e: e.tensor_scalar(out=vf[u], in0=vf[u], scalar1=mv[u][:, 0:1], scalar2=mv[u][:, 1:2],
                                                    op0=ALU.subtract, op1=ALU.mult), reads=[bst[u]], writes=[bv[u]])
                k.op(pool, lambda e: e.tensor_tensor(out=vf[u], in0=vf[u], in1=lng, op=ALU.mult), reads=[b_p], writes=[bv[u]])
                k.op(pool, lambda e: e.tensor_tensor(out=vnb[u], in0=vf[u], in1=lnb, op=ALU.add), reads=[b_p, bv[u]], writes=[bvn[u]])
                for hf in range(2):
                    ps, bps = PS[4 + hf], bPS[4 + hf]
                    for gg in range(4):
                        g = hf * 4 + gg
                        k.op(pe, lambda e, g=g, gg=gg: e.matmul(ps[:, gg * 128:(gg + 1) * 128], wsT[:, g, :], vnb[u][:, g * 128:(g + 1) * 128],
                                                                start=True, stop=True),
                             reads=[b_p, bvn[u]], writes=[bps], inc=(gg == 3))
                    for gg in range(4):
                        g = hf * 4 + gg
                        k.op(dve, lambda e, g=g, gg=gg: e.scalar_tensor_tensor(out=pbuf[u][:, g * 128:(g + 1) * 128], in0=ps[:, gg * 128:(gg + 1) * 128],
                                                                               scalar=bsc[:, g:g + 1], in1=ub[u][:, g * 128:(g + 1) * 128],
                                                                               op0=ALU.add, op1=ALU.mult),
                             reads=[bps, bu[u], b_p], writes=[bpb[u]])
                pt, bpt = PT[u], bPT[u]
                for kc in range(KC):
                    k.op(pe, lambda e, kc=kc: e.transpose(pt[:, kc * 128:(kc + 1) * 128], pbuf[u][:, kc * 128:(kc + 1) * 128], ident[:]),
                         reads=[bpb[u], b_const], writes=[bpt], inc=(kc == KC - 1))
                k.op(act, lambda e: e.copy(out=pT[u], in_=pt[:, :].rearrange("p (k t) -> p k t", k=KC)), reads=[bpt], writes=[bpT[u]])
                for half in range(2):
                    ps, bps = PS[half], bPS[half]
                    for kc in range(KC):
                        k.op(pe, lambda e, kc=kc, half=half: e.matmul(ps[:, :], pT[u][:, kc, :], w_o[:, kc, half * 512:(half + 1) * 512],
                                                                      start=(kc == 0), stop=(kc == KC - 1)),
                             reads=[bwo, bpT[u]], writes=[bps], inc=(kc == KC - 1))
                    res.add(ps, bps, t, half)

        def conv(bH):
            YW = 32 + T
            Y = SL[:, 0:3, :].rearrange("p s n -> p (s n)")[:, 0:KC * YW].rearrange("p (k t) -> p k t", k=KC)
            bY = [Buf() for _ in range(KC)]
            bHalo = Buf()
            cols = slot_f32(3, 0, 16 + 4 * KC + KC * CW)
            b1c = cols[:, 0:16]
            bdc = cols[:, 16:16 + KC]
            lgc = cols[:, 16 + KC:16 + 2 * KC]
            lbc = cols[:, 16 + 2 * KC:16 + 3 * KC]
            wdc = cols[:, 16 + 4 * KC:16 + 4 * KC + KC * CW].rearrange("p (k w) -> p k w", k=KC)
            b2row = slot_f32(3, 512, D)
            b_p = Buf()
            k.dma(sp, d_in, b1c, cb1_d[:, :], writes=[b_p])
            k.dma(sp, d_in, bdc, cbd_d[:, :], writes=[b_p])
            k.dma(sp, d_in, lgc, clg_d[:, :], writes=[b_p])
            k.dma(sp, d_in, lbc, clb_d[:, :], writes=[b_p])
            k.dma(sp, d_in, wdc, cwd_d.rearrange("p (k w) -> p k w", k=KC), writes=[b_p])
            k.dma(sp, d_in, b2row[0:1, :], cb2_d[:, :], writes=[b_p])
            wv = cw1_d.rearrange("(k p) n -> p k n", p=128)
            wr = [slot_bf(4, 2048 * q, 2048).rearrange("p (k h n) -> p k h n", k=KC, h=2) for q in range(3)]
            bwr = [Buf() for _ in range(3)]
            sgm = [slot_f32(5, 512 * q, 512) for q in range(2)]
            bsg = [Buf(), Buf()]
            hst = slot_bf(5, 4096, KC * 32).rearrange("p (k t) -> p k t", k=KC)
            hin = slot_bf(5, 4096 + KC * 32, KC * 32).rearrange("p (k t) -> p k t", k=KC)
            b_h = Buf()

            def loadw(jc):
                s = jc % 3
                k.dma(pool, d_w, wr[s][:, :, 0, :], wv[:, :, jc * 128:(jc + 1) * 128], writes=[bwr[s]])
                k.dma(pool, d_w, wr[s][:, :, 1, :], wv[:, :, D + jc * 128:D + (jc + 1) * 128], writes=[bwr[s]])

            loadw(0)
            loadw(1)
            n = 0
            corder = [NCH - 1] + list(range(NCH - 1))
            for jc in range(KC):
                s = jc % 3
                if jc + 2 < KC:
                    loadw(jc + 2)
                for c in corder:
                    pa, bpa = PS[(2 * n) % 4], bPS[(2 * n) % 4]
                    pg, bpg = PS[(2 * n + 1) % 4], bPS[(2 * n + 1) % 4]
                    u = n % 2
                    n += 1
                    for hf, (pp, bpp) in enumerate(((pa, bpa), (pg, bpg))):
                        for kc in range(KC):
                            k.op(pe, lambda e, kc=kc, hf=hf, pp=pp: e.matmul(pp[:, :], wr[s][:, kc, hf, :], R1[:, kc, c * 512:(c + 1) * 512],
                                                                             start=(kc == 0), stop=(kc == KC - 1)),
                                 reads=[bwr[s], bH[c]], writes=[bpp], inc=(kc == KC - 1))
                    k.op(act, lambda e, jc=jc: e.activation(out=sgm[u], in_=pg[:, :], func=AF.Sigmoid, bias=b1c[:, 8 + jc:9 + jc], scale=1.0),
                         reads=[bpg, b_p], writes=[bsg[u]])
                    k.op(dve, lambda e, jc=jc, c=c: e.scalar_tensor_tensor(out=Y[:, jc, 32 + c * 512:32 + (c + 1) * 512], in0=pa[:, :],
                                                                           scalar=b1c[:, jc:jc + 1], in1=sgm[u], op0=ALU.add, op1=ALU.mult),
                         reads=[bpa, bsg[u], b_p], writes=[bY[jc]])
            k.op(dve, lambda e: e.tensor_copy(out=hst, in_=Y[:, :, T:T + 32]), reads=bY, writes=[b_h])
            k.dma(sp, d_st, hlo_t.ap(), hst.rearrange("p k t -> p (k t)"), reads=[b_h], writes=[b_hlo])
            k.coll(c_sem, "AllGather", GROUPS, hlo_t.ap().opt(), hla_t.ap().opt(), reads=[b_hlo], writes=[b_hla])
            k.dma(sp, d_in, hin.rearrange("p k t -> p (k t)"), hla_t.ap()[0:128, :], reads=[b_hla], writes=[b_h])
            k.op(dve, lambda e: e.tensor_scalar(out=Y[:, :, 0:32], in0=hin, scalar1=flagc[:, 0:1], scalar2=None, op0=ALU.mult),
                 reads=[b_h, b_const], writes=bY)
            k.barrier()
            dgb = [SL[:, 4, 0:CW * 128].rearrange("p (w m) -> p w m", w=CW), SL[:, 5, 0:CW * 128].rearrange("p (w m) -> p w m", w=CW)]
            bdg = [Buf(), Buf()]
            bC = [Buf() for _ in range(NCH)]
            n = 0
            for jc in range(KC):
                u = jc % 2
                for w in range(CW):
                    E = dve if w % 2 == 0 else pool
                    k.op(E, lambda e, w=w, jc=jc: e.tensor_scalar(out=dgb[u][:, w, :], in0=ident[:], scalar1=wdc[:, jc, w:w + 1], scalar2=None, op0=ALU.mult),
                         reads=[b_p, b_const], writes=[bdg[u]])
                for c in range(NCH):
                    ps, bps = PS[n % 4], bPS[n % 4]
                    n += 1
                    for w in range(CW):
                        o = 32 + c * 512 - (CW - 1) + w
                        k.op(pe, lambda e, w=w, o=o, jc=jc: e.matmul(ps[:, :], dgb[u][:, w, :], Y[:, jc, o:o + 512], start=(w == 0), stop=(w == CW - 1)),
                             reads=[bdg[u], bY[jc]], writes=[bps], inc=(w == CW - 1))
                    k.op(act, lambda e, jc=jc, c=c: e.activation(out=R1[:, jc, c * 512:(c + 1) * 512], in_=ps[:, :], func=AF.Identity,
                                                                 bias=bdc[:, jc:jc + 1], scale=1.0),
                         reads=[bps, b_p], writes=[bC[c]])
            k.barrier()
            w2 = slot_bf(0, 0, KC * D).rearrange("p (k n) -> p k n", k=KC)
            bw2 = Buf()
            k.dma(pool, d_w, w2, cw2_d.rearrange("(k p) n -> p k n", p=128), writes=[bw2])
            ysq = [slot_bf(1, 4096 * q, 4096).rearrange("p (k t) -> p k t", k=KC) for q in range(2)]
            bys = [Buf(), Buf()]
            zT = [slot_bf(2, 4096 * q, 4096).rearrange("p (k t) -> p k t", k=KC) for q in range(2)]
            bz = [Buf(), Buf()]
            Rr = [slot_f32(4, 1024 * q, 512) for q in range(2)]
            MR = [slot_f32(4, 1024 * q + 512, 512) for q in range(2)]
            bR = [Buf(), Buf()]
            zt = [slot_f32(5, 512 * q, 512) for q in range(2)]
            bzt = [Buf(), Buf()]
            res = Resid(0)
            nz = 0
            nres = 0
            for c in range(NCH):
                u = c % 2
                for jc in range(KC):
                    k.op(pool, lambda e, jc=jc: e.tensor_tensor(out=ysq[u][:, jc, :], in0=R1[:, jc, c * 512:(c + 1) * 512], in1=R1[:, jc, c * 512:(c + 1) * 512], op=ALU.mult),
                         reads=[bC[c]], writes=[bys[u]])
                pm, bpm = PS[4], bPS[4]
                pq, bpq = PS[5], bPS[5]
                for jc in range(KC):
                    k.op(pe, lambda e, jc=jc: e.matmul(pm[:, :], invd[:], R1[:, jc, c * 512:(c + 1) * 512], start=(jc == 0), stop=(jc == KC - 1)),
                         reads=[bC[c], b_const], writes=[bpm], inc=(jc == KC - 1))
                for jc in range(KC):
                    k.op(pe, lambda e, jc=jc: e.matmul(pq[:, :], invd[:], ysq[u][:, jc, :], start=(jc == 0), stop=(jc == KC - 1)),
                         reads=[bys[u], b_const], writes=[bpq], inc=(jc == KC - 1))
                k.op(act, lambda e: e.activation(out=MR[u], in_=pm[:, :], func=AF.Square), reads=[bpm], writes=[bR[u]])
                k.op(dve, lambda e: e.tensor_tensor(out=Rr[u], in0=pq[:, :], in1=MR[u], op=ALU.subtract), reads=[bpq], writes=[bR[u]])
                k.op(act, lambda e: e.activation(out=Rr[u], in_=Rr[u], func=AF.Sqrt, bias=epsc[:, 0:1], scale=1.0), reads=[b_const], writes=[bR[u]])
                k.op(dve, lambda e: e.reciprocal(out=Rr[u], in_=Rr[u]), writes=[bR[u]])
                k.op(dve, lambda e: e.tensor_tensor(out=MR[u], in0=pm[:, :], in1=Rr[u], op=ALU.mult), reads=[bpm], writes=[bR[u]])
                for jc in range(KC):
                    v = nz % 2
                    nz += 1
                    k.op(dve, lambda e, jc=jc: e.tensor_tensor(out=zt[v], in0=R1[:, jc, c * 512:(c + 1) * 512], in1=Rr[u], op=ALU.mult),
                         reads=[bC[c], bR[u]], writes=[bzt[v]])
                    k.op(pool, lambda e: e.tensor_tensor(out=zt[v], in0=zt[v], in1=MR[u], op=ALU.subtract), reads=[bR[u]], writes=[bzt[v]])
                    k.op(act, lambda e, jc=jc: e.activation(out=zT[u][:, jc, :], in_=zt[v], func=AF.Silu, bias=lbc[:, jc:jc + 1], scale=lgc[:, jc:jc + 1]),
                         reads=[bzt[v], b_p], writes=[bz[u]])
                for tq in range(4):
                    t = 4 * c + tq
                    for half in range(2):
                        ps, bps = PS[nres % 4], bPS[nres % 4]
                        nres += 1
                        for kc in range(KC):
                            k.op(pe, lambda e, kc=kc, tq=tq, half=half: e.matmul(ps[:, :], zT[u][:, kc, tq * 128:(tq + 1) * 128], w2[:, kc, half * 512:(half + 1) * 512],
                                                                               start=(kc == 0), stop=False),
                                 reads=[bw2, bz[u]], writes=[bps], inc=False)
                        k.op(pe, lambda e, half=half: e.matmul(ps[:, :], onesf[0:1, :], b2row[0:1, half * 512:(half + 1) * 512], start=False, stop=True),
                             reads=[b_p, b_const], writes=[bps])
                        res.add(ps, bps, t, half)

        consts()
        load_x()
        for i in layers:
            kind = i % 3
            j = i // 3
            k.barrier()
            mods(i)
            k.barrier()
            bH = [Buf() for _ in range(NCH)]
            norm_to_hT(0, bH)
            k.barrier()
            if "mix" not in skip:
                if kind == 0:
                    fox(j, bH)
                elif kind == 1:
                    gmlp(bH)
                else:
                    conv(bH)
            k.barrier()
            bH = [Buf() for _ in range(NCH)]
            norm_to_hT(1, bH)
            k.barrier()
            if "mlp" not in skip:
                mlp(i, bH)
        k.barrier()
        ov = out_d.rearrange("(t p) d -> p t d", p=128)
        for t0 in range(0, NT, 4):
            k.dma(sp, d_st, ov[:, t0:t0 + 4, :], X[:, t0:t0 + 4, :], reads=bX[t0:t0 + 4])
        for E in k.engs:
            E.wait((d_st, d_st.cnt))
    return nc


def make_in_maps(inputs, S):
    T = S // 2
    f = lambda a: np.ascontiguousarray(np.asarray(a, dtype=np.float32))
    x = f(inputs["x"])
    c = f(inputs["c"])
    B = x.shape[0]
    shared = {
        "norm_mix": f(inputs["norm_mix"]), "norm_mlp": f(inputs["norm_mlp"]),
        "w_ada": f(inputs["w_ada"]), "b_ada": f(inputs["b_ada"]),
        "w_mlp_in": f(inputs["w_mlp_in"]), "w_mlp_out": f(inputs["w_mlp_out"]),
        "fox_w_in": f(inputs["fox_w_in"]),
        "fox_bf_bc": f(np.broadcast_to(np.asarray(inputs["fox_b_f"])[:, None, :], (2, 128, NH))),
        "fox_qg_bc": f(np.broadcast_to(np.asarray(inputs["fox_q_norm"])[:, None, :], (2, 128, HD))),
        "fox_kg_bc": f(np.broadcast_to(np.asarray(inputs["fox_k_norm"])[:, None, :], (2, 128, HD))),
        "fox_w_out": f(inputs["fox_w_out"]),
        "sg_w_in": f(inputs["sg_w_in"])[0],
        "sg_lng_bc": f(np.broadcast_to(np.asarray(inputs["sg_ln_g"])[0][None, :], (128, D))),
        "sg_lnb_bc": f(np.broadcast_to(np.asarray(inputs["sg_ln_b"])[0][None, :], (128, D))),
        "sg_w_s": f(inputs["sg_w_s"])[0],
        "sg_bs_c": f(np.asarray(inputs["sg_b_s"])[0].T),
        "sg_w_out": f(inputs["sg_w_out"])[0],
        "cv_w_pw1": f(inputs["cv_w_pw1"])[0],
        "cv_bpw1_c": f(np.asarray(inputs["cv_b_pw1"])[0].reshape(16, 128).T),
        "cv_wdw_c": f(np.asarray(inputs["cv_w_dw"])[0].reshape(CW, KC, 128).transpose(2, 1, 0).reshape(128, KC * CW)),
        "cv_bdw_c": f(np.asarray(inputs["cv_b_dw"])[0].reshape(KC, 128).T),
        "cv_lng_c": f(np.asarray(inputs["cv_ln_g"])[0].reshape(KC, 128).T),
        "cv_lnb_c": f(np.asarray(inputs["cv_ln_b"])[0].reshape(KC, 128).T),
        "cv_w_pw2": f(inputs["cv_w_pw2"])[0],
        "cv_bpw2": f(np.asarray(inputs["cv_b_pw2"])[0][None, :]),
    }
    maps = []
    for core in range(2 * B):
        b, r = core // 2, core % 2
        m = dict(shared)
        m["x"] = np.ascontiguousarray(x[b, r * T:(r + 1) * T, :])
        m["cT"] = np.ascontiguousarray(c[b].reshape(KC, 128).T)
        m["flag"] = np.full((128, 1), float(r), np.float32)
        maps.append(m)
    return maps


_NC_CACHE = {}


def run(inputs, S, layers=(0, 1, 2, 3), skip=(), trace=False):
    NT = S // 256
    key = (NT, tuple(layers), tuple(skip))
    if key not in _NC_CACHE:
        _NC_CACHE[key] = build(NT, layers, skip)
    nc = _NC_CACHE[key]
    maps = make_in_maps(inputs, S)
    res = run_bass_kernel_spmd(nc, maps, core_ids=list(range(8)), **({"trace": True} if trace else {}))
    B = 4
    T = S // 2
    out = np.empty((B, S, D), np.float32)
    for core in range(8):
        b, r = core // 2, core % 2
        out[b, r * T:(r + 1) * T, :] = res.results[core]["out"]
    return out, res


def kernel(**inputs):
    out, _ = run(inputs, 4096)
    return out
```
